# Optimizing a Trainium2 kernel written in Bass

```python
import math
import jax
import jax.numpy as jnp
from jax import lax
import numpy as np

D_MODEL = 1024
BATCH = 16
SEQ = 2048
DEPTH = 2

GRID_W = 64
CTX_LEN = 256
N_EVEN = (DEPTH + 1) // 2
N_ODD = DEPTH // 2
EPS = 1e-6
ROPE_THETA = 10000.0
Q_BLOCK = 128
ADA_CHUNKS = 6

SSD_HEADS = 16
SSD_HEADDIM = 64
SSD_INNER = SSD_HEADS * SSD_HEADDIM
SSD_GROUPS = 4
SSD_STATE = 128
SSD_CONV = 3
SSD_CHUNK = 128
SSD_CONV_DIM = SSD_INNER + 2 * SSD_GROUPS * SSD_STATE
SSD_COLS = SSD_INNER + SSD_CONV_DIM + 2 * SSD_HEADS

ATT_HEADS = 16
ATT_KV_HEADS = 4
ATT_HEADDIM = 64
ATT_Q = ATT_HEADS * ATT_HEADDIM
ATT_KV = ATT_KV_HEADS * ATT_HEADDIM
ATT_COLS = ATT_Q + 2 * ATT_KV

AB_IN = SSD_COLS + ATT_COLS
AB_OUT = SSD_INNER + ATT_Q

MLA_HEADS = 16
MLA_NOPE = 64
MLA_ROPE = 32
MLA_V = 64
MLA_Q_RANK = 384
MLA_KV_RANK = 256
MLA_IN = MLA_Q_RANK + MLA_KV_RANK + MLA_ROPE

FFN_HIDDEN = -(-(8 * D_MODEL) // (3 * 256)) * 256

kernel_name = 'hybrid_ssd_gqa_mla_prefix_dit'


def rmsnorm(x, g):
    xf = x.astype(jnp.float32)
    xf = xf * lax.rsqrt(jnp.mean(xf * xf, axis=-1, keepdims=True) + EPS)
    return (xf * g.astype(jnp.float32)).astype(x.dtype)


def modulate(h, shift, scale):
    return h * (1 + scale) + shift


def axial_rope(rows, rot_dim, dtype):
    n_freq = rot_dim // 4
    row = jnp.repeat(jnp.arange(rows, dtype=jnp.float32), GRID_W)
    col = jnp.tile(jnp.arange(GRID_W, dtype=jnp.float32), rows)
    inv = ROPE_THETA ** (-jnp.arange(n_freq, dtype=jnp.float32) / n_freq)
    ang = jnp.concatenate([row[:, None] * inv, col[:, None] * inv], axis=-1)
    return jnp.cos(ang).astype(dtype), jnp.sin(ang).astype(dtype)


def apply_rope(x, cos, sin):
    shape = (cos.shape[0],) + (1,) * (x.ndim - 3) + (cos.shape[1],)
    cos = cos.reshape(shape)
    sin = sin.reshape(shape)
    x1, x2 = jnp.split(x, 2, axis=-1)
    return jnp.concatenate([x1 * cos - x2 * sin, x1 * sin + x2 * cos], axis=-1)


def dwconv_centred(x, w, b):
    pad = w.shape[0] // 2
    y = lax.conv_general_dilated(x, w[:, None, :].astype(x.dtype), window_strides=(1,),
                                 padding=[(pad, pad)], dimension_numbers=('NWC', 'WIO', 'NWC'),
                                 feature_group_count=x.shape[-1])
    return y + b


def swiglu(h, w_up, w_down):
    g, u = jnp.split(h @ w_up, 2, axis=-1)
    return (jax.nn.silu(g) * u) @ w_down


def gqa_attend(q, k, v):
    scale = q.shape[-1] ** -0.5
    s = jnp.einsum('bqhgd,bkhd->bhgqk', q, k).astype(jnp.float32) * scale
    p = jax.nn.softmax(s, axis=-1).astype(v.dtype)
    return jnp.einsum('bhgqk,bkhd->bqhgd', p, v)


def blocked_attend(q, k, v):
    b, t = q.shape[:2]
    nb = t // Q_BLOCK
    qb = jnp.moveaxis(q.reshape((b, nb, Q_BLOCK) + q.shape[2:]), 1, 0)
    ob = lax.map(lambda qi: gqa_attend(qi, k, v), qb)
    return jnp.moveaxis(ob, 0, 1).reshape((b, t) + ob.shape[3:])


def ssd_chunked(xs, dt, a, bm, cm, h0):
    bsz, length, nh, p = xs.shape
    g, n = bm.shape[2], bm.shape[3]
    hpg = nh // g
    q = SSD_CHUNK
    nc = length // q
    x = xs.reshape(bsz, nc, q, g, hpg, p)
    dtc = dt.reshape(bsz, nc, q, g, hpg).astype(jnp.float32)
    bc = bm.reshape(bsz, nc, q, g, n)
    cc = cm.reshape(bsz, nc, q, g, n)
    acs = jnp.cumsum(dtc * a.reshape(g, hpg), axis=2)
    seg = acs[:, :, :, None] - acs[:, :, None, :]
    mask = jnp.tril(jnp.ones((q, q), dtype=bool))[None, None, :, :, None, None]
    lmat = jnp.exp(jnp.where(mask, seg, -jnp.inf))
    cb = jnp.einsum('bcign,bcjgn->bcijg', cc, bc)
    xdt = x * dtc[..., None]
    y_diag = jnp.einsum('bcijgh,bcjghp->bcighp', cb[..., None] * lmat, xdt)
    decay_end = jnp.exp(acs[:, :, -1:] - acs)
    states = jnp.einsum('bcjgn,bcjghp->bcghpn', bc, xdt * decay_end[..., None])
    chunk_decay = jnp.exp(acs[:, :, -1])

    def step(h, inp):
        s, d = inp
        return h * d[..., None, None] + s, h

    h_final, h_starts = lax.scan(step, h0.reshape(bsz, g, hpg, p, n),
                                 (jnp.moveaxis(states, 1, 0), jnp.moveaxis(chunk_decay, 1, 0)))
    h_starts = jnp.moveaxis(h_starts, 0, 1)
    y_off = jnp.einsum('bcign,bcghpn->bcighp', cc, h_starts) * jnp.exp(acs)[..., None]
    y = (y_diag + y_off).reshape(bsz, length, nh, p)
    return y, h_final.reshape(bsz, nh, p, n)


def ssd_prep(p, conv_w, conv_b, dt_bias):
    b, t = p.shape[:2]
    gn = SSD_GROUPS * SSD_STATE
    z = p[..., :SSD_INNER]
    xbc = jax.nn.silu(dwconv_centred(p[..., SSD_INNER:SSD_INNER + SSD_CONV_DIM], conv_w, conv_b))
    xs = xbc[..., :SSD_INNER].reshape(b, t, SSD_HEADS, SSD_HEADDIM)
    bm = xbc[..., SSD_INNER:SSD_INNER + gn].reshape(b, t, SSD_GROUPS, SSD_STATE)
    cm = xbc[..., SSD_INNER + gn:].reshape(b, t, SSD_GROUPS, SSD_STATE)
    dt_raw = p[..., SSD_INNER + SSD_CONV_DIM:].reshape(b, t, 2, SSD_HEADS)
    dt = jax.nn.softplus((dt_raw + dt_bias).astype(jnp.float32))
    return z, xs, bm, cm, dt


def ssd_branch(p_lat, p_ctx, conv_w, conv_b, a_log, dt_bias, d_skip, norm_g):
    a = -jnp.exp(a_log.astype(jnp.float32))
    z_l, x_l, b_l, c_l, dt_l = ssd_prep(p_lat, conv_w, conv_b, dt_bias)
    z_c, x_c, b_c, c_c, dt_c = ssd_prep(p_ctx, conv_w, conv_b, dt_bias)
    h0 = jnp.zeros((p_lat.shape[0], SSD_HEADS, SSD_HEADDIM, SSD_STATE), jnp.float32)
    flip = lambda u: u[:, ::-1]
    yc_f, hc_f = ssd_chunked(x_c, dt_c[:, :, 0], a[0], b_c, c_c, h0)
    yl_f, _ = ssd_chunked(x_l, dt_l[:, :, 0], a[0], b_l, c_l, hc_f)
    yc_b, hc_b = ssd_chunked(flip(x_c), flip(dt_c[:, :, 1]), a[1], flip(b_c), flip(c_c), h0)
    yl_b, _ = ssd_chunked(flip(x_l), flip(dt_l[:, :, 1]), a[1], flip(b_l), flip(c_l), hc_b)

    def finish(yf, yb, xs, z):
        y = (yf + flip(yb)).astype(xs.dtype) + xs * d_skip[:, None].astype(xs.dtype)
        b, t = y.shape[:2]
        return rmsnorm(y.reshape(b, t, SSD_INNER) * jax.nn.silu(z), norm_g)

    return finish(yl_f, yl_b, x_l, z_l), finish(yc_f, yc_b, x_c, z_c)


def gqa_q(p, q_g, rope):
    b, t = p.shape[:2]
    q = rmsnorm(p[..., :ATT_Q].reshape(b, t, ATT_KV_HEADS, ATT_HEADS // ATT_KV_HEADS, ATT_HEADDIM), q_g)
    return q if rope is None else apply_rope(q, *rope)


def gqa_kv(p, k_g, rope):
    b, t = p.shape[:2]
    k = rmsnorm(p[..., ATT_Q:ATT_Q + ATT_KV].reshape(b, t, ATT_KV_HEADS, ATT_HEADDIM), k_g)
    v = p[..., ATT_Q + ATT_KV:].reshape(b, t, ATT_KV_HEADS, ATT_HEADDIM)
    return (k if rope is None else apply_rope(k, *rope)), v


def ab_mixer(h_lat, h_ctx, w_in, w_out, conv_w, conv_b, a_log, dt_bias, d_skip, ssd_g, q_g, k_g,
             rope, ctx_out):
    b, t = h_lat.shape[:2]
    p_lat = h_lat @ w_in
    p_ctx = h_ctx @ w_in
    y_lat, y_ctx = ssd_branch(p_lat[..., :SSD_COLS], p_ctx[..., :SSD_COLS], conv_w, conv_b,
                              a_log, dt_bias, d_skip, ssd_g)
    a_lat, a_ctx = p_lat[..., SSD_COLS:], p_ctx[..., SSD_COLS:]
    k_c, v_c = gqa_kv(a_ctx, k_g, None)
    k_l, v_l = gqa_kv(a_lat, k_g, rope)
    k_all = jnp.concatenate([k_c, k_l], axis=1)
    v_all = jnp.concatenate([v_c, v_l], axis=1)
    o_att = blocked_attend(gqa_q(a_lat, q_g, rope), k_all, v_all).reshape(b, t, ATT_Q)
    o_lat = jnp.concatenate([y_lat, o_att], axis=-1) @ w_out
    if not ctx_out:
        return o_lat, None
    o_att_c = gqa_attend(gqa_q(a_ctx, q_g, None), k_c, v_c).reshape(b, h_ctx.shape[1], ATT_Q)
    o_ctx = jnp.concatenate([y_ctx, o_att_c], axis=-1) @ w_out
    return o_lat, o_ctx


def mla_latents(h, w_in):
    p = h @ w_in
    return p[..., :MLA_Q_RANK], p[..., MLA_Q_RANK:MLA_Q_RANK + MLA_KV_RANK], p[..., MLA_Q_RANK + MLA_KV_RANK:]


def mla_q(cq, q_norm_g, w_uq, rope):
    b, t = cq.shape[:2]
    q = (rmsnorm(cq, q_norm_g) @ w_uq).reshape(b, t, MLA_HEADS, MLA_NOPE + MLA_ROPE)
    q_nope, q_pe = q[..., :MLA_NOPE], q[..., MLA_NOPE:]
    if rope is not None:
        q_pe = apply_rope(q_pe, *rope)
    return jnp.concatenate([q_nope, q_pe], axis=-1)[:, :, :, None, :]


def mla_kv(ckv, k_pe, kv_norm_g, w_ukv, rope):
    b, t = ckv.shape[:2]
    kv = (rmsnorm(ckv, kv_norm_g) @ w_ukv).reshape(b, t, MLA_HEADS, MLA_NOPE + MLA_V)
    k_nope, v = kv[..., :MLA_NOPE], kv[..., MLA_NOPE:]
    k_pe = k_pe[:, :, None, :]
    if rope is not None:
        k_pe = apply_rope(k_pe, *rope)
    k = jnp.concatenate([k_nope, jnp.broadcast_to(k_pe, (b, t, MLA_HEADS, MLA_ROPE))], axis=-1)
    return k, v


def mla_mixer(h_lat, h_ctx, w_in, q_norm_g, w_uq, kv_norm_g, w_ukv, w_o, rope, ctx_out):
    b, t = h_lat.shape[:2]
    cq_l, ckv_l, kpe_l = mla_latents(h_lat, w_in)
    cq_c, ckv_c, kpe_c = mla_latents(h_ctx, w_in)
    k_c, v_c = mla_kv(ckv_c, kpe_c, kv_norm_g, w_ukv, None)
    k_l, v_l = mla_kv(ckv_l, kpe_l, kv_norm_g, w_ukv, rope)
    k_all = jnp.concatenate([k_c, k_l], axis=1)
    v_all = jnp.concatenate([v_c, v_l], axis=1)
    o = blocked_attend(mla_q(cq_l, q_norm_g, w_uq, rope), k_all, v_all)
    o_lat = o.reshape(b, t, MLA_HEADS * MLA_V) @ w_o
    if not ctx_out:
        return o_lat, None
    o_c = gqa_attend(mla_q(cq_c, q_norm_g, w_uq, None), k_c, v_c)
    return o_lat, o_c.reshape(b, h_ctx.shape[1], MLA_HEADS * MLA_V) @ w_o


def setup_inputs(seed: int = 0) -> dict:
    key = jax.random.key(seed)
    ks = iter(jax.random.split(key, 40))
    f32 = jnp.float32

    def nrm(shape, scale):
        return jax.random.normal(next(ks), shape, f32) * scale

    def gain(shape):
        return 1.0 + nrm(shape, 0.02)

    dt0 = jnp.exp(jax.random.uniform(next(ks), (N_EVEN, 2, SSD_HEADS), f32, math.log(1e-3), math.log(1e-1)))
    return {
        'x': nrm((BATCH, SEQ, D_MODEL), 1.0),
        'c': nrm((BATCH, D_MODEL), 1.0),
        'ctx': nrm((BATCH, CTX_LEN, D_MODEL), 1.0),
        'c_ctx': nrm((D_MODEL,), 1.0),
        'ada_w': nrm((DEPTH, D_MODEL, ADA_CHUNKS * D_MODEL), 0.5 * D_MODEL ** -0.5),
        'ada_b': nrm((DEPTH, ADA_CHUNKS * D_MODEL), 0.02),
        'norm1_g': gain((DEPTH, D_MODEL)),
        'norm2_g': gain((DEPTH, D_MODEL)),
        'ffn_w_up': nrm((DEPTH, D_MODEL, 2 * FFN_HIDDEN), D_MODEL ** -0.5),
        'ffn_w_down': nrm((DEPTH, FFN_HIDDEN, D_MODEL), FFN_HIDDEN ** -0.5),
        'ab_w_in': nrm((N_EVEN, D_MODEL, AB_IN), D_MODEL ** -0.5),
        'ab_w_out': nrm((N_EVEN, AB_OUT, D_MODEL), AB_OUT ** -0.5),
        'ssd_conv_w': nrm((N_EVEN, SSD_CONV, SSD_CONV_DIM), SSD_CONV ** -0.5),
        'ssd_conv_b': nrm((N_EVEN, SSD_CONV_DIM), 0.02),
        'ssd_a_log': jnp.log(jax.random.uniform(next(ks), (N_EVEN, 2, SSD_HEADS), f32, 1.0, 16.0)),
        'ssd_dt_bias': dt0 + jnp.log(-jnp.expm1(-dt0)),
        'ssd_d': gain((N_EVEN, SSD_HEADS)),
        'ssd_norm_g': gain((N_EVEN, SSD_INNER)),
        'att_q_g': gain((N_EVEN, ATT_HEADDIM)),
        'att_k_g': gain((N_EVEN, ATT_HEADDIM)),
        'mla_w_in': nrm((N_ODD, D_MODEL, MLA_IN), D_MODEL ** -0.5),
        'mla_q_norm_g': gain((N_ODD, MLA_Q_RANK)),
        'mla_w_uq': nrm((N_ODD, MLA_Q_RANK, MLA_HEADS * (MLA_NOPE + MLA_ROPE)), MLA_Q_RANK ** -0.5),
        'mla_kv_norm_g': gain((N_ODD, MLA_KV_RANK)),
        'mla_w_ukv': nrm((N_ODD, MLA_KV_RANK, MLA_HEADS * (MLA_NOPE + MLA_V)), MLA_KV_RANK ** -0.5),
        'mla_w_o': nrm((N_ODD, MLA_HEADS * MLA_V, D_MODEL), (MLA_HEADS * MLA_V) ** -0.5),
        'final_norm_g': gain((D_MODEL,)),
    }


def reference(x, c, ctx, c_ctx, ada_w, ada_b, norm1_g, norm2_g, ffn_w_up, ffn_w_down,
              ab_w_in, ab_w_out, ssd_conv_w, ssd_conv_b, ssd_a_log, ssd_dt_bias, ssd_d, ssd_norm_g,
              att_q_g, att_k_g, mla_w_in, mla_q_norm_g, mla_w_uq, mla_kv_norm_g, mla_w_ukv, mla_w_o,
              final_norm_g):
    rows = x.shape[1] // GRID_W
    rope_att = axial_rope(rows, ATT_HEADDIM, x.dtype)
    rope_mla = axial_rope(rows, MLA_ROPE, x.dtype)
    sc = jax.nn.silu(c)
    scc = jax.nn.silu(c_ctx)
    for i in range(DEPTH):
        last = i == DEPTH - 1
        j = i // 2
        mods = jnp.split((sc @ ada_w[i] + ada_b[i])[:, None, :], ADA_CHUNKS, axis=-1)
        mods_c = jnp.split(scc @ ada_w[i] + ada_b[i], ADA_CHUNKS, axis=-1)
        h_lat = modulate(rmsnorm(x, norm1_g[i]), mods[0], mods[1])
        h_ctx = modulate(rmsnorm(ctx, norm1_g[i]), mods_c[0], mods_c[1])
        if i % 2 == 0:
            o_lat, o_ctx = ab_mixer(h_lat, h_ctx, ab_w_in[j], ab_w_out[j], ssd_conv_w[j], ssd_conv_b[j],
                                    ssd_a_log[j], ssd_dt_bias[j], ssd_d[j], ssd_norm_g[j],
                                    att_q_g[j], att_k_g[j], rope_att, not last)
        else:
            o_lat, o_ctx = mla_mixer(h_lat, h_ctx, mla_w_in[j], mla_q_norm_g[j], mla_w_uq[j],
                                     mla_kv_norm_g[j], mla_w_ukv[j], mla_w_o[j], rope_mla, not last)
        x = x + mods[2] * o_lat
        h = modulate(rmsnorm(x, norm2_g[i]), mods[3], mods[4])
        x = x + mods[5] * swiglu(h, ffn_w_up[i], ffn_w_down[i])
        if not last:
            ctx = ctx + mods_c[2] * o_ctx
            hc = modulate(rmsnorm(ctx, norm2_g[i]), mods_c[3], mods_c[4])
            ctx = ctx + mods_c[5] * swiglu(hc, ffn_w_up[i], ffn_w_down[i])
    return rmsnorm(x, final_norm_g)
```

```python
import contextlib
import math
import numpy as np
import concourse.bass as bass
import concourse.mybir as mybir
from concourse.bass_utils import run_bass_kernel_spmd

F32 = mybir.dt.float32
BF16 = mybir.dt.bfloat16
AF = mybir.ActivationFunctionType
ALU = mybir.AluOpType

ENGS = ("pe", "act", "dve", "pool", "sp")
EPS = 1e-6
D = 1024
T = 2048
CT = 256
TT = T + CT
NB = 2
FH = 2816


class Res:
    __slots__ = ("name", "t", "last_w", "readers", "dsem", "dcount")

    def __init__(self, name, t=None):
        self.name = name
        self.t = t
        self.last_w = None
        self.readers = []
        self.dsem = None
        self.dcount = 0

    def __getitem__(self, k):
        return self.t[k]


class PhysSem:
    __slots__ = ("count", "handle", "idx")

    def __init__(self, idx):
        self.count = 0
        self.handle = None
        self.idx = idx


class Op:
    __slots__ = ("eng", "fn", "deps", "signal", "sig_idx", "is_dma", "dres", "dval", "dsem", "strict")

    def __init__(self, eng, fn):
        self.eng = eng
        self.fn = fn
        self.deps = []
        self.signal = False
        self.sig_idx = None
        self.is_dma = False
        self.dres = None
        self.dval = 0
        self.strict = False


class Ctx:
    def __init__(self, same_engine_sync=False):
        self.nc = bass.Bass("TRN2", target_bir_lowering=False)
        self.es = contextlib.ExitStack()
        self.ops = []
        self.same_engine_sync = same_engine_sync
        self.n_sems = 0
        self._uid = 0
        self.dres = {}
        self._bar_pos = 0
        self._sem_free = []
        self._sem_all = []
        self._scope_res = [[]]

    def uid(self, p):
        self._uid += 1
        return f"{p}{self._uid}"

    def sb(self, shape, dtype, name=None):
        name = "s_" + (name or self.uid("sb"))
        t = self.es.enter_context(self.nc.sbuf_tensor(name, list(shape), dtype))
        r = Res(name, t)
        self._scope_res[-1].append(r)
        return r

    def ps(self, shape, dtype=F32, name=None):
        name = "p_" + (name or self.uid("ps"))
        t = self.es.enter_context(self.nc.psum_tensor(name, list(shape), dtype))
        return Res(name, t)

    def dram(self, name, shape, dtype, kind="Internal"):
        return self.nc.dram_tensor(name, list(shape), dtype, kind=kind).ap()

    def dr(self, *key):
        r = self.dres.get(key)
        if r is None:
            r = Res("dr_" + "_".join(str(k) for k in key))
            self.dres[key] = r
        return r

    def sem(self, name):
        self.n_sems += 1
        return self.es.enter_context(self.nc.semaphore(name))

    def _track(self, op, reads, writes):
        deps = op.deps
        for r in reads:
            if r.last_w is not None:
                deps.append(r.last_w)
        for w in writes:
            if w.last_w is not None:
                deps.append(w.last_w)
            deps.extend(w.readers)
        for r in reads:
            r.readers.append(op)
        for w in writes:
            w.last_w = op
            w.readers = []
        self.ops.append(op)

    def op(self, eng, fn, reads=(), writes=(), strict=False):
        o = Op(eng, fn)
        o.strict = strict
        self._track(o, reads, writes)
        return o

    def dma(self, eng, out_ap, in_ap, sbres, reads=(), writes=(), **kw):
        o = Op(eng, lambda e: e.dma_start(out=out_ap, in_=in_ap, **kw))
        o.is_dma = True
        o.dres = sbres
        self._track(o, reads, writes)
        if sbres.dsem is None:
            if self._sem_free:
                sbres.dsem = self._sem_free.pop()
            else:
                sbres.dsem = PhysSem(len(self._sem_all))
                self._sem_all.append(sbres.dsem)
        sbres.dsem.count += 1
        o.dval = 16 * sbres.dsem.count
        o.dsem = sbres.dsem
        return o

    @contextlib.contextmanager
    def scope(self):
        outer = self.es
        inner = contextlib.ExitStack()
        self.es = inner
        self._scope_res.append([])
        try:
            with inner:
                yield
                self.barrier()
                for r in self._scope_res[-1]:
                    if r.dsem is not None:
                        self._sem_free.append(r.dsem)
                        r.dsem = None
        finally:
            self._scope_res.pop()
            self.es = outer

    def barrier(self):
        last = {}
        dmas = []
        for o in self.ops[self._bar_pos:]:
            if o.is_dma:
                dmas.append(o)
            else:
                last[o.eng] = o
        deps = list(last.values()) + dmas
        for e in ENGS:
            o = Op(e, lambda en: en.nop())
            o.deps.extend(deps)
            self.ops.append(o)
        self._bar_pos = len(self.ops)

    def finish(self, final_dmas):
        o = Op("sp", lambda e: e.nop())
        o.deps.extend(final_dmas)
        self.ops.append(o)

    def emit(self):
        nc = self.nc
        engobj = {"pe": nc.tensor, "act": nc.scalar, "dve": nc.vector, "pool": nc.gpsimd, "sp": nc.sync}
        ses = self.same_engine_sync
        for o in self.ops:
            for d in o.deps:
                if not d.is_dma:
                    if d.eng == o.eng and not o.is_dma and not ses and not o.strict:
                        continue
                    d.signal = True
        sidx = {e: 0 for e in ENGS}
        for o in self.ops:
            if not o.is_dma and o.signal:
                sidx[o.eng] += 1
                o.sig_idx = sidx[o.eng]
        esem = {e: self.sem("e_" + e) for e in ENGS}
        for ps_ in self._sem_all:
            ps_.handle = self.sem(f"dq{ps_.idx}")
        waited = {e: {} for e in ENGS}
        n_wait = 0
        for o in self.ops:
            eng = engobj[o.eng]
            w = {}
            for d in o.deps:
                if d.is_dma:
                    key = ("d", d.dsem.idx)
                    val = d.dval
                    sem = d.dsem
                else:
                    if d.eng == o.eng and not o.is_dma and not ses and not o.strict:
                        continue
                    key = ("e", d.eng)
                    val = d.sig_idx
                    sem = esem[d.eng]
                if key not in w or w[key][1] < val:
                    w[key] = (sem, val)
            wd = waited[o.eng]
            for key, (sem, val) in w.items():
                if wd.get(key, 0) >= val:
                    continue
                wd[key] = val
                eng.wait_ge(sem.handle if isinstance(sem, PhysSem) else sem, val)
                n_wait += 1
            if o.is_dma:
                o.fn(eng).then_inc(o.dsem.handle, 16)
            else:
                ins = o.fn(eng)
                if o.signal:
                    ins.then_inc(esem[o.eng], 1)
        self.stats = dict(n_ops=len(self.ops), n_wait=n_wait, n_sems=self.n_sems, sig=dict(sidx))
        return nc


class Pool:
    def __init__(self, ctx, shape, dtype, n, name, psum=False):
        mk = ctx.ps if psum else ctx.sb
        self.bufs = [mk(shape, dtype, name=f"{name}{i}") for i in range(n)]
        self.i = 0

    def get(self):
        b = self.bufs[self.i % len(self.bufs)]
        self.i += 1
        return b


def token_blocks():
    blks = [("c", 0, 0, CT)]
    for i in range(T // 512):
        blks.append(("l", i, CT + i * 512, 512))
    return blks


class Prog:
    def __init__(self, cfg):
        self.cfg = cfg
        self.c = Ctx(same_engine_sync=cfg.get("ses", False))
        self.dump = cfg.get("dump", ())
        self.outs = []

    def scratch(self, name, shape, dtype):
        kind = "ExternalOutput" if name in self.dump else "Internal"
        return self.c.dram(name, shape, dtype, kind)

    def build(self):
        c = self.c
        cfg = self.cfg
        nc = c.nc
        I = lambda n, s, d=F32: c.dram(n, s, d, "ExternalInput")
        self.I = I
        self.xT = I("xT", [NB, D, T])
        self.cxT = I("cxT", [NB, D, CT])
        self.cT = I("cT", [128, 8, 3])
        self.ada_w = I("ada_w", [2, D, 6 * D])
        self.ada_b3 = I("ada_b3", [2, 3, 6 * D])
        self.g1T = I("g1T", [2, 128, 8])
        self.g2T = I("g2T", [2, 128, 8])
        self.gfT = I("gfT", [128, 8])
        self.ffn_up = I("ffn_up", [2, D, 2 * FH])
        self.ffn_down = I("ffn_down", [2, FH, D])
        self.ident_in = I("ident", [128, 128])
        self.declare_mixer_inputs()
        self.outT = c.dram("outT", [NB, D, T], F32, "ExternalOutput")
        self.xres = self.scratch("xres", [NB, D, TT], F32)
        self.hT = self.scratch("hT", [NB, D, TT], BF16)
        self.ycT = self.scratch("ycT", [NB, 2 * D, TT], BF16)
        with c.es:
            self.alloc_common()
            with c.scope():
                self.phase_mods()
            self.copy_in_residual()
            for l in range(2):
                last = l == 1
                if not (l == 0 and cfg.get("skip_ab")) and not (l == 1 and cfg.get("skip_mla")):
                    self.phase_norm(l, which=1, ctx=True)
                    if l == 0:
                        self.layer0_mixer()
                        self.phase_outproj(l, self.ab_w_out, 16, ctx=True)
                    else:
                        self.layer1_mixer()
                        self.phase_outproj(l, self.mla_w_o, 8, ctx=False)
                if cfg.get("stop") == f"mix{l}":
                    break
                self.phase_norm(l, which=2, ctx=not last)
                self.phase_ffn(l, ctx=not last)
                if cfg.get("stop") == f"ffn{l}":
                    break
            self.phase_final()
            c.finish(self.outs)
            c.emit()
        return nc

    def alloc_common(self):
        c = self.c
        self.psA = Pool(c, [128, 512], F32, 4, "psA", psum=True)
        self.psB = Pool(c, [128, 512], F32, 2, "psB", psum=True)
        self.psC = Pool(c, [128, 512], F32, 2, "psC", psum=True)
        self.psum = self.psA
        self.ident = c.sb([128, 128], F32, "ident_sb")
        c.dma("sp", self.ident[:], self.ident_in[:, :], self.ident, writes=[self.ident])
        self.ident_bf = c.sb([128, 128], BF16, "ident_bf")
        c.op("dve", lambda e: e.tensor_copy(self.ident_bf[:], self.ident[:]), [self.ident], [self.ident_bf])
        self.epsc = c.sb([128, 1], F32, "epsc")
        c.op("dve", lambda e: e.memset(self.epsc[:], EPS), [], [self.epsc])
        self.ones_bf = c.sb([128, 128], BF16, "ones_bf")
        c.op("dve", lambda e: e.memset(self.ones_bf[:], 1.0), [], [self.ones_bf])
        self.ones_f = c.sb([128, 128], F32, "ones_f")
        c.op("dve", lambda e: e.memset(self.ones_f[:], 1.0), [], [self.ones_f])
        self.tab = {}
        for l in range(2):
            for nm in ("gs1", "sh1", "gate1", "gs2", "sh2", "gate2"):
                self.tab[(l, nm)] = c.sb([128, 8, 3], F32, f"tab_{nm}{l}")
        self.gf = c.sb([128, 8], F32, "gf")
        c.dma("sp", self.gf[:], self.gfT[:, :], self.gf, writes=[self.gf])

    def alloc_tok_pools(self):
        c = self.c
        u = c.uid("tp")
        self.xpool = Pool(c, [128, 8, 512], F32, 2, u + "xt")
        self.hpool = Pool(c, [128, 8, 512], BF16, 2, u + "ht")
        self.sqpool = Pool(c, [128, 8, 512], BF16, 1, u + "sq")
        self.xnpool = Pool(c, [128, 8, 512], F32, 1, u + "xn")
        self.rspool = Pool(c, [128, 512], F32, 2, u + "rs")

    def phase_mods(self):
        c = self.c
        cT = c.sb([128, 8, 3], F32, "cT")
        sc = c.sb([128, 8, 3], F32, "sc")
        c.dma("sp", cT[:], self.cT[:, :, :], cT, writes=[cT])
        c.op("act", lambda e: e.activation(out=sc[:], in_=cT[:], func=AF.Silu), [cT], [sc])
        wpool = Pool(c, [128, 8, 512], F32, 2, "adaw")
        adab = c.sb([3, 6 * D], F32, "adab")
        modtok = c.sb([3, 6 * D], F32, "modtok")
        for l in range(2):
            g1 = c.sb([128, 8], F32, f"g1_{l}")
            g2 = c.sb([128, 8], F32, f"g2_{l}")
            c.dma("sp", adab[:], self.ada_b3[l], adab, writes=[adab])
            c.dma("sp", g1[:], self.g1T[l], g1, writes=[g1])
            c.dma("sp", g2[:], self.g2T[l], g2, writes=[g2])
            wv = self.ada_w[l].rearrange("(k p) n -> p k n", p=128)
            for nb in range(12):
                wt = wpool.get()
                c.dma("sp", wt[:], wv[:, :, nb * 512:(nb + 1) * 512], wt, writes=[wt])
                ps = self.psum.get()
                for k in range(8):
                    c.op("pe", lambda e, ps=ps, wt=wt, k=k: e.matmul(ps[0:3, :], sc[:, k, :], wt[:, k, :],
                                                                  start=(k == 0), stop=(k == 7)),
                         [sc, wt], [ps])
                c.op("dve", lambda e, ps=ps, nb=nb: e.tensor_tensor(modtok[:, nb * 512:(nb + 1) * 512], ps[0:3, :],
                                                                  adab[:, nb * 512:(nb + 1) * 512], op=ALU.add),
                     [ps, adab], [modtok])
            ps = self.psum.get()
            for m in range(48):
                c.op("pe", lambda e, ps=ps, m=m: e.matmul(ps[:, m * 3:m * 3 + 3], modtok[0:3, m * 128:(m + 1) * 128],
                                                         self.ident[0:3, 0:3], start=True, stop=True),
                     [modtok, self.ident], [ps])
            modT = c.sb([128, 48, 3], F32, f"modT{l}")
            c.op("dve", lambda e, ps=ps, modT=modT: e.tensor_copy(modT[:].rearrange("p a b -> p (a b)"), ps[:, 0:144]),
                 [ps], [modT])
            tb = lambda nm: self.tab[(l, nm)]
            for nm, lo in (("sh1", 0), ("gate1", 16), ("sh2", 24), ("gate2", 40)):
                c.op("dve", lambda e, nm=nm, lo=lo, modT=modT, l=l: e.tensor_copy(self.tab[(l, nm)][:], modT[:, lo:lo + 8, :]),
                     [modT], [tb(nm)])
            for nm, lo, g in (("gs1", 8, g1), ("gs2", 32, g2)):
                c.op("dve", lambda e, nm=nm, lo=lo, g=g, modT=modT, l=l: e.scalar_tensor_tensor(
                    out=self.tab[(l, nm)][:], in0=modT[:, lo:lo + 8, :], scalar=1.0,
                    in1=g[:].unsqueeze(2).to_broadcast([128, 8, 3]), op0=ALU.add, op1=ALU.mult),
                     [modT, g], [tb(nm)])

    def copy_in_residual(self):
        c = self.c
        with c.scope():
            pool = Pool(c, [128, 8, 512], F32, 3, "cpin")
            for b in range(NB):
                for (seg, i, t0, n) in token_blocks():
                    xt = pool.get()
                    src = (self.cxT[b] if seg == "c" else self.xT[b]).rearrange("(k p) t -> p k t", p=128)
                    s0 = 0 if seg == "c" else i * 512
                    c.dma("sp", xt[:, :, :n], src[:, :, s0:s0 + n], xt, writes=[xt])
                    dst = self.xres[b].rearrange("(k p) t -> p k t", p=128)
                    c.dma("sp", dst[:, :, t0:t0 + n], xt[:, :, :n], xt, reads=[xt], writes=[c.dr("xres", b, seg, i)])

    def rstd(self, rs, ps, n, inv_n, rows=128, r0=0):
        c = self.c
        c.op("act", lambda e: e.activation(out=rs[r0:r0 + rows, :n], in_=ps[r0:r0 + rows, :n], func=AF.Sqrt, scale=inv_n,
                                           bias=self.epsc[r0:r0 + rows, :]),
             [ps, self.epsc], [rs])
        c.op("dve", lambda e: e.reciprocal(rs[r0:r0 + rows, :n], rs[r0:r0 + rows, :n]), [rs], [rs])

    def norm_tile(self, xt, n, gs, sh, out_fn):
        c = self.c
        sq = self.sqpool.get()
        c.op("act", lambda e: e.activation(out=sq[:, :, :n], in_=xt[:, :, :n], func=AF.Square), [xt], [sq])
        ps = self.psum.get()
        for k in range(8):
            c.op("pe", lambda e, k=k: e.matmul(ps[:, :n], self.ones_bf[:, :], sq[:, k, :n], start=(k == 0), stop=(k == 7)),
                 [sq, self.ones_bf], [ps])
        rs = self.rspool.get()
        self.rstd(rs, ps, n, 1.0 / D)
        xn = self.xnpool.get()
        c.op("dve", lambda e: e.tensor_tensor(xn[:, :, :n], xt[:, :, :n],
                                              rs[:, :n].unsqueeze(1).to_broadcast([128, 8, n]), op=ALU.mult),
             [xt, rs], [xn])
        for k in range(8):
            oap, ores = out_fn(k)
            if sh is not None:
                c.op("act", lambda e, k=k, oap=oap: e.activation(out=oap, in_=xn[:, k, :n], func=AF.Identity,
                                                                 scale=gs[:, k:k + 1], bias=sh[:, k:k + 1]),
                     [xn], [ores])
            else:
                c.op("act", lambda e, k=k, oap=oap: e.activation(out=oap, in_=xn[:, k, :n], func=AF.Identity,
                                                                 scale=gs[:, k:k + 1]),
                     [xn], [ores])

    def phase_norm(self, l, which, ctx):
        c = self.c
        gs_t = self.tab[(l, f"gs{which}")]
        sh_t = self.tab[(l, f"sh{which}")]
        with c.scope():
            self.alloc_tok_pools()
            for b in range(NB):
                for (seg, i, t0, n) in token_blocks():
                    if seg == "c" and not ctx:
                        continue
                    j = 2 if seg == "c" else b
                    xt = self.xpool.get()
                    src = self.xres[b].rearrange("(k p) t -> p k t", p=128)
                    c.dma("sp", xt[:, :, :n], src[:, :, t0:t0 + n], xt, reads=[c.dr("xres", b, seg, i)], writes=[xt])
                    ht = self.hpool.get()
                    self.norm_tile(xt, n, gs_t[:, :, j], sh_t[:, :, j], lambda k, ht=ht, n=n: (ht[:, k, :n], ht))
                    dst = self.hT[b].rearrange("(k p) t -> p k t", p=128)
                    c.dma("sp", dst[:, :, t0:t0 + n], ht[:, :, :n], ht, reads=[ht], writes=[c.dr("hT", b, seg, i)])

    def phase_ffn(self, l, ctx):
        c = self.c
        HH = FH // 2
        NM = HH // 128
        with c.scope():
            w_up = c.sb([128, 8, 2 * HH], BF16, f"w_up{l}")
            w_dn = c.sb([128, NM, D], BF16, f"w_dn{l}")
            actpool = Pool(c, [128, NM, 512], BF16, 2, f"act{l}_")
            sgpool = Pool(c, [128, 512], F32, 2, f"sg{l}_")
            xpool = Pool(c, [128, 8, 512], F32, 2, f"fx{l}_")
            hpool = Pool(c, [128, 8, 512], BF16, 2, f"fh{l}_")
            gate = self.tab[(l, "gate2")]
            upv = self.ffn_up[l].rearrange("(k p) n -> p k n", p=128)
            dnv = self.ffn_down[l].rearrange("(m p) n -> p m n", p=128)
            for half in range(2):
                for k in range(8):
                    c.dma("pool", w_up[:, k, 0:HH], upv[:, k, half * HH:(half + 1) * HH], w_up, writes=[w_up])
                    c.dma("pool", w_up[:, k, HH:2 * HH], upv[:, k, FH + half * HH:FH + (half + 1) * HH], w_up, writes=[w_up])
                for m in range(NM):
                    c.dma("pool", w_dn[:, m, :], dnv[:, half * NM + m, :], w_dn, writes=[w_dn])
                for b in range(NB):
                    for (seg, i, t0, n) in token_blocks():
                        if seg == "c" and not ctx:
                            continue
                        j = 2 if seg == "c" else b
                        ht = hpool.get()
                        src = self.hT[b].rearrange("(k p) t -> p k t", p=128)
                        c.dma("sp", ht[:, :, :n], src[:, :, t0:t0 + n], ht, reads=[c.dr("hT", b, seg, i)], writes=[ht])
                        xt = xpool.get()
                        xsrc = self.xres[b].rearrange("(k p) t -> p k t", p=128)
                        c.dma("sp", xt[:, :, :n], xsrc[:, :, t0:t0 + n], xt, reads=[c.dr("xres", b, seg, i)], writes=[xt])
                        act = actpool.get()
                        for m in range(NM):
                            pg = self.psA.get()
                            pu = self.psA.get()
                            for k in range(8):
                                c.op("pe", lambda e, pg=pg, k=k, m=m, ht=ht, n=n: e.matmul(
                                    pg[:, :n], w_up[:, k, m * 128:(m + 1) * 128], ht[:, k, :n], start=(k == 0), stop=(k == 7)),
                                     [w_up, ht], [pg])
                            for k in range(8):
                                c.op("pe", lambda e, pu=pu, k=k, m=m, ht=ht, n=n: e.matmul(
                                    pu[:, :n], w_up[:, k, HH + m * 128:HH + (m + 1) * 128], ht[:, k, :n], start=(k == 0), stop=(k == 7)),
                                     [w_up, ht], [pu])
                            sg = sgpool.get()
                            c.op("act", lambda e, sg=sg, pg=pg, n=n: e.activation(out=sg[:, :n], in_=pg[:, :n], func=AF.Silu), [pg], [sg])
                            c.op("dve", lambda e, sg=sg, pu=pu, act=act, m=m, n=n: e.tensor_tensor(
                                act[:, m, :n], sg[:, :n], pu[:, :n], op=ALU.mult), [sg, pu], [act])
                        for k in range(8):
                            po = self.psB.get()
                            for m in range(NM):
                                c.op("pe", lambda e, po=po, k=k, m=m, act=act, n=n: e.matmul(
                                    po[:, :n], w_dn[:, m, k * 128:(k + 1) * 128], act[:, m, :n], start=(m == 0), stop=(m == NM - 1)),
                                     [w_dn, act], [po])
                            c.op("dve", lambda e, po=po, k=k, xt=xt, n=n, j=j: e.scalar_tensor_tensor(
                                out=xt[:, k, :n], in0=po[:, :n], scalar=gate[:, k, j:j + 1], in1=xt[:, k, :n],
                                op0=ALU.mult, op1=ALU.add), [po, xt, gate], [xt])
                        c.dma("sp", xsrc[:, :, t0:t0 + n], xt[:, :, :n], xt, reads=[xt], writes=[c.dr("xres", b, seg, i)])

    def phase_outproj(self, l, w_dram, nch, ctx):
        c = self.c
        with c.scope():
            w = c.sb([128, nch, D], BF16, f"wo{l}")
            wv = w_dram.rearrange("(m p) n -> p m n", p=128)
            for m in range(nch):
                c.dma("pool", w[:, m, :], wv[:, m, :], w, writes=[w])
            ypool = Pool(c, [128, nch, 512], BF16, 2, f"oy{l}_")
            xpool = Pool(c, [128, 8, 512], F32, 2, f"ox{l}_")
            gate = self.tab[(l, "gate1")]
            for b in range(NB):
                for (seg, i, t0, n) in token_blocks():
                    if seg == "c" and not ctx:
                        continue
                    j = 2 if seg == "c" else b
                    yt = ypool.get()
                    src = self.ycT[b].rearrange("(k p) t -> p k t", p=128)
                    c.dma("sp", yt[:, :, :n], src[:, 0:nch, t0:t0 + n], yt,
                          reads=[c.dr("ycT", b, seg, i, 0), c.dr("ycT", b, seg, i, 1)], writes=[yt])
                    xt = xpool.get()
                    xsrc = self.xres[b].rearrange("(k p) t -> p k t", p=128)
                    c.dma("sp", xt[:, :, :n], xsrc[:, :, t0:t0 + n], xt, reads=[c.dr("xres", b, seg, i)], writes=[xt])
                    for k in range(8):
                        po = self.psA.get()
                        for m in range(nch):
                            c.op("pe", lambda e, po=po, k=k, m=m, yt=yt, n=n: e.matmul(
                                po[:, :n], w[:, m, k * 128:(k + 1) * 128], yt[:, m, :n], start=(m == 0), stop=(m == nch - 1)),
                                 [w, yt], [po])
                        c.op("dve", lambda e, po=po, k=k, xt=xt, n=n, j=j: e.scalar_tensor_tensor(
                            out=xt[:, k, :n], in0=po[:, :n], scalar=gate[:, k, j:j + 1], in1=xt[:, k, :n],
                            op0=ALU.mult, op1=ALU.add), [po, xt, gate], [xt])
                    c.dma("sp", xsrc[:, :, t0:t0 + n], xt[:, :, :n], xt, reads=[xt], writes=[c.dr("xres", b, seg, i)])

    def phase_final(self):
        c = self.c
        with c.scope():
            self.alloc_tok_pools()
            for b in range(NB):
                for (seg, i, t0, n) in token_blocks():
                    if seg == "c":
                        continue
                    xt = self.xpool.get()
                    src = self.xres[b].rearrange("(k p) t -> p k t", p=128)
                    c.dma("sp", xt[:, :, :n], src[:, :, t0:t0 + n], xt, reads=[c.dr("xres", b, seg, i)], writes=[xt])
                    ot = self.xpool.get()
                    self.norm_tile(xt, n, self.gf[:, :], None, lambda k, ot=ot, n=n: (ot[:, k, :n], ot))
                    dst = self.outT[b].rearrange("(k p) t -> p k t", p=128)
                    d = c.dma("sp", dst[:, :, i * 512:i * 512 + n], ot[:, :, :n], ot, reads=[ot])
                    self.outs.append(d)
    def declare_mixer_inputs(self):
        I = self.I
        self.mla_w_in = I("mla_w_in", [D, 768])
        self.mla_w_uq = I("mla_w_uq", [384, 16 * 128])
        self.mla_w_uk = I("mla_w_uk", [256, 1024])
        self.mla_w_uv = I("mla_w_uv", [256, 1024])
        self.mla_w_o = I("mla_w_o", [1024, 1024])
        self.mla_gq = I("mla_gq", [128, 3])
        self.mla_gkv = I("mla_gkv", [128, 2])
        self.cs1 = I("cs1", [128, T])
        self.sn1 = I("sn1", [128, T])
        self.Q1T = self.scratch("Q1T", [NB, 16, 96, TT], BF16)
        self.K1T = self.scratch("K1T", [NB, 16, 96, TT], BF16)
        self.V1 = self.scratch("V1", [NB, TT, 16, 65], BF16)
        self.w_ssd = I("w_ssd", [D, 3104])
        self.w_att = I("w_att", [D, 2816])
        self.ab_w_out = I("ab_w_out", [2 * D, D])
        self.convw = I("convw", [128, 16, 3])
        self.convb = I("convb", [128, 16])
        self.dtb_bc = I("dtb_bc", [128, 32])
        self.alog_bc = I("alog_bc", [128, 32])
        self.dsk_bc = I("dsk_bc", [128, 16])
        self.ssdg_bc = I("ssdg_bc", [128, 1024])
        self.masks = I("masks", [4, 128, 128])
        self.blk64 = I("blk64", [128, 128])
        self.gqk = I("gqk", [128, 4])
        self.cs0 = I("cs0", [128, T])
        self.sn0 = I("sn0", [128, T])
        self.Q0T = self.scratch("Q0T", [NB, 16, 64, TT], BF16)
        self.K0T = self.scratch("K0T", [NB, 4, 64, TT], BF16)
        self.V0 = self.scratch("V0", [NB, TT, 4, 65], BF16)
        self.xB = self.scratch("xB", [NB, TT, 1536], BF16)
        self.BCT = self.scratch("BCT", [NB, 8, 128, TT], BF16)
        self.zs = self.scratch("zs", [NB, TT, 1024], F32)
        self.dtr = self.scratch("dtr", [NB, TT, 32], F32)
        self.yf = self.scratch("yf", [NB, TT, 1024], F32)
        if "ybd" in self.dump:
            self.ybd = self.scratch("ybd", [NB, TT, 1024], F32)

    def attention(self, tag, b, H, n_kv, dk, scale, KT_d, QT_d, V_d, row0, qsets):
        c = self.c
        with c.scope():
            KT = c.sb([dk, n_kv, TT], BF16, f"KT{tag}")
            VA = c.sb([128, 18, n_kv, 65], BF16, f"VA{tag}")
            for g in range(n_kv):
                c.dma("sp", KT[:, g, :], KT_d[b, g], KT, writes=[KT])
            vv = V_d[b].rearrange("(tt p) g e -> p tt g e", p=128)
            for tt in range(18):
                c.dma("sp", VA[:, tt, :, :], vv[:, tt, :, :], VA, writes=[VA])
            qpool = Pool(c, [dk, 512], BF16, 3, f"aq{tag}")
            epool = Pool(c, [128, 512], BF16, 6, f"ae{tag}")
            opool = Pool(c, [128, 8, 512], BF16, 2, f"ao{tag}")
            rcp = Pool(c, [128, 512], F32, 2, f"ar{tag}")
            bcp = Pool(c, [64, 512], F32, 2, f"ab{tag}")
            gsz = H // n_kv
            LA = 3
            for (seg, i, t0, n, nkt) in qsets:
                ot = opool.get()
                items = [(h, kt) for h in range(H) for kt in range(nkt)]
                qts, pos, ets = {}, {}, {}

                def emit_s(h, kt):
                    g = h // gsz
                    if kt == 0:
                        qt = qpool.get()
                        c.dma("sp", qt[:, :n], QT_d[b, h, :, t0:t0 + n], qt, writes=[qt])
                        qts[h] = qt
                    qt = qts[h]
                    pss = self.psA.get()
                    c.op("pe", lambda e, pss=pss, g=g, kt=kt, qt=qt, n=n: e.matmul(
                        pss[:, :n], KT[:, g, kt * 128:(kt + 1) * 128], qt[:, :n], start=True, stop=True), [KT, qt], [pss])
                    et = epool.get()
                    c.op("act", lambda e, et=et, pss=pss, n=n: e.activation(out=et[:, :n], in_=pss[:, :n], func=AF.Exp, scale=scale),
                         [pss], [et])
                    ets[(h, kt)] = et

                def emit_pv(h, kt):
                    g = h // gsz
                    if kt == 0:
                        pos[h] = self.psB.get()
                    po = pos[h]
                    et = ets.pop((h, kt))
                    c.op("pe", lambda e, po=po, g=g, kt=kt, et=et, n=n, nkt=nkt: e.matmul(
                        po[0:65, :n], VA[:, kt, g, :], et[:, :n], start=(kt == 0), stop=(kt == nkt - 1)), [VA, et], [po])
                    if kt == nkt - 1:
                        rc = rcp.get()
                        c.op("dve", lambda e, rc=rc, po=po, n=n: e.reciprocal(rc[64:65, :n], po[64:65, :n]), [po], [rc])
                        pb = self.psC.get()
                        c.op("pe", lambda e, pb=pb, rc=rc, n=n: e.matmul(pb[0:64, :n], self.ones_f[64:65, 0:64], rc[64:65, :n],
                                                                   start=True, stop=True), [rc, self.ones_f], [pb])
                        bs = bcp.get()
                        c.op("act", lambda e, bs=bs, pb=pb, n=n: e.activation(out=bs[0:64, :n], in_=pb[0:64, :n], func=AF.Copy), [pb], [bs])
                        r = (h % 2) * 64
                        c.op("dve", lambda e, r=r, h=h, po=po, bs=bs, n=n, ot=ot: e.tensor_tensor(
                            ot[r:r + 64, h // 2, :n], po[0:64, :n], bs[0:64, :n], op=ALU.mult), [po, bs], [ot])

                for j in range(len(items) + LA):
                    if j < len(items):
                        emit_s(*items[j])
                    if j >= LA:
                        emit_pv(*items[j - LA])
                dst = self.ycT[b][row0:row0 + 1024, :].rearrange("(k p) t -> p k t", p=128)
                c.dma("sp", dst[:, :, t0:t0 + n], ot[:, :, :n], ot, reads=[ot], writes=[c.dr("ycT", b, seg, i, row0 // 1024)])

    def layer1_mixer(self):
        c = self.c
        blks = token_blocks()
        with c.scope():
            w_in = c.sb([128, 8, 768], BF16, "m_w_in")
            w_uq = c.sb([128, 3, 2048], BF16, "m_w_uq")
            w_uk = c.sb([128, 2, 1024], BF16, "m_w_uk")
            w_uv = c.sb([128, 2, 1024], BF16, "m_w_uv")
            gq = c.sb([128, 3], F32, "m_gq")
            gkv = c.sb([128, 2], F32, "m_gkv")
            cs = c.sb([128, T], F32, "m_cs")
            sn = c.sb([128, T], F32, "m_sn")
            wv = self.mla_w_in.rearrange("(k p) n -> p k n", p=128)
            for k in range(8):
                c.dma("pool", w_in[:, k, :], wv[:, k, :], w_in, writes=[w_in])
            for (wt, src) in ((w_uq, self.mla_w_uq), (w_uk, self.mla_w_uk), (w_uv, self.mla_w_uv)):
                sv = src.rearrange("(k p) n -> p k n", p=128)
                for k in range(sv.shape[1]):
                    c.dma("pool", wt[:, k, :], sv[:, k, :], wt, writes=[wt])
            c.dma("sp", gq[:], self.mla_gq[:, :], gq, writes=[gq])
            c.dma("sp", gkv[:], self.mla_gkv[:, :], gkv, writes=[gkv])
            c.dma("sp", cs[:], self.cs1[:, :], cs, writes=[cs])
            c.dma("sp", sn[:], self.sn1[:, :], sn, writes=[sn])
            hTb = c.sb([128, 8, TT], BF16, "m_hT")
            cqn = c.sb([128, 3, TT], BF16, "m_cqn")
            ckvn = c.sb([128, 2, TT], BF16, "m_ckvn")
            kper = c.sb([128, TT], BF16, "m_kper")
            rawp = Pool(c, [128, 3, 512], F32, 2, "m_raw")
            sqp = Pool(c, [128, 3, 512], BF16, 2, "m_sq")
            rsp = Pool(c, [128, 512], F32, 2, "m_rs")
            tmp = Pool(c, [128, 512], F32, 4, "m_tmp")
            qop = Pool(c, [128, 512], BF16, 3, "m_qo")
            khp = Pool(c, [96, TT], BF16, 2, "m_kh")
            vtp = Pool(c, [128, 16, 65], BF16, 3, "m_vt")
            for b in range(NB):
                hv = self.hT[b].rearrange("(k p) t -> p k t", p=128)
                for k in range(8):
                    c.dma("sp", hTb[:, k, :], hv[:, k, :], hTb, writes=[hTb])
                for (seg, i, t0, n) in blks:
                    for (c0, nc_, gt, dst, inv) in ((0, 3, gq, cqn, 1.0 / 384), (3, 2, gkv, ckvn, 1.0 / 256)):
                        raw = rawp.get()
                        sq = sqp.get()
                        for cc in range(nc_):
                            ps = self.psA.get()
                            for k in range(8):
                                c.op("pe", lambda e, ps=ps, k=k, cc=cc, c0=c0, t0=t0, n=n: e.matmul(
                                    ps[:, :n], w_in[:, k, (c0 + cc) * 128:(c0 + cc + 1) * 128], hTb[:, k, t0:t0 + n],
                                    start=(k == 0), stop=(k == 7)), [w_in, hTb], [ps])
                            c.op("act", lambda e, ps=ps, raw=raw, cc=cc, n=n: e.activation(out=raw[:, cc, :n], in_=ps[:, :n], func=AF.Copy),
                                 [ps], [raw])
                            c.op("act", lambda e, ps=ps, sq=sq, cc=cc, n=n: e.activation(out=sq[:, cc, :n], in_=ps[:, :n], func=AF.Square),
                                 [ps], [sq])
                        pss = self.psC.get()
                        for cc in range(nc_):
                            c.op("pe", lambda e, pss=pss, sq=sq, cc=cc, n=n, nc_=nc_: e.matmul(
                                pss[:, :n], self.ones_bf[:, :], sq[:, cc, :n], start=(cc == 0), stop=(cc == nc_ - 1)),
                                 [sq, self.ones_bf], [pss])
                        rs = rsp.get()
                        self.rstd(rs, pss, n, inv)
                        for cc in range(nc_):
                            c.op("dve", lambda e, dst=dst, raw=raw, rs=rs, gt=gt, cc=cc, t0=t0, n=n: e.scalar_tensor_tensor(
                                out=dst[:, cc, t0:t0 + n], in0=raw[:, cc, :n], scalar=gt[:, cc:cc + 1], in1=rs[:, :n],
                                op0=ALU.mult, op1=ALU.mult), [raw, rs, gt], [dst])
                    ps = self.psA.get()
                    for k in range(8):
                        c.op("pe", lambda e, ps=ps, k=k, t0=t0, n=n: e.matmul(
                            ps[:, :n], w_in[:, k, 640:768], hTb[:, k, t0:t0 + n], start=(k == 0), stop=(k == 7)), [w_in, hTb], [ps])
                    if seg == "c":
                        c.op("act", lambda e, ps=ps, t0=t0, n=n: e.activation(out=kper[64:96, t0:t0 + n], in_=ps[64:96, :n], func=AF.Copy),
                             [ps], [kper])
                    else:
                        l0 = t0 - CT
                        self.rope32(ps, kper, t0, n, l0, cs, sn, tmp)
                for h in range(16):
                    for (seg, i, t0, n) in blks:
                        if seg == "c":
                            continue
                        ps = self.psA.get()
                        for kk in range(3):
                            c.op("pe", lambda e, ps=ps, kk=kk, h=h, t0=t0, n=n: e.matmul(
                                ps[:, :n], w_uq[:, kk, h * 128:(h + 1) * 128], cqn[:, kk, t0:t0 + n], start=(kk == 0), stop=(kk == 2)),
                                 [w_uq, cqn], [ps])
                        qo = qop.get()
                        c.op("act", lambda e, ps=ps, qo=qo, n=n: e.activation(out=qo[0:64, :n], in_=ps[0:64, :n], func=AF.Copy), [ps], [qo])
                        self.rope32(ps, qo, 0, n, t0 - CT, cs, sn, tmp)
                        c.dma("sp", self.Q1T[b, h, :, t0:t0 + n], qo[0:96, :n], qo, reads=[qo])
                    kh = khp.get()
                    for (seg, i, t0, n) in blks:
                        ps = self.psA.get()
                        for kk in range(2):
                            c.op("pe", lambda e, ps=ps, kk=kk, h=h, t0=t0, n=n: e.matmul(
                                ps[0:64, :n], w_uk[:, kk, h * 64:(h + 1) * 64], ckvn[:, kk, t0:t0 + n], start=(kk == 0), stop=(kk == 1)),
                                 [w_uk, ckvn], [ps])
                        c.op("act", lambda e, ps=ps, kh=kh, t0=t0, n=n: e.activation(out=kh[0:64, t0:t0 + n], in_=ps[0:64, :n], func=AF.Copy),
                             [ps], [kh])
                    c.op("dve", lambda e, kh=kh: e.tensor_copy(kh[64:96, :], kper[64:96, :]), [kper], [kh])
                    c.dma("sp", self.K1T[b, h], kh[:, :], kh, reads=[kh])
                for tt in range(18):
                    vt = vtp.get()
                    c.op("pool", lambda e, vt=vt: e.memset(vt[:], 1.0), [], [vt])
                    for hf in range(2):
                        ps = self.psA.get()
                        for kk in range(2):
                            c.op("pe", lambda e, ps=ps, kk=kk, hf=hf, tt=tt: e.matmul(
                                ps[:, :], ckvn[:, kk, tt * 128:(tt + 1) * 128], w_uv[:, kk, hf * 512:(hf + 1) * 512],
                                start=(kk == 0), stop=(kk == 1)), [ckvn, w_uv], [ps])
                        c.op("act", lambda e, ps=ps, vt=vt, hf=hf: e.activation(
                            out=vt[:, hf * 8:(hf + 1) * 8, 0:64], in_=ps[:, :].rearrange("p (h d) -> p h d", d=64), func=AF.Copy),
                             [ps], [vt])
                    c.dma("sp", self.V1[b, tt * 128:(tt + 1) * 128], vt[:], vt, reads=[vt])
        for b in range(NB):
            qsets = [(seg, i, t0, n, 18) for (seg, i, t0, n) in blks if seg == "l"]
            self.attention(f"m{b}", b, 16, 16, 96, 96 ** -0.5, self.K1T, self.Q1T, self.V1, 0, qsets)

    def rope32(self, ps, dst, d0, n, l0, cs, sn, tmp):
        c = self.c
        t1 = tmp.get()
        t2 = tmp.get()
        c.op("act", lambda e: e.activation(out=t1[64:96, :n], in_=ps[96:128, :n], func=AF.Copy), [ps], [t1])
        c.op("dve", lambda e: e.tensor_tensor(t1[64:96, :n], t1[64:96, :n], sn[64:96, l0:l0 + n], op=ALU.mult), [t1, sn], [t1])
        c.op("dve", lambda e: e.tensor_tensor(t2[64:96, :n], ps[64:96, :n], cs[64:96, l0:l0 + n], op=ALU.mult), [ps, cs], [t2])
        c.op("dve", lambda e: e.tensor_tensor(dst[64:96, d0:d0 + n], t1[64:96, :n], t2[64:96, :n], op=ALU.add), [t1, t2], [dst])
    def layer0_mixer(self):
        self.l0_proj_ssd()
        self.l0_proj_att()
        self.l0_ssd_scan()
        blks = token_blocks()
        for b in range(NB):
            qsets = [(seg, i, t0, n, 2 if seg == "c" else 18) for (seg, i, t0, n) in blks]
            self.attention(f"a{b}", b, 16, 4, 64, 0.125, self.K0T, self.Q0T, self.V0, 1024, qsets)

    def load_hTb(self, hTb, b):
        c = self.c
        hv = self.hT[b].rearrange("(k p) t -> p k t", p=128)
        for k in range(8):
            c.dma("sp", hTb[:, k, :], hv[:, k, :], hTb, writes=[hTb])

    def l0_proj_ssd(self):
        c = self.c
        blks = token_blocks()
        with c.scope():
            w = c.sb([128, 8, 3104], BF16, "s_w")
            wv = self.w_ssd.rearrange("(k p) n -> p k n", p=128)
            for k in range(8):
                c.dma("pool", w[:, k, :], wv[:, k, :], w, writes=[w])
            cw = c.sb([128, 16, 3], F32, "s_cw")
            cb = c.sb([128, 16], F32, "s_cb")
            c.dma("sp", cw[:], self.convw[:, :, :], cw, writes=[cw])
            c.dma("sp", cb[:], self.convb[:, :], cb, writes=[cb])
            hTb = c.sb([128, 8, TT], BF16, "s_hT")
            rawp = Pool(c, [128, TT], F32, 2, "s_raw")
            yp = Pool(c, [128, TT], F32, 1, "s_y")
            ysp = Pool(c, [128, TT], BF16, 2, "s_ys")
            xBt = c.sb([128, 18, 1536], BF16, "s_xBt")
            ztp = Pool(c, [128, 1024], F32, 2, "s_zt")
            dta = c.sb([128, 18, 32], F32, "s_dta")
            for b in range(NB):
                self.load_hTb(hTb, b)
                for cc in range(16):
                    raw = rawp.get()
                    for (seg, i, t0, n) in blks:
                        ps = self.psA.get()
                        for k in range(8):
                            c.op("pe", lambda e, ps=ps, k=k, cc=cc, t0=t0, n=n: e.matmul(
                                ps[:, :n], w[:, k, cc * 128:(cc + 1) * 128], hTb[:, k, t0:t0 + n], start=(k == 0), stop=(k == 7)),
                                 [w, hTb], [ps])
                        c.op("act", lambda e, ps=ps, raw=raw, t0=t0, n=n: e.activation(out=raw[:, t0:t0 + n], in_=ps[:, :n], func=AF.Copy),
                             [ps], [raw])
                    y = yp.get()
                    c.op("act", lambda e, y=y, raw=raw, cc=cc: e.activation(out=y[:, :], in_=raw[:, :], func=AF.Identity,
                                                                           scale=cw[:, cc, 1:2], bias=cb[:, cc:cc + 1]), [raw, cw, cb], [y])
                    for (s0, s1) in ((0, CT), (CT, TT)):
                        c.op("dve", lambda e, y=y, raw=raw, cc=cc, s0=s0, s1=s1: e.scalar_tensor_tensor(
                            out=y[:, s0 + 1:s1], in0=raw[:, s0:s1 - 1], scalar=cw[:, cc, 0:1], in1=y[:, s0 + 1:s1],
                            op0=ALU.mult, op1=ALU.add), [raw, y, cw], [y])
                        c.op("dve", lambda e, y=y, raw=raw, cc=cc, s0=s0, s1=s1: e.scalar_tensor_tensor(
                            out=y[:, s0:s1 - 1], in0=raw[:, s0 + 1:s1], scalar=cw[:, cc, 2:3], in1=y[:, s0:s1 - 1],
                            op0=ALU.mult, op1=ALU.add), [raw, y, cw], [y])
                    ys = ysp.get()
                    c.op("act", lambda e, ys=ys, y=y: e.activation(out=ys[:, :], in_=y[:, :], func=AF.Silu), [y], [ys])
                    if cc < 12:
                        for t4 in range(0, 18, 4):
                            nt = min(4, 18 - t4)
                            ps = self.psA.get()
                            for q in range(nt):
                                tt = t4 + q
                                c.op("pe", lambda e, ps=ps, ys=ys, tt=tt, q=q: e.matmul(
                                    ps[:, q * 128:(q + 1) * 128], ys[:, tt * 128:(tt + 1) * 128], self.ident_bf[:, :], start=True, stop=True),
                                     [ys, self.ident_bf], [ps])
                            c.op("dve", lambda e, ps=ps, t4=t4, nt=nt, cc=cc: e.tensor_copy(
                                xBt[:, t4:t4 + nt, cc * 128:(cc + 1) * 128], ps[:, 0:nt * 128].rearrange("p (a b) -> p a b", b=128)),
                                 [ps], [xBt])
                    if cc >= 8:
                        c.dma("sp", self.BCT[b, cc - 8], ys[:, :], ys, reads=[ys])
                c.dma("sp", self.xB[b].rearrange("(tt p) f -> p tt f", p=128), xBt[:], xBt, reads=[xBt])
                for tt in range(18):
                    zt = ztp.get()
                    for hf in range(2):
                        ps = self.psA.get()
                        for k in range(8):
                            c.op("pe", lambda e, ps=ps, k=k, tt=tt, hf=hf: e.matmul(
                                ps[:, :], hTb[:, k, tt * 128:(tt + 1) * 128], w[:, k, 2048 + hf * 512:2048 + (hf + 1) * 512],
                                start=(k == 0), stop=(k == 7)), [w, hTb], [ps])
                        c.op("act", lambda e, ps=ps, zt=zt, hf=hf: e.activation(out=zt[:, hf * 512:(hf + 1) * 512], in_=ps[:, :], func=AF.Silu),
                             [ps], [zt])
                    c.dma("sp", self.zs[b, tt * 128:(tt + 1) * 128, :], zt[:], zt, reads=[zt])
                    ps = self.psA.get()
                    for k in range(8):
                        c.op("pe", lambda e, ps=ps, k=k, tt=tt: e.matmul(
                            ps[:, 0:32], hTb[:, k, tt * 128:(tt + 1) * 128], w[:, k, 3072:3104], start=(k == 0), stop=(k == 7)),
                             [w, hTb], [ps])
                    c.op("dve", lambda e, ps=ps, tt=tt: e.tensor_copy(dta[:, tt, :], ps[:, 0:32]), [ps], [dta])
                c.dma("sp", self.dtr[b].rearrange("(tt p) f -> p tt f", p=128), dta[:], dta, reads=[dta])

    def l0_proj_att(self):
        c = self.c
        blks = token_blocks()
        with c.scope():
            w = c.sb([128, 8, 2816], BF16, "a_w")
            wv = self.w_att.rearrange("(k p) n -> p k n", p=128)
            for k in range(8):
                c.dma("pool", w[:, k, :], wv[:, k, :], w, writes=[w])
            hTb = c.sb([128, 8, TT], BF16, "a_hT")
            cs = c.sb([128, T], F32, "a_cs")
            sn = c.sb([128, T], F32, "a_sn")
            gqk = c.sb([128, 4], F32, "a_gqk")
            blk = c.sb([128, 128], F32, "a_blk")
            blkb = c.sb([128, 128], BF16, "a_blkb")
            c.dma("sp", cs[:], self.cs0[:, :], cs, writes=[cs])
            c.dma("sp", sn[:], self.sn0[:, :], sn, writes=[sn])
            c.dma("sp", gqk[:], self.gqk[:, :], gqk, writes=[gqk])
            c.dma("sp", blk[:], self.blk64[:, :], blk, writes=[blk])
            c.op("dve", lambda e: e.tensor_copy(blkb[:], blk[:]), [blk], [blkb])
            tabs = []
            for j, src in enumerate((cs, sn, cs, sn)):
                t = c.sb([128, T], F32, f"a_tab{j}")
                c.op("pool", lambda e, t=t, src=src, j=j: e.tensor_scalar(t[:], src[:], gqk[:, j:j + 1], None, op0=ALU.mult), [src, gqk], [t])
                tabs.append(t)
            sqp = Pool(c, [128, 512], BF16, 2, "a_sq")
            rsp = Pool(c, [128, 512], F32, 2, "a_rs")
            tp = Pool(c, [128, 512], F32, 4, "a_t")
            op_ = Pool(c, [128, 512], BF16, 3, "a_o")
            vtp = Pool(c, [128, 4, 65], BF16, 3, "a_vt")
            for b in range(NB):
                self.load_hTb(hTb, b)
                for cc in range(10):
                    isq = cc < 8
                    c0 = cc * 128 if isq else 2048 + (cc - 8) * 128
                    r0 = c0 + (1024 if isq else 256)
                    gc, gs = (tabs[0], tabs[1]) if isq else (tabs[2], tabs[3])
                    gcol = 0 if isq else 2
                    for (seg, i, t0, n) in blks:
                        psq = self.psA.get()
                        for k in range(8):
                            c.op("pe", lambda e, ps=psq, k=k, c0=c0, t0=t0, n=n: e.matmul(
                                ps[:, :n], w[:, k, c0:c0 + 128], hTb[:, k, t0:t0 + n], start=(k == 0), stop=(k == 7)), [w, hTb], [psq])
                        sq = sqp.get()
                        c.op("act", lambda e, sq=sq, ps=psq, n=n: e.activation(out=sq[:, :n], in_=ps[:, :n], func=AF.Square), [psq], [sq])
                        pss = self.psC.get()
                        c.op("pe", lambda e, pss=pss, sq=sq, n=n: e.matmul(pss[:, :n], blkb[:, :], sq[:, :n], start=True, stop=True),
                             [blkb, sq], [pss])
                        rs = rsp.get()
                        self.rstd(rs, pss, n, 1.0 / 64)
                        o = op_.get()
                        if seg == "c":
                            c.op("dve", lambda e, o=o, ps=psq, rs=rs, gcol=gcol, n=n: e.scalar_tensor_tensor(
                                out=o[:, :n], in0=ps[:, :n], scalar=gqk[:, gcol:gcol + 1], in1=rs[:, :n], op0=ALU.mult, op1=ALU.mult),
                                 [psq, rs, gqk], [o])
                        else:
                            l0 = t0 - CT
                            psr = self.psA.get()
                            for k in range(8):
                                c.op("pe", lambda e, ps=psr, k=k, r0=r0, t0=t0, n=n: e.matmul(
                                    ps[:, :n], w[:, k, r0:r0 + 128], hTb[:, k, t0:t0 + n], start=(k == 0), stop=(k == 7)), [w, hTb], [psr])
                            t1 = tp.get()
                            t2 = tp.get()
                            c.op("dve", lambda e, t1=t1, ps=psq, gc=gc, l0=l0, n=n: e.tensor_tensor(t1[:, :n], ps[:, :n], gc[:, l0:l0 + n], op=ALU.mult),
                                 [psq, gc], [t1])
                            c.op("dve", lambda e, t2=t2, ps=psr, gs=gs, l0=l0, n=n: e.tensor_tensor(t2[:, :n], ps[:, :n], gs[:, l0:l0 + n], op=ALU.mult),
                                 [psr, gs], [t2])
                            c.op("pool", lambda e, t1=t1, t2=t2, n=n: e.tensor_tensor(t1[:, :n], t1[:, :n], t2[:, :n], op=ALU.add), [t1, t2], [t1])
                            c.op("dve", lambda e, o=o, t1=t1, rs=rs, n=n: e.tensor_tensor(o[:, :n], t1[:, :n], rs[:, :n], op=ALU.mult), [t1, rs], [o])
                        for hh in range(2):
                            if isq:
                                dst = self.Q0T[b, 2 * cc + hh, :, t0:t0 + n]
                            else:
                                dst = self.K0T[b, 2 * (cc - 8) + hh, :, t0:t0 + n]
                            c.dma("sp", dst, o[hh * 64:(hh + 1) * 64, :n], o, reads=[o])
                for tt in range(18):
                    vt = vtp.get()
                    c.op("pool", lambda e, vt=vt: e.memset(vt[:], 1.0), [], [vt])
                    ps = self.psA.get()
                    for k in range(8):
                        c.op("pe", lambda e, ps=ps, k=k, tt=tt: e.matmul(
                            ps[:, 0:256], hTb[:, k, tt * 128:(tt + 1) * 128], w[:, k, 2560:2816], start=(k == 0), stop=(k == 7)),
                             [w, hTb], [ps])
                    c.op("act", lambda e, ps=ps, vt=vt: e.activation(out=vt[:, :, 0:64], in_=ps[:, 0:256].rearrange("p (h d) -> p h d", d=64),
                                                                     func=AF.Copy), [ps], [vt])
                    c.dma("sp", self.V0[b, tt * 128:(tt + 1) * 128], vt[:], vt, reads=[vt])

    def l0_ssd_scan(self):
        c = self.c
        with c.scope():
            S = type("S", (), {})()
            mk = c.sb([128, 4, 128], F32, "d_masks")
            c.dma("sp", mk[:], self.masks.rearrange("m p f -> p m f"), mk, writes=[mk])
            S.mk = mk
            dtb = c.sb([128, 32], F32, "d_dtb")
            alog = c.sb([128, 32], F32, "d_alog")
            S.abc = c.sb([128, 32], F32, "d_abc")
            S.dsk = c.sb([128, 16], F32, "d_dsk")
            S.ssdg = c.sb([128, 1024], F32, "d_ssdg")
            S.onec = c.sb([128, 1], F32, "d_onec")
            c.op("dve", lambda e: e.memset(S.onec[:], 1.0), [], [S.onec])
            c.dma("sp", dtb[:], self.dtb_bc[:, :], dtb, writes=[dtb])
            c.dma("sp", alog[:], self.alog_bc[:, :], alog, writes=[alog])
            c.dma("sp", S.dsk[:], self.dsk_bc[:, :], S.dsk, writes=[S.dsk])
            c.dma("sp", S.ssdg[:], self.ssdg_bc[:, :], S.ssdg, writes=[S.ssdg])
            c.op("act", lambda e: e.activation(out=S.abc[:], in_=alog[:], func=AF.Exp), [alog], [S.abc])
            c.op("dve", lambda e: e.tensor_scalar(S.abc[:], S.abc[:], -1.0, None, op0=ALU.mult), [S.abc], [S.abc])
            S.dtb = dtb
            S.hst = c.sb([128, 1024], F32, "d_hst")
            S.hbf = c.sb([128, 1024], BF16, "d_hbf")
            S.xbp = Pool(c, [128, 1536], BF16, 2, "d_xb")
            S.bcp = Pool(c, [128, 8, 128], BF16, 2, "d_bc")
            S.dtp = Pool(c, [128, 32], F32, 2, "d_dt")
            S.smp = Pool(c, [128, 32], F32, 16, "d_sm")
            S.xdtp = Pool(c, [128, 16, 64], BF16, 4, "d_xdt")
            S.yop = Pool(c, [128, 1024], F32, 2, "d_yo")
            S.cbp = Pool(c, [128, 128], F32, 2, "d_cb")
            S.rgp = Pool(c, [128, 4, 128], F32, 2, "d_rg")
            S.ep = Pool(c, [128, 512], F32, 2, "d_e")
            S.mtp = Pool(c, [128, 4, 128], BF16, 2, "d_mt")
            S.ydp = Pool(c, [128, 1024], F32, 3, "d_yd")
            S.zp = Pool(c, [128, 1024], F32, 2, "d_z")
            S.ynp = Pool(c, [128, 1024], BF16, 2, "d_yn")
            S.ytp = Pool(c, [128, 8, 128], BF16, 2, "d_yt")
            S.pX = self.psA
            S.pY = self.psB
            S.pZ = self.psC
            for b in range(NB):
                for dr_ in range(2):
                    c.op("dve", lambda e: e.memset(S.hst[:], 0.0), [], [S.hst])
                    c.op("dve", lambda e: e.memset(S.hbf[:], 0.0), [], [S.hbf])
                    order = list(range(18)) if dr_ == 0 else [1, 0] + list(range(17, 1, -1))
                    for ch in order:
                        self.ssd_chunk(b, ch, dr_, S)

    def ssd_chunk(self, b, ch, dr_, S):
        c = self.c
        tok = slice(ch * 128, (ch + 1) * 128)
        cols = slice(dr_ * 16, dr_ * 16 + 16)
        Tm = S.mk[:, dr_, :]
        U = S.mk[:, 2 + dr_, :]
        xBt = S.xbp.get()
        c.dma("sp", xBt[:], self.xB[b, tok, :], xBt, writes=[xBt])
        BCt = S.bcp.get()
        c.dma("sp", BCt[:], self.BCT[b].rearrange("g p t -> p g t")[:, :, tok], BCt, writes=[BCt])
        dtt = S.dtp.get()
        c.dma("sp", dtt[:], self.dtr[b, tok, :], dtt, writes=[dtt])
        xs, ax, dt, dA, sm1, sm2, dtd = (S.smp.get() for _ in range(7))
        c.op("dve", lambda e: e.tensor_tensor(xs[:], dtt[:], S.dtb[:], op=ALU.add), [dtt, S.dtb], [xs])
        c.op("act", lambda e: e.activation(out=ax[:], in_=xs[:], func=AF.Abs), [xs], [ax])
        c.op("act", lambda e: e.activation(out=ax[:], in_=ax[:], func=AF.Exp, scale=-1.0), [ax], [ax])
        c.op("act", lambda e: e.activation(out=ax[:], in_=ax[:], func=AF.Ln, bias=S.onec[:, :]), [ax, S.onec], [ax])
        c.op("dve", lambda e: e.scalar_tensor_tensor(out=dt[:], in0=xs[:], scalar=0.0, in1=ax[:], op0=ALU.max, op1=ALU.add), [xs, ax], [dt])
        c.op("dve", lambda e: e.tensor_tensor(dA[:], dt[:], S.abc[:], op=ALU.mult), [dt, S.abc], [dA])
        pc = S.pX.get()
        c.op("pe", lambda e: e.matmul(pc[:, 0:16], Tm, dA[:, cols], start=True, stop=True), [S.mk, dA], [pc])
        c.op("pe", lambda e: e.matmul(pc[:, 16:32], self.ones_f[:, :], dA[:, cols], start=True, stop=True), [self.ones_f, dA], [pc])
        c.op("act", lambda e: e.activation(out=sm1[:], in_=pc[:, 0:32], func=AF.Copy), [pc], [sm1])
        c.op("act", lambda e: e.activation(out=sm2[:], in_=pc[:, 0:32], func=AF.Exp), [pc], [sm2])
        c.op("dve", lambda e: e.tensor_tensor(dtd[:, 0:16], sm1[:, 16:32], sm1[:, 0:16], op=ALU.subtract), [sm1], [dtd])
        c.op("act", lambda e: e.activation(out=dtd[:, 0:16], in_=dtd[:, 0:16], func=AF.Exp), [dtd], [dtd])
        c.op("dve", lambda e: e.tensor_tensor(dtd[:, 0:16], dtd[:, 0:16], dt[:, cols], op=ALU.mult), [dtd, dt], [dtd])
        xdt = S.xdtp.get()
        xdtd = S.xdtp.get()
        xv = xBt[:, 0:1024].rearrange("p (h d) -> p h d", d=64)
        c.op("pool", lambda e: e.tensor_tensor(xdt[:], xv, dt[:, cols].unsqueeze(2).to_broadcast([128, 16, 64]), op=ALU.mult),
             [xBt, dt], [xdt])
        c.op("pool", lambda e: e.tensor_tensor(xdtd[:], xv, dtd[:, 0:16].unsqueeze(2).to_broadcast([128, 16, 64]), op=ALU.mult),
             [xBt, dtd], [xdtd])
        pyo = [S.pZ.get(), S.pZ.get()]
        for g in range(4):
            c.op("pe", lambda e, g=g: e.matmul(pyo[g // 2][:, (g % 2) * 256:(g % 2 + 1) * 256], BCt[:, 4 + g, :],
                                               S.hbf[:, g * 256:(g + 1) * 256], start=True, stop=True), [BCt, S.hbf], [pyo[g // 2]])
        yo = S.yop.get()
        for hf in range(2):
            c.op("dve", lambda e, hf=hf: e.tensor_tensor(
                yo[:, hf * 512:(hf + 1) * 512].rearrange("p (h d) -> p h d", d=64),
                pyo[hf][:, :].rearrange("p (h d) -> p h d", d=64),
                sm2[:, hf * 8:(hf + 1) * 8].unsqueeze(2).to_broadcast([128, 8, 64]), op=ALU.mult), [pyo[hf], sm2], [yo])
        pyd = [S.pY.get(), S.pY.get()]
        for g in range(4):
            pcb = S.pX.get()
            c.op("pe", lambda e, pcb=pcb, g=g: e.matmul(pcb[:, 0:128], BCt[:, g, :], BCt[:, 4 + g, :], start=True, stop=True), [BCt], [pcb])
            cbm = S.cbp.get()
            c.op("dve", lambda e, cbm=cbm, pcb=pcb: e.tensor_tensor(cbm[:], pcb[:, 0:128], Tm, op=ALU.mult), [pcb, S.mk], [cbm])
            rg = S.rgp.get()
            c.op("pool", lambda e, rg=rg, g=g: e.tensor_tensor(
                rg[:], Tm.unsqueeze(1).to_broadcast([128, 4, 128]),
                dA[:, dr_ * 16 + 4 * g:dr_ * 16 + 4 * g + 4].unsqueeze(2).to_broadcast([128, 4, 128]), op=ALU.mult), [S.mk, dA], [rg])
            pseg = S.pX.get()
            c.op("pe", lambda e, pseg=pseg, rg=rg: e.matmul(pseg[:, :], U, rg[:].rearrange("p a b -> p (a b)"), start=True, stop=True),
                 [S.mk, rg], [pseg])
            E = S.ep.get()
            c.op("act", lambda e, E=E, pseg=pseg: e.activation(out=E[:], in_=pseg[:, :], func=AF.Exp), [pseg], [E])
            MT = S.mtp.get()
            c.op("dve", lambda e, MT=MT, E=E, cbm=cbm: e.tensor_tensor(
                MT[:], E[:].rearrange("p (a b) -> p a b", b=128), cbm[:].unsqueeze(1).to_broadcast([128, 4, 128]), op=ALU.mult),
                 [E, cbm], [MT])
            for hh in range(4):
                h = 4 * g + hh
                c.op("pe", lambda e, MT=MT, hh=hh, h=h: e.matmul(pyd[h // 8][:, (h % 8) * 64:(h % 8 + 1) * 64], MT[:, hh, :], xdt[:, h, :],
                                                             start=True, stop=True), [MT, xdt], [pyd[h // 8]])
        yd = S.ydp.get()
        for hf in range(2):
            c.op("dve", lambda e, hf=hf: e.tensor_tensor(yd[:, hf * 512:(hf + 1) * 512], pyd[hf][:, :], yo[:, hf * 512:(hf + 1) * 512], op=ALU.add),
                 [pyd[hf], yo], [yd])
        pst = [S.pZ.get(), S.pZ.get()]
        for g in range(4):
            c.op("pe", lambda e, g=g: e.matmul(pst[g // 2][:, (g % 2) * 256:(g % 2 + 1) * 256], xBt[:, 1024 + g * 128:1024 + (g + 1) * 128],
                                               xdtd[:, 4 * g:4 * g + 4, :].rearrange("p a b -> p (a b)"), start=True, stop=True),
                 [xBt, xdtd], [pst[g // 2]])
        c.op("dve", lambda e: e.tensor_tensor(S.hst[:].rearrange("p (h d) -> p h d", d=64), S.hst[:].rearrange("p (h d) -> p h d", d=64),
                                              sm2[:, 16:32].unsqueeze(2).to_broadcast([128, 16, 64]), op=ALU.mult), [S.hst, sm2], [S.hst])
        for hf in range(2):
            c.op("dve", lambda e, hf=hf: e.tensor_tensor(S.hst[:, hf * 512:(hf + 1) * 512], S.hst[:, hf * 512:(hf + 1) * 512], pst[hf][:, :],
                                                        op=ALU.add), [S.hst, pst[hf]], [S.hst])
        c.op("act", lambda e: e.activation(out=S.hbf[:], in_=S.hst[:], func=AF.Copy), [S.hst], [S.hbf])
        if dr_ == 0:
            c.dma("sp", self.yf[b, tok, :], yd[:], yd, reads=[yd], writes=[c.dr("yf", b, ch)])
            return
        if "ybd" in self.dump:
            c.dma("sp", self.ybd[b, tok, :], yd[:], yd, reads=[yd])
        yfl = S.ydp.get()
        c.dma("sp", yfl[:], self.yf[b, tok, :], yfl, reads=[c.dr("yf", b, ch)], writes=[yfl])
        zt = S.zp.get()
        c.dma("sp", zt[:], self.zs[b, tok, :], zt, writes=[zt])
        c.op("dve", lambda e: e.tensor_tensor(yd[:], yd[:], yfl[:], op=ALU.add), [yd, yfl], [yd])
        c.op("pool", lambda e: e.tensor_tensor(yfl[:].rearrange("p (h d) -> p h d", d=64), xv,
                                               S.dsk[:, :].unsqueeze(2).to_broadcast([128, 16, 64]), op=ALU.mult), [xBt, S.dsk], [yfl])
        c.op("dve", lambda e: e.tensor_tensor(yd[:], yd[:], yfl[:], op=ALU.add), [yd, yfl], [yd])
        c.op("dve", lambda e: e.tensor_tensor(yd[:], yd[:], zt[:], op=ALU.mult), [yd, zt], [yd])
        c.op("pool", lambda e: e.tensor_tensor(zt[:], yd[:], yd[:], op=ALU.mult), [yd], [zt])
        ss = S.smp.get()
        c.op("dve", lambda e: e.reduce_sum(ss[:, 0:1], zt[:], axis=mybir.AxisListType.X), [zt], [ss])
        c.op("act", lambda e: e.activation(out=ss[:, 0:1], in_=ss[:, 0:1], func=AF.Sqrt, scale=1.0 / 1024, bias=self.epsc[:, :]),
             [ss, self.epsc], [ss])
        c.op("dve", lambda e: e.reciprocal(ss[:, 0:1], ss[:, 0:1]), [ss], [ss])
        yn = S.ynp.get()
        c.op("dve", lambda e: e.scalar_tensor_tensor(out=yn[:], in0=yd[:], scalar=ss[:, 0:1], in1=S.ssdg[:], op0=ALU.mult, op1=ALU.mult),
             [yd, ss, S.ssdg], [yn], strict=True)
        yt = S.ytp.get()
        for q4 in range(2):
            ps = S.pX.get()
            for q in range(4):
                cc = q4 * 4 + q
                c.op("pe", lambda e, ps=ps, q=q, cc=cc: e.matmul(ps[:, q * 128:(q + 1) * 128], yn[:, cc * 128:(cc + 1) * 128], self.ident_bf[:, :],
                                                                start=True, stop=True), [yn, self.ident_bf], [ps])
            c.op("act", lambda e, ps=ps, q4=q4: e.activation(out=yt[:, q4 * 4:(q4 + 1) * 4, :], in_=ps[:, :].rearrange("p (a b) -> p a b", b=128),
                                                            func=AF.Copy), [ps], [yt])
        seg, i = ("c", 0) if ch < 2 else ("l", (ch - 2) // 4)
        dst = self.ycT[b][0:1024, :].rearrange("(k p) t -> p k t", p=128)
        c.dma("sp", dst[:, :, tok], yt[:], yt, reads=[yt], writes=[c.dr("ycT", b, "ssd", ch)])


def fm(v):
    v = np.asarray(v)
    n = v.shape[-1] // 128
    return np.ascontiguousarray(np.swapaxes(v.reshape(v.shape[:-1] + (n, 128)), -1, -2))


def rope_tables(rot_dim):
    n_freq = rot_dim // 4
    rows = T // 64
    row = np.repeat(np.arange(rows, dtype=np.float32), 64)
    col = np.tile(np.arange(64, dtype=np.float32), rows)
    inv = (np.float32(10000.0) ** (-np.arange(n_freq, dtype=np.float32) / np.float32(n_freq))).astype(np.float32)
    ang = np.concatenate([row[:, None] * inv, col[:, None] * inv], -1).astype(np.float32)
    return np.cos(ang).astype(np.float32), np.sin(ang).astype(np.float32)


def make_shared(inp):
    f32 = np.float32
    ca = lambda a: np.ascontiguousarray(a, dtype=f32)
    sh = dict(
        ada_w=ca(inp["ada_w"]),
        ada_b3=ca(np.repeat(inp["ada_b"][:, None, :], 3, axis=1)),
        g1T=fm(inp["norm1_g"]), g2T=fm(inp["norm2_g"]), gfT=fm(inp["final_norm_g"]),
        ffn_up=ca(inp["ffn_w_up"]), ffn_down=ca(inp["ffn_w_down"]),
        ident=np.eye(128, dtype=f32),
    )
    p16 = (np.arange(32) + 16) % 32
    w = inp["mla_w_in"][0]
    sh["mla_w_in"] = ca(np.concatenate([w[:, :640], np.zeros((D, 64), f32), w[:, 640:672], w[:, 640:672][:, p16]], 1))
    wq = inp["mla_w_uq"][0].reshape(384, 16, 96)
    sh["mla_w_uq"] = ca(np.concatenate([wq[:, :, :64], wq[:, :, 64:], wq[:, :, 64:][:, :, p16]], 2).reshape(384, 2048))
    wkv = inp["mla_w_ukv"][0].reshape(256, 16, 128)
    sh["mla_w_uk"] = ca(wkv[:, :, :64].reshape(256, 1024))
    sh["mla_w_uv"] = ca(wkv[:, :, 64:].reshape(256, 1024))
    sh["mla_w_o"] = ca(inp["mla_w_o"][0])
    sh["mla_gq"] = fm(inp["mla_q_norm_g"][0])
    sh["mla_gkv"] = fm(inp["mla_kv_norm_g"][0])
    c1, s1 = rope_tables(32)
    cs1 = np.zeros((128, T), f32)
    sn1 = np.zeros((128, T), f32)
    for d in range(32):
        m = d % 16
        cs1[64 + d] = c1[:, m]
        sn1[64 + d] = -s1[:, m] if d < 16 else s1[:, m]
    sh["cs1"], sh["sn1"] = cs1, sn1
    w = inp["ab_w_in"][0]
    sh["w_ssd"] = ca(np.concatenate([w[:, 1024:3072], w[:, 0:1024], w[:, 3072:3104]], 1))
    p32 = (np.arange(64) + 32) % 64
    q = w[:, 3104:4128].reshape(D, 16, 64)
    k = w[:, 4128:4384].reshape(D, 4, 64)
    sh["w_att"] = ca(np.concatenate([q.reshape(D, 1024), q[:, :, p32].reshape(D, 1024), k.reshape(D, 256),
                                     k[:, :, p32].reshape(D, 256), w[:, 4384:4640]], 1))
    sh["ab_w_out"] = ca(inp["ab_w_out"][0])
    cw = inp["ssd_conv_w"][0]
    sh["convw"] = ca(cw.T.reshape(16, 128, 3).transpose(1, 0, 2))
    sh["convb"] = fm(inp["ssd_conv_b"][0])
    sh["dtb_bc"] = ca(np.tile(inp["ssd_dt_bias"][0].reshape(1, 32), (128, 1)))
    sh["alog_bc"] = ca(np.tile(inp["ssd_a_log"][0].reshape(1, 32), (128, 1)))
    sh["dsk_bc"] = ca(np.tile(inp["ssd_d"][0].reshape(1, 16), (128, 1)))
    sh["ssdg_bc"] = ca(np.tile(inp["ssd_norm_g"][0].reshape(1, 1024), (128, 1)))
    one = np.ones((128, 128), f32)
    sh["masks"] = ca(np.stack([np.triu(one), np.tril(one), np.tril(one, -1), np.triu(one, 1)]))
    blk = np.zeros((128, 128), f32)
    blk[:64, :64] = 1
    blk[64:, 64:] = 1
    sh["blk64"] = blk
    gq = inp["att_q_g"][0]
    gk = inp["att_k_g"][0]
    d64 = np.arange(128) % 64
    sh["gqk"] = ca(np.stack([gq[d64], gq[p32[d64]], gk[d64], gk[p32[d64]]], 1))
    c0, s0 = rope_tables(64)
    cs0 = np.zeros((128, T), f32)
    sn0 = np.zeros((128, T), f32)
    for r in range(128):
        d = r % 64
        m = d % 32
        cs0[r] = c0[:, m]
        sn0[r] = -s0[:, m] if d < 32 else s0[:, m]
    sh["cs0"], sh["sn0"] = cs0, sn0
    return sh


def make_in_maps(inp, n_cores=8):
    maps = []
    shared = make_shared(inp)
    for ci in range(n_cores):
        bs = slice(ci * NB, (ci + 1) * NB)
        m = dict(shared)
        m["xT"] = np.ascontiguousarray(np.swapaxes(inp["x"][bs], 1, 2))
        m["cxT"] = np.ascontiguousarray(np.swapaxes(inp["ctx"][bs], 1, 2))
        cv = np.stack([inp["c"][ci * NB], inp["c"][ci * NB + 1], inp["c_ctx"]], axis=-1)
        m["cT"] = np.ascontiguousarray(cv.reshape(8, 128, 3).transpose(1, 0, 2))
        maps.append(m)
    return maps


_CACHE = {}


def get_prog(cfg_key=()):
    if cfg_key not in _CACHE:
        p = Prog(dict(cfg_key))
        p.build()
        _CACHE[cfg_key] = p
    return _CACHE[cfg_key]


def kernel(**inputs):
    inp = {k: np.asarray(v) for k, v in inputs.items()}
    p = get_prog()
    maps = make_in_maps(inp)
    res = run_bass_kernel_spmd(p.c.nc, maps, core_ids=list(range(8)), trace=True)
    out = np.empty((16, T, D), np.float32)
    for ci in range(8):
        o = res.results[ci]["outT"]
        out[ci * NB:(ci + 1) * NB] = np.swapaxes(o, 1, 2)
    return out
```

```python
import contextlib
import math
import numpy as np
import concourse.bass as bass
import concourse.mybir as mybir
from concourse.bass_utils import run_bass_kernel_spmd

F32 = mybir.dt.float32
BF16 = mybir.dt.bfloat16
AF = mybir.ActivationFunctionType
ALU = mybir.AluOpType

ENGS = ("pe", "act", "dve", "pool", "sp")
EPS = 1e-6
D = 1024
T = 2048
CT = 256
TT = T + CT
NB = 2
FH = 2816


class Res:
    __slots__ = ("name", "t", "last_w", "readers", "dsem", "dcount")

    def __init__(self, name, t=None):
        self.name = name
        self.t = t
        self.last_w = None
        self.readers = []
        self.dsem = None
        self.dcount = 0

    def __getitem__(self, k):
        return self.t[k]


class PhysSem:
    __slots__ = ("count", "handle", "idx")

    def __init__(self, idx):
        self.count = 0
        self.handle = None
        self.idx = idx


class Op:
    __slots__ = ("eng", "fn", "deps", "signal", "sig_idx", "is_dma", "dres", "dval", "dsem", "strict")

    def __init__(self, eng, fn):
        self.eng = eng
        self.fn = fn
        self.deps = []
        self.signal = False
        self.sig_idx = None
        self.is_dma = False
        self.dres = None
        self.dval = 0
        self.strict = False


class Ctx:
    def __init__(self, same_engine_sync=False):
        self.nc = bass.Bass("TRN2", target_bir_lowering=False)
        self.es = contextlib.ExitStack()
        self.ops = []
        self.same_engine_sync = same_engine_sync
        self.n_sems = 0
        self._uid = 0
        self.dres = {}
        self._bar_pos = 0
        self._sem_free = []
        self._sem_all = []
        self._scope_res = [[]]

    def uid(self, p):
        self._uid += 1
        return f"{p}{self._uid}"

    def sb(self, shape, dtype, name=None):
        name = "s_" + (name or self.uid("sb"))
        t = self.es.enter_context(self.nc.sbuf_tensor(name, list(shape), dtype))
        r = Res(name, t)
        self._scope_res[-1].append(r)
        return r

    def ps(self, shape, dtype=F32, name=None):
        name = "p_" + (name or self.uid("ps"))
        t = self.es.enter_context(self.nc.psum_tensor(name, list(shape), dtype))
        return Res(name, t)

    def dram(self, name, shape, dtype, kind="Internal"):
        return self.nc.dram_tensor(name, list(shape), dtype, kind=kind).ap()

    def dr(self, *key):
        r = self.dres.get(key)
        if r is None:
            r = Res("dr_" + "_".join(str(k) for k in key))
            self.dres[key] = r
        return r

    def sem(self, name):
        self.n_sems += 1
        return self.es.enter_context(self.nc.semaphore(name))

    def _track(self, op, reads, writes):
        deps = op.deps
        for r in reads:
            if r.last_w is not None:
                deps.append(r.last_w)
        for w in writes:
            if w.last_w is not None:
                deps.append(w.last_w)
            deps.extend(w.readers)
        for r in reads:
            r.readers.append(op)
        for w in writes:
            w.last_w = op
            w.readers = []
        self.ops.append(op)

    def op(self, eng, fn, reads=(), writes=(), strict=False):
        o = Op(eng, fn)
        o.strict = strict
        self._track(o, reads, writes)
        return o

    def dma(self, eng, out_ap, in_ap, sbres, reads=(), writes=(), **kw):
        o = Op(eng, lambda e: e.dma_start(out=out_ap, in_=in_ap, **kw))
        o.is_dma = True
        o.dres = sbres
        self._track(o, reads, writes)
        if sbres.dsem is None:
            if self._sem_free:
                sbres.dsem = self._sem_free.pop()
            else:
                sbres.dsem = PhysSem(len(self._sem_all))
                self._sem_all.append(sbres.dsem)
        sbres.dsem.count += 1
        o.dval = 16 * sbres.dsem.count
        o.dsem = sbres.dsem
        return o

    @contextlib.contextmanager
    def scope(self):
        outer = self.es
        inner = contextlib.ExitStack()
        self.es = inner
        self._scope_res.append([])
        try:
            with inner:
                yield
                self.barrier()
                for r in self._scope_res[-1]:
                    if r.dsem is not None:
                        self._sem_free.append(r.dsem)
                        r.dsem = None
        finally:
            self._scope_res.pop()
            self.es = outer

    def barrier(self):
        last = {}
        dmas = []
        for o in self.ops[self._bar_pos:]:
            if o.is_dma:
                dmas.append(o)
            else:
                last[o.eng] = o
        deps = list(last.values()) + dmas
        for e in ENGS:
            o = Op(e, lambda en: en.nop())
            o.deps.extend(deps)
            self.ops.append(o)
        self._bar_pos = len(self.ops)

    def finish(self, final_dmas):
        o = Op("sp", lambda e: e.nop())
        o.deps.extend(final_dmas)
        self.ops.append(o)

    def emit(self):
        nc = self.nc
        engobj = {"pe": nc.tensor, "act": nc.scalar, "dve": nc.vector, "pool": nc.gpsimd, "sp": nc.sync}
        ses = self.same_engine_sync
        for o in self.ops:
            for d in o.deps:
                if not d.is_dma:
                    if d.eng == o.eng and not o.is_dma and not ses and not o.strict:
                        continue
                    d.signal = True
        sidx = {e: 0 for e in ENGS}
        for o in self.ops:
            if not o.is_dma and o.signal:
                sidx[o.eng] += 1
                o.sig_idx = sidx[o.eng]
        esem = {e: self.sem("e_" + e) for e in ENGS}
        for ps_ in self._sem_all:
            ps_.handle = self.sem(f"dq{ps_.idx}")
        waited = {e: {} for e in ENGS}
        n_wait = 0
        for o in self.ops:
            eng = engobj[o.eng]
            w = {}
            for d in o.deps:
                if d.is_dma:
                    key = ("d", d.dsem.idx)
                    val = d.dval
                    sem = d.dsem
                else:
                    if d.eng == o.eng and not o.is_dma and not ses and not o.strict:
                        continue
                    key = ("e", d.eng)
                    val = d.sig_idx
                    sem = esem[d.eng]
                if key not in w or w[key][1] < val:
                    w[key] = (sem, val)
            wd = waited[o.eng]
            for key, (sem, val) in w.items():
                if wd.get(key, 0) >= val:
                    continue
                wd[key] = val
                eng.wait_ge(sem.handle if isinstance(sem, PhysSem) else sem, val)
                n_wait += 1
            if o.is_dma:
                o.fn(eng).then_inc(o.dsem.handle, 16)
            else:
                ins = o.fn(eng)
                if o.signal:
                    ins.then_inc(esem[o.eng], 1)
        self.stats = dict(n_ops=len(self.ops), n_wait=n_wait, n_sems=self.n_sems, sig=dict(sidx))
        return nc


class Pool:
    def __init__(self, ctx, shape, dtype, n, name, psum=False):
        mk = ctx.ps if psum else ctx.sb
        self.bufs = [mk(shape, dtype, name=f"{name}{i}") for i in range(n)]
        self.i = 0

    def get(self):
        b = self.bufs[self.i % len(self.bufs)]
        self.i += 1
        return b


def token_blocks():
    blks = [("c", 0, 0, CT)]
    for i in range(T // 512):
        blks.append(("l", i, CT + i * 512, 512))
    return blks


class Prog:
    def __init__(self, cfg):
        self.cfg = cfg
        self.c = Ctx(same_engine_sync=cfg.get("ses", False))
        self.dump = cfg.get("dump", ())
        self.stq = cfg.get("stq", "pool")
        self.outs = []

    def scratch(self, name, shape, dtype):
        kind = "ExternalOutput" if name in self.dump else "Internal"
        return self.c.dram(name, shape, dtype, kind)

    def build(self):
        c = self.c
        cfg = self.cfg
        nc = c.nc
        I = lambda n, s, d=F32: c.dram(n, s, d, "ExternalInput")
        self.I = I
        self.xT = I("xT", [NB, D, T])
        self.cxT = I("cxT", [NB, D, CT])
        self.cT = I("cT", [128, 8, 3])
        self.ada_w = I("ada_w", [2, D, 6 * D])
        self.ada_b3 = I("ada_b3", [2, 3, 6 * D])
        self.g1T = I("g1T", [2, 128, 8])
        self.g2T = I("g2T", [2, 128, 8])
        self.gfT = I("gfT", [128, 8])
        self.ffn_up = I("ffn_up", [2, D, 2 * FH])
        self.ffn_down = I("ffn_down", [2, FH, D])
        self.ident_in = I("ident", [128, 128])
        self.declare_mixer_inputs()
        self.outT = c.dram("outT", [NB, D, T], F32, "ExternalOutput")
        self.xres = self.scratch("xres", [NB, D, TT], F32)
        self.hT = self.scratch("hT", [NB, D, TT], BF16)
        self.ycT = self.scratch("ycT", [NB, 2 * D, TT], BF16)
        with c.es:
            self.alloc_common()
            with c.scope():
                self.phase_mods()
            self.copy_in_residual()
            for l in range(2):
                last = l == 1
                if not (l == 0 and cfg.get("skip_ab")) and not (l == 1 and cfg.get("skip_mla")):
                    self.phase_norm(l, which=1, ctx=True)
                    if l == 0:
                        self.layer0_mixer()
                        self.phase_outproj(l, self.ab_w_out, 16, ctx=True)
                    else:
                        self.layer1_mixer()
                        self.phase_outproj(l, self.mla_w_o, 8, ctx=False)
                if cfg.get("stop") == f"mix{l}":
                    break
                self.phase_norm(l, which=2, ctx=not last)
                self.phase_ffn(l, ctx=not last)
                if cfg.get("stop") == f"ffn{l}":
                    break
            self.phase_final()
            c.finish(self.outs)
            c.emit()
        return nc

    def alloc_common(self):
        c = self.c
        self.psA = Pool(c, [128, 512], F32, 4, "psA", psum=True)
        self.psB = Pool(c, [128, 512], F32, 2, "psB", psum=True)
        self.psC = Pool(c, [128, 512], F32, 2, "psC", psum=True)
        self.psum = self.psA
        self.ident = c.sb([128, 128], F32, "ident_sb")
        c.dma("sp", self.ident[:], self.ident_in[:, :], self.ident, writes=[self.ident])
        self.ident_bf = c.sb([128, 128], BF16, "ident_bf")
        c.op("dve", lambda e: e.tensor_copy(self.ident_bf[:], self.ident[:]), [self.ident], [self.ident_bf])
        self.epsc = c.sb([128, 1], F32, "epsc")
        c.op("dve", lambda e: e.memset(self.epsc[:], EPS), [], [self.epsc])
        self.ones_bf = c.sb([128, 128], BF16, "ones_bf")
        c.op("dve", lambda e: e.memset(self.ones_bf[:], 1.0), [], [self.ones_bf])
        self.ones_f = c.sb([128, 128], F32, "ones_f")
        c.op("dve", lambda e: e.memset(self.ones_f[:], 1.0), [], [self.ones_f])
        self.tab = {}
        for l in range(2):
            for nm in ("gs1", "sh1", "gate1", "gs2", "sh2", "gate2"):
                self.tab[(l, nm)] = c.sb([128, 8, 3], F32, f"tab_{nm}{l}")
        self.gf = c.sb([128, 8], F32, "gf")
        c.dma("sp", self.gf[:], self.gfT[:, :], self.gf, writes=[self.gf])

    def alloc_tok_pools(self):
        c = self.c
        u = c.uid("tp")
        self.xpool = Pool(c, [128, 8, 512], F32, 2, u + "xt")
        self.hpool = Pool(c, [128, 8, 512], BF16, 2, u + "ht")
        self.sqpool = Pool(c, [128, 8, 512], BF16, 1, u + "sq")
        self.xnpool = Pool(c, [128, 8, 512], F32, 1, u + "xn")
        self.rspool = Pool(c, [128, 512], F32, 2, u + "rs")

    def phase_mods(self):
        c = self.c
        cT = c.sb([128, 8, 3], F32, "cT")
        sc = c.sb([128, 8, 3], F32, "sc")
        c.dma("sp", cT[:], self.cT[:, :, :], cT, writes=[cT])
        c.op("act", lambda e: e.activation(out=sc[:], in_=cT[:], func=AF.Silu), [cT], [sc])
        wpool = Pool(c, [128, 8, 512], F32, 2, "adaw")
        adab = c.sb([3, 6 * D], F32, "adab")
        modtok = c.sb([3, 6 * D], F32, "modtok")
        for l in range(2):
            g1 = c.sb([128, 8], F32, f"g1_{l}")
            g2 = c.sb([128, 8], F32, f"g2_{l}")
            c.dma("sp", adab[:], self.ada_b3[l], adab, writes=[adab])
            c.dma("sp", g1[:], self.g1T[l], g1, writes=[g1])
            c.dma("sp", g2[:], self.g2T[l], g2, writes=[g2])
            wv = self.ada_w[l].rearrange("(k p) n -> p k n", p=128)
            for nb in range(12):
                wt = wpool.get()
                c.dma("sp", wt[:], wv[:, :, nb * 512:(nb + 1) * 512], wt, writes=[wt])
                ps = self.psum.get()
                for k in range(8):
                    c.op("pe", lambda e, ps=ps, wt=wt, k=k: e.matmul(ps[0:3, :], sc[:, k, :], wt[:, k, :],
                                                                  start=(k == 0), stop=(k == 7)),
                         [sc, wt], [ps])
                c.op("dve", lambda e, ps=ps, nb=nb: e.tensor_tensor(modtok[:, nb * 512:(nb + 1) * 512], ps[0:3, :],
                                                                  adab[:, nb * 512:(nb + 1) * 512], op=ALU.add),
                     [ps, adab], [modtok])
            ps = self.psum.get()
            for m in range(48):
                c.op("pe", lambda e, ps=ps, m=m: e.matmul(ps[:, m * 3:m * 3 + 3], modtok[0:3, m * 128:(m + 1) * 128],
                                                         self.ident[0:3, 0:3], start=True, stop=True),
                     [modtok, self.ident], [ps])
            modT = c.sb([128, 48, 3], F32, f"modT{l}")
            c.op("dve", lambda e, ps=ps, modT=modT: e.tensor_copy(modT[:].rearrange("p a b -> p (a b)"), ps[:, 0:144]),
                 [ps], [modT])
            tb = lambda nm: self.tab[(l, nm)]
            for nm, lo in (("sh1", 0), ("gate1", 16), ("sh2", 24), ("gate2", 40)):
                c.op("dve", lambda e, nm=nm, lo=lo, modT=modT, l=l: e.tensor_copy(self.tab[(l, nm)][:], modT[:, lo:lo + 8, :]),
                     [modT], [tb(nm)])
            for nm, lo, g in (("gs1", 8, g1), ("gs2", 32, g2)):
                c.op("dve", lambda e, nm=nm, lo=lo, g=g, modT=modT, l=l: e.scalar_tensor_tensor(
                    out=self.tab[(l, nm)][:], in0=modT[:, lo:lo + 8, :], scalar=1.0,
                    in1=g[:].unsqueeze(2).to_broadcast([128, 8, 3]), op0=ALU.add, op1=ALU.mult),
                     [modT, g], [tb(nm)])

    def copy_in_residual(self):
        c = self.c
        with c.scope():
            pool = Pool(c, [128, 8, 512], F32, 3, "cpin")
            for b in range(NB):
                for (seg, i, t0, n) in token_blocks():
                    xt = pool.get()
                    src = (self.cxT[b] if seg == "c" else self.xT[b]).rearrange("(k p) t -> p k t", p=128)
                    s0 = 0 if seg == "c" else i * 512
                    c.dma("sp", xt[:, :, :n], src[:, :, s0:s0 + n], xt, writes=[xt])
                    dst = self.xres[b].rearrange("(k p) t -> p k t", p=128)
                    c.dma(self.stq, dst[:, :, t0:t0 + n], xt[:, :, :n], xt, reads=[xt], writes=[c.dr("xres", b, seg, i)])

    def rstd(self, rs, ps, n, inv_n, rows=128, r0=0):
        c = self.c
        c.op("act", lambda e: e.activation(out=rs[r0:r0 + rows, :n], in_=ps[r0:r0 + rows, :n], func=AF.Sqrt, scale=inv_n,
                                           bias=self.epsc[r0:r0 + rows, :]),
             [ps, self.epsc], [rs])
        c.op("dve", lambda e: e.reciprocal(rs[r0:r0 + rows, :n], rs[r0:r0 + rows, :n]), [rs], [rs])

    def norm_tile(self, xt, n, gs, sh, out_fn):
        c = self.c
        sq = self.sqpool.get()
        c.op("act", lambda e: e.activation(out=sq[:, :, :n], in_=xt[:, :, :n], func=AF.Square), [xt], [sq])
        ps = self.psum.get()
        for k in range(8):
            c.op("pe", lambda e, k=k: e.matmul(ps[:, :n], self.ones_bf[:, :], sq[:, k, :n], start=(k == 0), stop=(k == 7)),
                 [sq, self.ones_bf], [ps])
        rs = self.rspool.get()
        self.rstd(rs, ps, n, 1.0 / D)
        xn = self.xnpool.get()
        c.op("dve", lambda e: e.tensor_tensor(xn[:, :, :n], xt[:, :, :n],
                                              rs[:, :n].unsqueeze(1).to_broadcast([128, 8, n]), op=ALU.mult),
             [xt, rs], [xn])
        for k in range(8):
            oap, ores = out_fn(k)
            if sh is not None:
                c.op("act", lambda e, k=k, oap=oap: e.activation(out=oap, in_=xn[:, k, :n], func=AF.Identity,
                                                                 scale=gs[:, k:k + 1], bias=sh[:, k:k + 1]),
                     [xn], [ores])
            else:
                c.op("act", lambda e, k=k, oap=oap: e.activation(out=oap, in_=xn[:, k, :n], func=AF.Identity,
                                                                 scale=gs[:, k:k + 1]),
                     [xn], [ores])

    def phase_norm(self, l, which, ctx):
        c = self.c
        gs_t = self.tab[(l, f"gs{which}")]
        sh_t = self.tab[(l, f"sh{which}")]
        with c.scope():
            self.alloc_tok_pools()
            for b in range(NB):
                for (seg, i, t0, n) in token_blocks():
                    if seg == "c" and not ctx:
                        continue
                    j = 2 if seg == "c" else b
                    xt = self.xpool.get()
                    src = self.xres[b].rearrange("(k p) t -> p k t", p=128)
                    c.dma("sp", xt[:, :, :n], src[:, :, t0:t0 + n], xt, reads=[c.dr("xres", b, seg, i)], writes=[xt])
                    ht = self.hpool.get()
                    self.norm_tile(xt, n, gs_t[:, :, j], sh_t[:, :, j], lambda k, ht=ht, n=n: (ht[:, k, :n], ht))
                    dst = self.hT[b].rearrange("(k p) t -> p k t", p=128)
                    c.dma(self.stq, dst[:, :, t0:t0 + n], ht[:, :, :n], ht, reads=[ht], writes=[c.dr("hT", b, seg, i)])

    def phase_ffn(self, l, ctx):
        c = self.c
        HH = FH // 2
        NM = HH // 128
        with c.scope():
            w_up = c.sb([128, 8, 2 * HH], BF16, f"w_up{l}")
            w_dn = c.sb([128, NM, D], BF16, f"w_dn{l}")
            actpool = Pool(c, [128, NM, 512], BF16, 2, f"act{l}_")
            sgpool = Pool(c, [128, 512], F32, 2, f"sg{l}_")
            xpool = Pool(c, [128, 8, 512], F32, 2, f"fx{l}_")
            hpool = Pool(c, [128, 8, 512], BF16, 2, f"fh{l}_")
            gate = self.tab[(l, "gate2")]
            upv = self.ffn_up[l].rearrange("(k p) n -> p k n", p=128)
            dnv = self.ffn_down[l].rearrange("(m p) n -> p m n", p=128)
            for half in range(2):
                for k in range(8):
                    c.dma("pool", w_up[:, k, 0:HH], upv[:, k, half * HH:(half + 1) * HH], w_up, writes=[w_up])
                    c.dma("pool", w_up[:, k, HH:2 * HH], upv[:, k, FH + half * HH:FH + (half + 1) * HH], w_up, writes=[w_up])
                for m in range(NM):
                    c.dma("pool", w_dn[:, m, :], dnv[:, half * NM + m, :], w_dn, writes=[w_dn])
                for b in range(NB):
                    for (seg, i, t0, n) in token_blocks():
                        if seg == "c" and not ctx:
                            continue
                        j = 2 if seg == "c" else b
                        ht = hpool.get()
                        src = self.hT[b].rearrange("(k p) t -> p k t", p=128)
                        c.dma("sp", ht[:, :, :n], src[:, :, t0:t0 + n], ht, reads=[c.dr("hT", b, seg, i)], writes=[ht])
                        xt = xpool.get()
                        xsrc = self.xres[b].rearrange("(k p) t -> p k t", p=128)
                        c.dma("sp", xt[:, :, :n], xsrc[:, :, t0:t0 + n], xt, reads=[c.dr("xres", b, seg, i)], writes=[xt])
                        act = actpool.get()
                        for m in range(NM):
                            pg = self.psA.get()
                            pu = self.psA.get()
                            for k in range(8):
                                c.op("pe", lambda e, pg=pg, k=k, m=m, ht=ht, n=n: e.matmul(
                                    pg[:, :n], w_up[:, k, m * 128:(m + 1) * 128], ht[:, k, :n], start=(k == 0), stop=(k == 7)),
                                     [w_up, ht], [pg])
                            for k in range(8):
                                c.op("pe", lambda e, pu=pu, k=k, m=m, ht=ht, n=n: e.matmul(
                                    pu[:, :n], w_up[:, k, HH + m * 128:HH + (m + 1) * 128], ht[:, k, :n], start=(k == 0), stop=(k == 7)),
                                     [w_up, ht], [pu])
                            sg = sgpool.get()
                            c.op("act", lambda e, sg=sg, pg=pg, n=n: e.activation(out=sg[:, :n], in_=pg[:, :n], func=AF.Silu), [pg], [sg])
                            c.op("dve", lambda e, sg=sg, pu=pu, act=act, m=m, n=n: e.tensor_tensor(
                                act[:, m, :n], sg[:, :n], pu[:, :n], op=ALU.mult), [sg, pu], [act])
                        for k in range(8):
                            po = self.psB.get()
                            for m in range(NM):
                                c.op("pe", lambda e, po=po, k=k, m=m, act=act, n=n: e.matmul(
                                    po[:, :n], w_dn[:, m, k * 128:(k + 1) * 128], act[:, m, :n], start=(m == 0), stop=(m == NM - 1)),
                                     [w_dn, act], [po])
                            c.op("dve", lambda e, po=po, k=k, xt=xt, n=n, j=j: e.scalar_tensor_tensor(
                                out=xt[:, k, :n], in0=po[:, :n], scalar=gate[:, k, j:j + 1], in1=xt[:, k, :n],
                                op0=ALU.mult, op1=ALU.add), [po, xt, gate], [xt])
                        c.dma(self.stq, xsrc[:, :, t0:t0 + n], xt[:, :, :n], xt, reads=[xt], writes=[c.dr("xres", b, seg, i)])

    def phase_outproj(self, l, w_dram, nch, ctx):
        c = self.c
        with c.scope():
            w = c.sb([128, nch, D], BF16, f"wo{l}")
            wv = w_dram.rearrange("(m p) n -> p m n", p=128)
            for m in range(nch):
                c.dma("pool", w[:, m, :], wv[:, m, :], w, writes=[w])
            ypool = Pool(c, [128, nch, 512], BF16, 2, f"oy{l}_")
            xpool = Pool(c, [128, 8, 512], F32, 2, f"ox{l}_")
            gate = self.tab[(l, "gate1")]
            for b in range(NB):
                for (seg, i, t0, n) in token_blocks():
                    if seg == "c" and not ctx:
                        continue
                    j = 2 if seg == "c" else b
                    yt = ypool.get()
                    src = self.ycT[b].rearrange("(k p) t -> p k t", p=128)
                    c.dma("sp", yt[:, :, :n], src[:, 0:nch, t0:t0 + n], yt,
                          reads=[c.dr("ycT", b, seg, i, 0), c.dr("ycT", b, seg, i, 1)], writes=[yt])
                    xt = xpool.get()
                    xsrc = self.xres[b].rearrange("(k p) t -> p k t", p=128)
                    c.dma("sp", xt[:, :, :n], xsrc[:, :, t0:t0 + n], xt, reads=[c.dr("xres", b, seg, i)], writes=[xt])
                    for k in range(8):
                        po = self.psA.get()
                        for m in range(nch):
                            c.op("pe", lambda e, po=po, k=k, m=m, yt=yt, n=n: e.matmul(
                                po[:, :n], w[:, m, k * 128:(k + 1) * 128], yt[:, m, :n], start=(m == 0), stop=(m == nch - 1)),
                                 [w, yt], [po])
                        c.op("dve", lambda e, po=po, k=k, xt=xt, n=n, j=j: e.scalar_tensor_tensor(
                            out=xt[:, k, :n], in0=po[:, :n], scalar=gate[:, k, j:j + 1], in1=xt[:, k, :n],
                            op0=ALU.mult, op1=ALU.add), [po, xt, gate], [xt])
                    c.dma(self.stq, xsrc[:, :, t0:t0 + n], xt[:, :, :n], xt, reads=[xt], writes=[c.dr("xres", b, seg, i)])

    def phase_final(self):
        c = self.c
        with c.scope():
            self.alloc_tok_pools()
            for b in range(NB):
                for (seg, i, t0, n) in token_blocks():
                    if seg == "c":
                        continue
                    xt = self.xpool.get()
                    src = self.xres[b].rearrange("(k p) t -> p k t", p=128)
                    c.dma("sp", xt[:, :, :n], src[:, :, t0:t0 + n], xt, reads=[c.dr("xres", b, seg, i)], writes=[xt])
                    ot = self.xpool.get()
                    self.norm_tile(xt, n, self.gf[:, :], None, lambda k, ot=ot, n=n: (ot[:, k, :n], ot))
                    dst = self.outT[b].rearrange("(k p) t -> p k t", p=128)
                    d = c.dma(self.stq, dst[:, :, i * 512:i * 512 + n], ot[:, :, :n], ot, reads=[ot])
                    self.outs.append(d)
    def declare_mixer_inputs(self):
        I = self.I
        self.mla_w_in = I("mla_w_in", [D, 768])
        self.mla_w_uq = I("mla_w_uq", [384, 16 * 128])
        self.mla_w_uk = I("mla_w_uk", [256, 1024])
        self.mla_w_uv = I("mla_w_uv", [256, 1024])
        self.mla_w_o = I("mla_w_o", [1024, 1024])
        self.mla_gq = I("mla_gq", [128, 3])
        self.mla_gkv = I("mla_gkv", [128, 2])
        self.cs1 = I("cs1", [128, T])
        self.sn1 = I("sn1", [128, T])
        self.Q1T = self.scratch("Q1T", [NB, 16, 96, TT], BF16)
        self.K1T = self.scratch("K1T", [NB, 16, 96, TT], BF16)
        self.V1 = self.scratch("V1", [NB, TT, 16, 128], BF16)
        self.w_ssd = I("w_ssd", [D, 3104])
        self.w_att = I("w_att", [D, 2816])
        self.ab_w_out = I("ab_w_out", [2 * D, D])
        self.convw = I("convw", [128, 16, 3])
        self.convb = I("convb", [128, 16])
        self.dtb_bc = I("dtb_bc", [128, 32])
        self.alog_bc = I("alog_bc", [128, 32])
        self.dsk_bc = I("dsk_bc", [128, 16])
        self.ssdg_bc = I("ssdg_bc", [128, 1024])
        self.masks = I("masks", [4, 128, 128])
        self.blk64 = I("blk64", [128, 128])
        self.gqk = I("gqk", [128, 4])
        self.cs0 = I("cs0", [128, T])
        self.sn0 = I("sn0", [128, T])
        self.Q0T = self.scratch("Q0T", [NB, 16, 64, TT], BF16)
        self.K0T = self.scratch("K0T", [NB, 4, 64, TT], BF16)
        self.V0 = self.scratch("V0", [NB, TT, 4, 128], BF16)
        self.xB = self.scratch("xB", [NB, TT, 1536], BF16)
        self.BCT = self.scratch("BCT", [NB, 8, 128, TT], BF16)
        self.zs = self.scratch("zs", [NB, TT, 1024], F32)
        self.dtr = self.scratch("dtr", [NB, TT, 32], F32)
        self.yf = self.scratch("yf", [NB, TT, 1024], F32)
        if "ybd" in self.dump:
            self.ybd = self.scratch("ybd", [NB, TT, 1024], F32)

    def attention(self, tag, b, H, n_kv, dk, scale, KT_d, QT_d, V_d, row0, qsets):
        c = self.c
        with c.scope():
            KT = c.sb([dk, n_kv, TT], BF16, f"KT{tag}")
            VA = c.sb([128, 18, n_kv, 128], BF16, f"VA{tag}")
            for g in range(n_kv):
                c.dma("sp", KT[:, g, :], KT_d[b, g], KT, writes=[KT])
            vv = V_d[b].rearrange("(tt p) g e -> p tt g e", p=128)
            for tt in range(18):
                c.dma("sp", VA[:, tt, :, :], vv[:, tt, :, :], VA, writes=[VA])
            qpool = Pool(c, [dk, 512], BF16, 3, f"aq{tag}")
            epool = Pool(c, [128, 512], BF16, 6, f"ae{tag}")
            opool = Pool(c, [128, 8, 512], BF16, 2, f"ao{tag}")
            rcp = Pool(c, [128, 512], F32, 2, f"ar{tag}")
            bcp = Pool(c, [64, 512], F32, 2, f"ab{tag}")
            gsz = H // n_kv
            LA = 3
            for (seg, i, t0, n, nkt) in qsets:
                ot = opool.get()
                items = [(h, kt) for h in range(H) for kt in range(nkt)]
                qts, pos, ets = {}, {}, {}

                def emit_s(h, kt):
                    g = h // gsz
                    if kt == 0:
                        qt = qpool.get()
                        c.dma("sp", qt[:, :n], QT_d[b, h, :, t0:t0 + n], qt, writes=[qt])
                        qts[h] = qt
                    qt = qts[h]
                    pss = self.psA.get()
                    for _rep in range(self.cfg.get("s_rep", 1)):
                        c.op("pe", lambda e, pss=pss, g=g, kt=kt, qt=qt, n=n: e.matmul(
                            pss[:, :n], KT[:, g, kt * 128:(kt + 1) * 128], qt[:, :n], start=True, stop=True), [KT, qt], [pss])
                    et = epool.get()
                    c.op("act", lambda e, et=et, pss=pss, n=n: e.activation(out=et[:, :n], in_=pss[:, :n], func=AF.Exp, scale=scale),
                         [pss], [et])
                    ets[(h, kt)] = et

                def emit_pv(h, kt):
                    g = h // gsz
                    if kt == 0:
                        pos[h] = self.psB.get()
                    po = pos[h]
                    et = ets.pop((h, kt))
                    c.op("pe", lambda e, po=po, g=g, kt=kt, et=et, n=n, nkt=nkt: e.matmul(
                        po[:, :n], VA[:, kt, g, :], et[:, :n], start=(kt == 0), stop=(kt == nkt - 1)), [VA, et], [po])
                    if kt == nkt - 1:
                        bs = bcp.get()
                        c.op("act", lambda e, bs=bs, po=po, n=n: e.activation(out=bs[0:64, :n], in_=po[64:128, :n], func=AF.Copy), [po], [bs])
                        c.op("dve", lambda e, bs=bs, n=n: e.reciprocal(bs[0:64, :n], bs[0:64, :n]), [bs], [bs])
                        r = (h % 2) * 64
                        c.op("dve", lambda e, r=r, h=h, po=po, bs=bs, n=n, ot=ot: e.tensor_tensor(
                            ot[r:r + 64, h // 2, :n], po[0:64, :n], bs[0:64, :n], op=ALU.mult), [po, bs], [ot])

                for j in range(len(items) + LA):
                    if j < len(items):
                        emit_s(*items[j])
                    if j >= LA:
                        emit_pv(*items[j - LA])
                dst = self.ycT[b][row0:row0 + 1024, :].rearrange("(k p) t -> p k t", p=128)
                c.dma(self.stq, dst[:, :, t0:t0 + n], ot[:, :, :n], ot, reads=[ot], writes=[c.dr("ycT", b, seg, i, row0 // 1024)])

    def layer1_mixer(self):
        c = self.c
        blks = token_blocks()
        with c.scope():
            w_in = c.sb([128, 8, 768], BF16, "m_w_in")
            w_uq = c.sb([128, 3, 2048], BF16, "m_w_uq")
            w_uk = c.sb([128, 2, 1024], BF16, "m_w_uk")
            w_uv = c.sb([128, 2, 1024], BF16, "m_w_uv")
            gq = c.sb([128, 3], F32, "m_gq")
            gkv = c.sb([128, 2], F32, "m_gkv")
            cs = c.sb([128, T], F32, "m_cs")
            sn = c.sb([128, T], F32, "m_sn")
            wv = self.mla_w_in.rearrange("(k p) n -> p k n", p=128)
            for k in range(8):
                c.dma("pool", w_in[:, k, :], wv[:, k, :], w_in, writes=[w_in])
            for (wt, src) in ((w_uq, self.mla_w_uq), (w_uk, self.mla_w_uk), (w_uv, self.mla_w_uv)):
                sv = src.rearrange("(k p) n -> p k n", p=128)
                for k in range(sv.shape[1]):
                    c.dma("pool", wt[:, k, :], sv[:, k, :], wt, writes=[wt])
            c.dma("sp", gq[:], self.mla_gq[:, :], gq, writes=[gq])
            c.dma("sp", gkv[:], self.mla_gkv[:, :], gkv, writes=[gkv])
            c.dma("sp", cs[:], self.cs1[:, :], cs, writes=[cs])
            c.dma("sp", sn[:], self.sn1[:, :], sn, writes=[sn])
            hTb = c.sb([128, 8, TT], BF16, "m_hT")
            cqn = c.sb([128, 3, TT], BF16, "m_cqn")
            ckvn = c.sb([128, 2, TT], BF16, "m_ckvn")
            kper = c.sb([128, TT], BF16, "m_kper")
            rawp = Pool(c, [128, 3, 512], F32, 2, "m_raw")
            sqp = Pool(c, [128, 3, 512], BF16, 2, "m_sq")
            rsp = Pool(c, [128, 512], F32, 2, "m_rs")
            tmp = Pool(c, [128, 512], F32, 4, "m_tmp")
            qop = Pool(c, [128, 512], BF16, 3, "m_qo")
            khp = Pool(c, [96, TT], BF16, 2, "m_kh")
            vtp = Pool(c, [128, 16, 128], BF16, 3, "m_vt")
            for b in range(NB):
                hv = self.hT[b].rearrange("(k p) t -> p k t", p=128)
                for k in range(8):
                    c.dma("sp", hTb[:, k, :], hv[:, k, :], hTb, writes=[hTb])
                for (seg, i, t0, n) in blks:
                    for (c0, nc_, gt, dst, inv) in ((0, 3, gq, cqn, 1.0 / 384), (3, 2, gkv, ckvn, 1.0 / 256)):
                        raw = rawp.get()
                        sq = sqp.get()
                        for cc in range(nc_):
                            ps = self.psA.get()
                            for k in range(8):
                                c.op("pe", lambda e, ps=ps, k=k, cc=cc, c0=c0, t0=t0, n=n: e.matmul(
                                    ps[:, :n], w_in[:, k, (c0 + cc) * 128:(c0 + cc + 1) * 128], hTb[:, k, t0:t0 + n],
                                    start=(k == 0), stop=(k == 7)), [w_in, hTb], [ps])
                            c.op("act", lambda e, ps=ps, raw=raw, cc=cc, n=n: e.activation(out=raw[:, cc, :n], in_=ps[:, :n], func=AF.Copy),
                                 [ps], [raw])
                            c.op("act", lambda e, ps=ps, sq=sq, cc=cc, n=n: e.activation(out=sq[:, cc, :n], in_=ps[:, :n], func=AF.Square),
                                 [ps], [sq])
                        pss = self.psC.get()
                        for cc in range(nc_):
                            c.op("pe", lambda e, pss=pss, sq=sq, cc=cc, n=n, nc_=nc_: e.matmul(
                                pss[:, :n], self.ones_bf[:, :], sq[:, cc, :n], start=(cc == 0), stop=(cc == nc_ - 1)),
                                 [sq, self.ones_bf], [pss])
                        rs = rsp.get()
                        self.rstd(rs, pss, n, inv)
                        for cc in range(nc_):
                            c.op("dve", lambda e, dst=dst, raw=raw, rs=rs, gt=gt, cc=cc, t0=t0, n=n: e.scalar_tensor_tensor(
                                out=dst[:, cc, t0:t0 + n], in0=raw[:, cc, :n], scalar=gt[:, cc:cc + 1], in1=rs[:, :n],
                                op0=ALU.mult, op1=ALU.mult), [raw, rs, gt], [dst])
                    ps = self.psA.get()
                    for k in range(8):
                        c.op("pe", lambda e, ps=ps, k=k, t0=t0, n=n: e.matmul(
                            ps[:, :n], w_in[:, k, 640:768], hTb[:, k, t0:t0 + n], start=(k == 0), stop=(k == 7)), [w_in, hTb], [ps])
                    if seg == "c":
                        c.op("act", lambda e, ps=ps, t0=t0, n=n: e.activation(out=kper[64:96, t0:t0 + n], in_=ps[64:96, :n], func=AF.Copy),
                             [ps], [kper])
                    else:
                        l0 = t0 - CT
                        self.rope32(ps, kper, t0, n, l0, cs, sn, tmp)
                for h in range(16):
                    for (seg, i, t0, n) in blks:
                        if seg == "c":
                            continue
                        ps = self.psA.get()
                        for kk in range(3):
                            c.op("pe", lambda e, ps=ps, kk=kk, h=h, t0=t0, n=n: e.matmul(
                                ps[:, :n], w_uq[:, kk, h * 128:(h + 1) * 128], cqn[:, kk, t0:t0 + n], start=(kk == 0), stop=(kk == 2)),
                                 [w_uq, cqn], [ps])
                        qo = qop.get()
                        c.op("act", lambda e, ps=ps, qo=qo, n=n: e.activation(out=qo[0:64, :n], in_=ps[0:64, :n], func=AF.Copy), [ps], [qo])
                        self.rope32(ps, qo, 0, n, t0 - CT, cs, sn, tmp)
                        c.dma("sp", self.Q1T[b, h, :, t0:t0 + n], qo[0:96, :n], qo, reads=[qo])
                    kh = khp.get()
                    for (seg, i, t0, n) in blks:
                        ps = self.psA.get()
                        for kk in range(2):
                            c.op("pe", lambda e, ps=ps, kk=kk, h=h, t0=t0, n=n: e.matmul(
                                ps[0:64, :n], w_uk[:, kk, h * 64:(h + 1) * 64], ckvn[:, kk, t0:t0 + n], start=(kk == 0), stop=(kk == 1)),
                                 [w_uk, ckvn], [ps])
                        c.op("act", lambda e, ps=ps, kh=kh, t0=t0, n=n: e.activation(out=kh[0:64, t0:t0 + n], in_=ps[0:64, :n], func=AF.Copy),
                             [ps], [kh])
                    c.op("dve", lambda e, kh=kh: e.tensor_copy(kh[64:96, :], kper[64:96, :]), [kper], [kh])
                    c.dma("sp", self.K1T[b, h], kh[:, :], kh, reads=[kh])
                for tt in range(18):
                    vt = vtp.get()
                    c.op("pool", lambda e, vt=vt: e.memset(vt[:], 1.0), [], [vt])
                    for hf in range(2):
                        ps = self.psA.get()
                        for kk in range(2):
                            c.op("pe", lambda e, ps=ps, kk=kk, hf=hf, tt=tt: e.matmul(
                                ps[:, :], ckvn[:, kk, tt * 128:(tt + 1) * 128], w_uv[:, kk, hf * 512:(hf + 1) * 512],
                                start=(kk == 0), stop=(kk == 1)), [ckvn, w_uv], [ps])
                        c.op("act", lambda e, ps=ps, vt=vt, hf=hf: e.activation(
                            out=vt[:, hf * 8:(hf + 1) * 8, 0:64], in_=ps[:, :].rearrange("p (h d) -> p h d", d=64), func=AF.Copy),
                             [ps], [vt])
                    c.dma("sp", self.V1[b, tt * 128:(tt + 1) * 128], vt[:], vt, reads=[vt])
        for b in range(NB):
            qsets = [(seg, i, t0, n, 18) for (seg, i, t0, n) in blks if seg == "l"]
            self.attention(f"m{b}", b, 16, 16, 96, 96 ** -0.5, self.K1T, self.Q1T, self.V1, 0, qsets)

    def rope32(self, ps, dst, d0, n, l0, cs, sn, tmp):
        c = self.c
        t1 = tmp.get()
        t2 = tmp.get()
        c.op("act", lambda e: e.activation(out=t1[64:96, :n], in_=ps[96:128, :n], func=AF.Copy), [ps], [t1])
        c.op("dve", lambda e: e.tensor_tensor(t1[64:96, :n], t1[64:96, :n], sn[64:96, l0:l0 + n], op=ALU.mult), [t1, sn], [t1])
        c.op("dve", lambda e: e.tensor_tensor(t2[64:96, :n], ps[64:96, :n], cs[64:96, l0:l0 + n], op=ALU.mult), [ps, cs], [t2])
        c.op("dve", lambda e: e.tensor_tensor(dst[64:96, d0:d0 + n], t1[64:96, :n], t2[64:96, :n], op=ALU.add), [t1, t2], [dst])
    def layer0_mixer(self):
        self.l0_proj_ssd()
        self.l0_proj_att()
        self.l0_ssd_scan()
        blks = token_blocks()
        for b in range(NB):
            qsets = [(seg, i, t0, n, 2 if seg == "c" else 18) for (seg, i, t0, n) in blks]
            self.attention(f"a{b}", b, 16, 4, 64, 0.125, self.K0T, self.Q0T, self.V0, 1024, qsets)

    def load_hTb(self, hTb, b):
        c = self.c
        hv = self.hT[b].rearrange("(k p) t -> p k t", p=128)
        for k in range(8):
            c.dma("sp", hTb[:, k, :], hv[:, k, :], hTb, writes=[hTb])

    def l0_proj_ssd(self):
        c = self.c
        blks = token_blocks()
        with c.scope():
            w = c.sb([128, 8, 3104], BF16, "s_w")
            wv = self.w_ssd.rearrange("(k p) n -> p k n", p=128)
            for k in range(8):
                c.dma("pool", w[:, k, :], wv[:, k, :], w, writes=[w])
            cw = c.sb([128, 16, 3], F32, "s_cw")
            cb = c.sb([128, 16], F32, "s_cb")
            c.dma("sp", cw[:], self.convw[:, :, :], cw, writes=[cw])
            c.dma("sp", cb[:], self.convb[:, :], cb, writes=[cb])
            hTb = c.sb([128, 8, TT], BF16, "s_hT")
            rawp = Pool(c, [128, TT], F32, 2, "s_raw")
            yp = Pool(c, [128, TT], F32, 1, "s_y")
            ysp = Pool(c, [128, TT], BF16, 2, "s_ys")
            xBt = c.sb([128, 18, 1536], BF16, "s_xBt")
            ztp = Pool(c, [128, 1024], F32, 2, "s_zt")
            dta = c.sb([128, 18, 32], F32, "s_dta")
            for b in range(NB):
                self.load_hTb(hTb, b)
                for cc in range(16):
                    raw = rawp.get()
                    for (seg, i, t0, n) in blks:
                        ps = self.psA.get()
                        for k in range(8):
                            c.op("pe", lambda e, ps=ps, k=k, cc=cc, t0=t0, n=n: e.matmul(
                                ps[:, :n], w[:, k, cc * 128:(cc + 1) * 128], hTb[:, k, t0:t0 + n], start=(k == 0), stop=(k == 7)),
                                 [w, hTb], [ps])
                        c.op("act", lambda e, ps=ps, raw=raw, t0=t0, n=n: e.activation(out=raw[:, t0:t0 + n], in_=ps[:, :n], func=AF.Copy),
                             [ps], [raw])
                    y = yp.get()
                    c.op("act", lambda e, y=y, raw=raw, cc=cc: e.activation(out=y[:, :], in_=raw[:, :], func=AF.Identity,
                                                                           scale=cw[:, cc, 1:2], bias=cb[:, cc:cc + 1]), [raw, cw, cb], [y])
                    for (s0, s1) in ((0, CT), (CT, TT)):
                        c.op("dve", lambda e, y=y, raw=raw, cc=cc, s0=s0, s1=s1: e.scalar_tensor_tensor(
                            out=y[:, s0 + 1:s1], in0=raw[:, s0:s1 - 1], scalar=cw[:, cc, 0:1], in1=y[:, s0 + 1:s1],
                            op0=ALU.mult, op1=ALU.add), [raw, y, cw], [y])
                        c.op("dve", lambda e, y=y, raw=raw, cc=cc, s0=s0, s1=s1: e.scalar_tensor_tensor(
                            out=y[:, s0:s1 - 1], in0=raw[:, s0 + 1:s1], scalar=cw[:, cc, 2:3], in1=y[:, s0:s1 - 1],
                            op0=ALU.mult, op1=ALU.add), [raw, y, cw], [y])
                    ys = ysp.get()
                    c.op("act", lambda e, ys=ys, y=y: e.activation(out=ys[:, :], in_=y[:, :], func=AF.Silu), [y], [ys])
                    if cc < 12:
                        for t4 in range(0, 18, 4):
                            nt = min(4, 18 - t4)
                            ps = self.psA.get()
                            for q in range(nt):
                                tt = t4 + q
                                c.op("pe", lambda e, ps=ps, ys=ys, tt=tt, q=q: e.matmul(
                                    ps[:, q * 128:(q + 1) * 128], ys[:, tt * 128:(tt + 1) * 128], self.ident_bf[:, :], start=True, stop=True),
                                     [ys, self.ident_bf], [ps])
                            c.op("dve", lambda e, ps=ps, t4=t4, nt=nt, cc=cc: e.tensor_copy(
                                xBt[:, t4:t4 + nt, cc * 128:(cc + 1) * 128], ps[:, 0:nt * 128].rearrange("p (a b) -> p a b", b=128)),
                                 [ps], [xBt])
                    if cc >= 8:
                        c.dma("sp", self.BCT[b, cc - 8], ys[:, :], ys, reads=[ys])
                c.dma("sp", self.xB[b].rearrange("(tt p) f -> p tt f", p=128), xBt[:], xBt, reads=[xBt])
                for tt in range(18):
                    zt = ztp.get()
                    for hf in range(2):
                        ps = self.psA.get()
                        for k in range(8):
                            c.op("pe", lambda e, ps=ps, k=k, tt=tt, hf=hf: e.matmul(
                                ps[:, :], hTb[:, k, tt * 128:(tt + 1) * 128], w[:, k, 2048 + hf * 512:2048 + (hf + 1) * 512],
                                start=(k == 0), stop=(k == 7)), [w, hTb], [ps])
                        c.op("act", lambda e, ps=ps, zt=zt, hf=hf: e.activation(out=zt[:, hf * 512:(hf + 1) * 512], in_=ps[:, :], func=AF.Silu),
                             [ps], [zt])
                    c.dma("sp", self.zs[b, tt * 128:(tt + 1) * 128, :], zt[:], zt, reads=[zt])
                    ps = self.psA.get()
                    for k in range(8):
                        c.op("pe", lambda e, ps=ps, k=k, tt=tt: e.matmul(
                            ps[:, 0:32], hTb[:, k, tt * 128:(tt + 1) * 128], w[:, k, 3072:3104], start=(k == 0), stop=(k == 7)),
                             [w, hTb], [ps])
                    c.op("dve", lambda e, ps=ps, tt=tt: e.tensor_copy(dta[:, tt, :], ps[:, 0:32]), [ps], [dta])
                c.dma("sp", self.dtr[b].rearrange("(tt p) f -> p tt f", p=128), dta[:], dta, reads=[dta])

    def l0_proj_att(self):
        c = self.c
        blks = token_blocks()
        with c.scope():
            w = c.sb([128, 8, 2816], BF16, "a_w")
            wv = self.w_att.rearrange("(k p) n -> p k n", p=128)
            for k in range(8):
                c.dma("pool", w[:, k, :], wv[:, k, :], w, writes=[w])
            hTb = c.sb([128, 8, TT], BF16, "a_hT")
            cs = c.sb([128, T], F32, "a_cs")
            sn = c.sb([128, T], F32, "a_sn")
            gqk = c.sb([128, 4], F32, "a_gqk")
            blk = c.sb([128, 128], F32, "a_blk")
            blkb = c.sb([128, 128], BF16, "a_blkb")
            c.dma("sp", cs[:], self.cs0[:, :], cs, writes=[cs])
            c.dma("sp", sn[:], self.sn0[:, :], sn, writes=[sn])
            c.dma("sp", gqk[:], self.gqk[:, :], gqk, writes=[gqk])
            c.dma("sp", blk[:], self.blk64[:, :], blk, writes=[blk])
            c.op("dve", lambda e: e.tensor_copy(blkb[:], blk[:]), [blk], [blkb])
            tabs = []
            for j, src in enumerate((cs, sn, cs, sn)):
                t = c.sb([128, T], F32, f"a_tab{j}")
                c.op("pool", lambda e, t=t, src=src, j=j: e.tensor_scalar(t[:], src[:], gqk[:, j:j + 1], None, op0=ALU.mult), [src, gqk], [t])
                tabs.append(t)
            sqp = Pool(c, [128, 512], BF16, 2, "a_sq")
            rsp = Pool(c, [128, 512], F32, 2, "a_rs")
            tp = Pool(c, [128, 512], F32, 4, "a_t")
            op_ = Pool(c, [128, 512], BF16, 3, "a_o")
            vtp = Pool(c, [128, 4, 128], BF16, 3, "a_vt")
            for b in range(NB):
                self.load_hTb(hTb, b)
                for cc in range(10):
                    isq = cc < 8
                    c0 = cc * 128 if isq else 2048 + (cc - 8) * 128
                    r0 = c0 + (1024 if isq else 256)
                    gc, gs = (tabs[0], tabs[1]) if isq else (tabs[2], tabs[3])
                    gcol = 0 if isq else 2
                    for (seg, i, t0, n) in blks:
                        psq = self.psA.get()
                        for k in range(8):
                            c.op("pe", lambda e, ps=psq, k=k, c0=c0, t0=t0, n=n: e.matmul(
                                ps[:, :n], w[:, k, c0:c0 + 128], hTb[:, k, t0:t0 + n], start=(k == 0), stop=(k == 7)), [w, hTb], [psq])
                        sq = sqp.get()
                        c.op("act", lambda e, sq=sq, ps=psq, n=n: e.activation(out=sq[:, :n], in_=ps[:, :n], func=AF.Square), [psq], [sq])
                        pss = self.psC.get()
                        c.op("pe", lambda e, pss=pss, sq=sq, n=n: e.matmul(pss[:, :n], blkb[:, :], sq[:, :n], start=True, stop=True),
                             [blkb, sq], [pss])
                        rs = rsp.get()
                        self.rstd(rs, pss, n, 1.0 / 64)
                        o = op_.get()
                        if seg == "c":
                            c.op("dve", lambda e, o=o, ps=psq, rs=rs, gcol=gcol, n=n: e.scalar_tensor_tensor(
                                out=o[:, :n], in0=ps[:, :n], scalar=gqk[:, gcol:gcol + 1], in1=rs[:, :n], op0=ALU.mult, op1=ALU.mult),
                                 [psq, rs, gqk], [o])
                        else:
                            l0 = t0 - CT
                            psr = self.psA.get()
                            for k in range(8):
                                c.op("pe", lambda e, ps=psr, k=k, r0=r0, t0=t0, n=n: e.matmul(
                                    ps[:, :n], w[:, k, r0:r0 + 128], hTb[:, k, t0:t0 + n], start=(k == 0), stop=(k == 7)), [w, hTb], [psr])
                            t1 = tp.get()
                            t2 = tp.get()
                            c.op("dve", lambda e, t1=t1, ps=psq, gc=gc, l0=l0, n=n: e.tensor_tensor(t1[:, :n], ps[:, :n], gc[:, l0:l0 + n], op=ALU.mult),
                                 [psq, gc], [t1])
                            c.op("dve", lambda e, t2=t2, ps=psr, gs=gs, l0=l0, n=n: e.tensor_tensor(t2[:, :n], ps[:, :n], gs[:, l0:l0 + n], op=ALU.mult),
                                 [psr, gs], [t2])
                            c.op("pool", lambda e, t1=t1, t2=t2, n=n: e.tensor_tensor(t1[:, :n], t1[:, :n], t2[:, :n], op=ALU.add), [t1, t2], [t1])
                            c.op("dve", lambda e, o=o, t1=t1, rs=rs, n=n: e.tensor_tensor(o[:, :n], t1[:, :n], rs[:, :n], op=ALU.mult), [t1, rs], [o])
                        for hh in range(2):
                            if isq:
                                dst = self.Q0T[b, 2 * cc + hh, :, t0:t0 + n]
                            else:
                                dst = self.K0T[b, 2 * (cc - 8) + hh, :, t0:t0 + n]
                            c.dma("sp", dst, o[hh * 64:(hh + 1) * 64, :n], o, reads=[o])
                for tt in range(18):
                    vt = vtp.get()
                    c.op("pool", lambda e, vt=vt: e.memset(vt[:], 1.0), [], [vt])
                    ps = self.psA.get()
                    for k in range(8):
                        c.op("pe", lambda e, ps=ps, k=k, tt=tt: e.matmul(
                            ps[:, 0:256], hTb[:, k, tt * 128:(tt + 1) * 128], w[:, k, 2560:2816], start=(k == 0), stop=(k == 7)),
                             [w, hTb], [ps])
                    c.op("act", lambda e, ps=ps, vt=vt: e.activation(out=vt[:, :, 0:64], in_=ps[:, 0:256].rearrange("p (h d) -> p h d", d=64),
                                                                     func=AF.Copy), [ps], [vt])
                    c.dma("sp", self.V0[b, tt * 128:(tt + 1) * 128], vt[:], vt, reads=[vt])

    def l0_ssd_scan(self):
        c = self.c
        with c.scope():
            S = type("S", (), {})()
            mk = c.sb([128, 4, 128], F32, "d_masks")
            c.dma("sp", mk[:], self.masks.rearrange("m p f -> p m f"), mk, writes=[mk])
            S.mk = mk
            dtb = c.sb([128, 32], F32, "d_dtb")
            alog = c.sb([128, 32], F32, "d_alog")
            S.abc = c.sb([128, 32], F32, "d_abc")
            S.dsk = c.sb([128, 16], F32, "d_dsk")
            S.ssdg = c.sb([128, 1024], F32, "d_ssdg")
            S.onec = c.sb([128, 1], F32, "d_onec")
            c.op("dve", lambda e: e.memset(S.onec[:], 1.0), [], [S.onec])
            c.dma("sp", dtb[:], self.dtb_bc[:, :], dtb, writes=[dtb])
            c.dma("sp", alog[:], self.alog_bc[:, :], alog, writes=[alog])
            c.dma("sp", S.dsk[:], self.dsk_bc[:, :], S.dsk, writes=[S.dsk])
            c.dma("sp", S.ssdg[:], self.ssdg_bc[:, :], S.ssdg, writes=[S.ssdg])
            c.op("act", lambda e: e.activation(out=S.abc[:], in_=alog[:], func=AF.Exp), [alog], [S.abc])
            c.op("dve", lambda e: e.tensor_scalar(S.abc[:], S.abc[:], -1.0, None, op0=ALU.mult), [S.abc], [S.abc])
            S.dtb = dtb
            S.hsts = [c.sb([128, 1024], F32, f"d_hst{i}") for i in range(NB)]
            S.hbfs = [c.sb([128, 1024], BF16, f"d_hbf{i}") for i in range(NB)]
            S.xbp = Pool(c, [128, 1536], BF16, 5, "d_xb")
            S.bcp = Pool(c, [128, 8, 128], BF16, 5, "d_bc")
            S.dtp = Pool(c, [128, 32], F32, 5, "d_dt")
            S.smp = Pool(c, [128, 32], F32, 32, "d_sm")
            S.xdtp = Pool(c, [128, 16, 64], BF16, 6, "d_xdt")
            S.yop = Pool(c, [128, 1024], F32, 3, "d_yo")
            S.cbp = Pool(c, [128, 128], F32, 5, "d_cb")
            S.rgp = Pool(c, [128, 4, 128], F32, 5, "d_rg")
            S.ep = Pool(c, [128, 512], F32, 5, "d_e")
            S.mtp = Pool(c, [128, 4, 128], BF16, 5, "d_mt")
            S.ydp = Pool(c, [128, 1024], F32, 3, "d_yd")
            S.zp = Pool(c, [128, 1024], F32, 5, "d_z")
            S.yfp = Pool(c, [128, 1024], F32, 5, "d_yfl")
            S.ynp = Pool(c, [128, 1024], BF16, 2, "d_yn")
            S.ytp = Pool(c, [128, 8, 128], BF16, 2, "d_yt")
            S.pX = self.psA
            S.pY = self.psB
            S.pZ = self.psC
            for dr_ in range(2):
                for b in range(NB):
                    c.op("dve", lambda e, b=b: e.memset(S.hsts[b][:], 0.0), [], [S.hsts[b]])
                    c.op("dve", lambda e, b=b: e.memset(S.hbfs[b][:], 0.0), [], [S.hbfs[b]])
                order = list(range(18)) if dr_ == 0 else [1, 0] + list(range(17, 1, -1))
                steps = [(b, ch) for ch in order for b in range(NB)]
                PF = 2
                loads = {}
                for j in range(len(steps) + PF):
                    if j < len(steps):
                        loads[j] = self.ssd_loads(steps[j][0], steps[j][1], dr_, S)
                    if j >= PF:
                        bb, ch = steps[j - PF]
                        self.ssd_chunk(bb, ch, dr_, S, loads.pop(j - PF))

    def ssd_loads(self, b, ch, dr_, S):
        c = self.c
        tok = slice(ch * 128, (ch + 1) * 128)
        xBt = S.xbp.get()
        c.dma("sp", xBt[:], self.xB[b, tok, :], xBt, writes=[xBt])
        BCt = S.bcp.get()
        c.dma("sp", BCt[:], self.BCT[b].rearrange("g p t -> p g t")[:, :, tok], BCt, writes=[BCt])
        dtt = S.dtp.get()
        c.dma("sp", dtt[:], self.dtr[b, tok, :], dtt, writes=[dtt])
        yfl = zt = None
        if dr_ == 1:
            yfl = S.yfp.get()
            c.dma("sp", yfl[:], self.yf[b, tok, :], yfl, reads=[c.dr("yf", b, ch)], writes=[yfl])
            zt = S.zp.get()
            c.dma("sp", zt[:], self.zs[b, tok, :], zt, writes=[zt])
        return xBt, BCt, dtt, yfl, zt

    def ssd_chunk(self, b, ch, dr_, S, L):
        c = self.c
        hst = S.hsts[b]
        hbf = S.hbfs[b]
        tok = slice(ch * 128, (ch + 1) * 128)
        cols = slice(dr_ * 16, dr_ * 16 + 16)
        Tm = S.mk[:, dr_, :]
        U = S.mk[:, 2 + dr_, :]
        xBt, BCt, dtt, yfl, zt = L
        xs, ax, dt, dA, sm1, sm2, dtd = (S.smp.get() for _ in range(7))
        c.op("dve", lambda e: e.tensor_tensor(xs[:], dtt[:], S.dtb[:], op=ALU.add), [dtt, S.dtb], [xs])
        c.op("act", lambda e: e.activation(out=ax[:], in_=xs[:], func=AF.Abs), [xs], [ax])
        c.op("act", lambda e: e.activation(out=ax[:], in_=ax[:], func=AF.Exp, scale=-1.0), [ax], [ax])
        c.op("act", lambda e: e.activation(out=ax[:], in_=ax[:], func=AF.Ln, bias=S.onec[:, :]), [ax, S.onec], [ax])
        c.op("dve", lambda e: e.scalar_tensor_tensor(out=dt[:], in0=xs[:], scalar=0.0, in1=ax[:], op0=ALU.max, op1=ALU.add), [xs, ax], [dt])
        c.op("dve", lambda e: e.tensor_tensor(dA[:], dt[:], S.abc[:], op=ALU.mult), [dt, S.abc], [dA])
        pc = S.pX.get()
        c.op("pe", lambda e: e.matmul(pc[:, 0:16], Tm, dA[:, cols], start=True, stop=True), [S.mk, dA], [pc])
        c.op("pe", lambda e: e.matmul(pc[:, 16:32], self.ones_f[:, :], dA[:, cols], start=True, stop=True), [self.ones_f, dA], [pc])
        c.op("act", lambda e: e.activation(out=sm1[:], in_=pc[:, 0:32], func=AF.Copy), [pc], [sm1])
        c.op("act", lambda e: e.activation(out=sm2[:], in_=pc[:, 0:32], func=AF.Exp), [pc], [sm2])
        c.op("dve", lambda e: e.tensor_tensor(dtd[:, 0:16], sm1[:, 16:32], sm1[:, 0:16], op=ALU.subtract), [sm1], [dtd])
        c.op("act", lambda e: e.activation(out=dtd[:, 0:16], in_=dtd[:, 0:16], func=AF.Exp), [dtd], [dtd])
        c.op("dve", lambda e: e.tensor_tensor(dtd[:, 0:16], dtd[:, 0:16], dt[:, cols], op=ALU.mult), [dtd, dt], [dtd])
        xdt = S.xdtp.get()
        xdtd = S.xdtp.get()
        xv = xBt[:, 0:1024].rearrange("p (h d) -> p h d", d=64)
        c.op("pool", lambda e: e.tensor_tensor(xdt[:], xv, dt[:, cols].unsqueeze(2).to_broadcast([128, 16, 64]), op=ALU.mult),
             [xBt, dt], [xdt])
        c.op("pool", lambda e: e.tensor_tensor(xdtd[:], xv, dtd[:, 0:16].unsqueeze(2).to_broadcast([128, 16, 64]), op=ALU.mult),
             [xBt, dtd], [xdtd])
        pyo = [S.pZ.get(), S.pZ.get()]
        for g in range(4):
            c.op("pe", lambda e, g=g: e.matmul(pyo[g // 2][:, (g % 2) * 256:(g % 2 + 1) * 256], BCt[:, 4 + g, :],
                                               hbf[:, g * 256:(g + 1) * 256], start=True, stop=True), [BCt, hbf], [pyo[g // 2]])
        yo = S.yop.get()
        for hf in range(2):
            c.op("dve", lambda e, hf=hf: e.tensor_tensor(
                yo[:, hf * 512:(hf + 1) * 512].rearrange("p (h d) -> p h d", d=64),
                pyo[hf][:, :].rearrange("p (h d) -> p h d", d=64),
                sm2[:, hf * 8:(hf + 1) * 8].unsqueeze(2).to_broadcast([128, 8, 64]), op=ALU.mult), [pyo[hf], sm2], [yo])
        pyd = [S.pY.get(), S.pY.get()]
        cbms, rgs, psegs, Es, MTs = [], [], [], [], []
        for g in range(4):
            pcb = S.pX.get()
            c.op("pe", lambda e, pcb=pcb, g=g: e.matmul(pcb[:, 0:128], BCt[:, g, :], BCt[:, 4 + g, :], start=True, stop=True), [BCt], [pcb])
            cbm = S.cbp.get()
            c.op("dve", lambda e, cbm=cbm, pcb=pcb: e.tensor_tensor(cbm[:], pcb[:, 0:128], Tm, op=ALU.mult), [pcb, S.mk], [cbm])
            cbms.append(cbm)
            rg = S.rgp.get()
            c.op("pool", lambda e, rg=rg, g=g: e.tensor_tensor(
                rg[:], Tm.unsqueeze(1).to_broadcast([128, 4, 128]),
                dA[:, dr_ * 16 + 4 * g:dr_ * 16 + 4 * g + 4].unsqueeze(2).to_broadcast([128, 4, 128]), op=ALU.mult), [S.mk, dA], [rg])
            rgs.append(rg)
        for g in range(4):
            pseg = S.pX.get()
            rg = rgs[g]
            c.op("pe", lambda e, pseg=pseg, rg=rg: e.matmul(pseg[:, :], U, rg[:].rearrange("p a b -> p (a b)"), start=True, stop=True),
                 [S.mk, rg], [pseg])
            E = S.ep.get()
            c.op("act", lambda e, E=E, pseg=pseg: e.activation(out=E[:], in_=pseg[:, :], func=AF.Exp), [pseg], [E])
            Es.append(E)
        for g in range(4):
            MT = S.mtp.get()
            E, cbm = Es[g], cbms[g]
            c.op("dve", lambda e, MT=MT, E=E, cbm=cbm: e.tensor_tensor(
                MT[:], E[:].rearrange("p (a b) -> p a b", b=128), cbm[:].unsqueeze(1).to_broadcast([128, 4, 128]), op=ALU.mult),
                 [E, cbm], [MT])
            MTs.append(MT)
        for g in range(4):
            MT = MTs[g]
            for hh in range(4):
                h = 4 * g + hh
                c.op("pe", lambda e, MT=MT, hh=hh, h=h: e.matmul(pyd[h // 8][:, (h % 8) * 64:(h % 8 + 1) * 64], MT[:, hh, :], xdt[:, h, :],
                                                             start=True, stop=True), [MT, xdt], [pyd[h // 8]])
        yd = S.ydp.get()
        for hf in range(2):
            c.op("dve", lambda e, hf=hf: e.tensor_tensor(yd[:, hf * 512:(hf + 1) * 512], pyd[hf][:, :], yo[:, hf * 512:(hf + 1) * 512], op=ALU.add),
                 [pyd[hf], yo], [yd])
        pst = [S.pZ.get(), S.pZ.get()]
        for g in range(4):
            c.op("pe", lambda e, g=g: e.matmul(pst[g // 2][:, (g % 2) * 256:(g % 2 + 1) * 256], xBt[:, 1024 + g * 128:1024 + (g + 1) * 128],
                                               xdtd[:, 4 * g:4 * g + 4, :].rearrange("p a b -> p (a b)"), start=True, stop=True),
                 [xBt, xdtd], [pst[g // 2]])
        c.op("dve", lambda e: e.tensor_tensor(hst[:].rearrange("p (h d) -> p h d", d=64), hst[:].rearrange("p (h d) -> p h d", d=64),
                                              sm2[:, 16:32].unsqueeze(2).to_broadcast([128, 16, 64]), op=ALU.mult), [hst, sm2], [hst])
        for hf in range(2):
            c.op("dve", lambda e, hf=hf: e.tensor_tensor(hst[:, hf * 512:(hf + 1) * 512], hst[:, hf * 512:(hf + 1) * 512], pst[hf][:, :],
                                                        op=ALU.add), [hst, pst[hf]], [hst])
        c.op("act", lambda e: e.activation(out=hbf[:], in_=hst[:], func=AF.Copy), [hst], [hbf])
        if dr_ == 0:
            c.dma("sp", self.yf[b, tok, :], yd[:], yd, reads=[yd], writes=[c.dr("yf", b, ch)])
            return
        if "ybd" in self.dump:
            c.dma("sp", self.ybd[b, tok, :], yd[:], yd, reads=[yd])
        c.op("dve", lambda e: e.tensor_tensor(yd[:], yd[:], yfl[:], op=ALU.add), [yd, yfl], [yd])
        c.op("pool", lambda e: e.tensor_tensor(yfl[:].rearrange("p (h d) -> p h d", d=64), xv,
                                               S.dsk[:, :].unsqueeze(2).to_broadcast([128, 16, 64]), op=ALU.mult), [xBt, S.dsk], [yfl])
        c.op("dve", lambda e: e.tensor_tensor(yd[:], yd[:], yfl[:], op=ALU.add), [yd, yfl], [yd])
        c.op("dve", lambda e: e.tensor_tensor(yd[:], yd[:], zt[:], op=ALU.mult), [yd, zt], [yd])
        c.op("pool", lambda e: e.tensor_tensor(zt[:], yd[:], yd[:], op=ALU.mult), [yd], [zt])
        ss = S.smp.get()
        c.op("dve", lambda e: e.reduce_sum(ss[:, 0:1], zt[:], axis=mybir.AxisListType.X), [zt], [ss])
        c.op("act", lambda e: e.activation(out=ss[:, 0:1], in_=ss[:, 0:1], func=AF.Sqrt, scale=1.0 / 1024, bias=self.epsc[:, :]),
             [ss, self.epsc], [ss])
        c.op("dve", lambda e: e.reciprocal(ss[:, 0:1], ss[:, 0:1]), [ss], [ss])
        yn = S.ynp.get()
        c.op("dve", lambda e: e.scalar_tensor_tensor(out=yn[:], in0=yd[:], scalar=ss[:, 0:1], in1=S.ssdg[:], op0=ALU.mult, op1=ALU.mult),
             [yd, ss, S.ssdg], [yn], strict=True)
        yt = S.ytp.get()
        for q4 in range(2):
            ps = S.pX.get()
            for q in range(4):
                cc = q4 * 4 + q
                c.op("pe", lambda e, ps=ps, q=q, cc=cc: e.matmul(ps[:, q * 128:(q + 1) * 128], yn[:, cc * 128:(cc + 1) * 128], self.ident_bf[:, :],
                                                                start=True, stop=True), [yn, self.ident_bf], [ps])
            c.op("act", lambda e, ps=ps, q4=q4: e.activation(out=yt[:, q4 * 4:(q4 + 1) * 4, :], in_=ps[:, :].rearrange("p (a b) -> p a b", b=128),
                                                            func=AF.Copy), [ps], [yt])
        seg, i = ("c", 0) if ch < 2 else ("l", (ch - 2) // 4)
        dst = self.ycT[b][0:1024, :].rearrange("(k p) t -> p k t", p=128)
        c.dma("sp", dst[:, :, tok], yt[:], yt, reads=[yt], writes=[c.dr("ycT", b, "ssd", ch)])


def fm(v):
    v = np.asarray(v)
    n = v.shape[-1] // 128
    return np.ascontiguousarray(np.swapaxes(v.reshape(v.shape[:-1] + (n, 128)), -1, -2))


def rope_tables(rot_dim):
    n_freq = rot_dim // 4
    rows = T // 64
    row = np.repeat(np.arange(rows, dtype=np.float32), 64)
    col = np.tile(np.arange(64, dtype=np.float32), rows)
    inv = (np.float32(10000.0) ** (-np.arange(n_freq, dtype=np.float32) / np.float32(n_freq))).astype(np.float32)
    ang = np.concatenate([row[:, None] * inv, col[:, None] * inv], -1).astype(np.float32)
    return np.cos(ang).astype(np.float32), np.sin(ang).astype(np.float32)


def make_shared(inp):
    f32 = np.float32
    ca = lambda a: np.ascontiguousarray(a, dtype=f32)
    sh = dict(
        ada_w=ca(inp["ada_w"]),
        ada_b3=ca(np.repeat(inp["ada_b"][:, None, :], 3, axis=1)),
        g1T=fm(inp["norm1_g"]), g2T=fm(inp["norm2_g"]), gfT=fm(inp["final_norm_g"]),
        ffn_up=ca(inp["ffn_w_up"]), ffn_down=ca(inp["ffn_w_down"]),
        ident=np.eye(128, dtype=f32),
    )
    p16 = (np.arange(32) + 16) % 32
    w = inp["mla_w_in"][0]
    sh["mla_w_in"] = ca(np.concatenate([w[:, :640], np.zeros((D, 64), f32), w[:, 640:672], w[:, 640:672][:, p16]], 1))
    wq = inp["mla_w_uq"][0].reshape(384, 16, 96)
    sh["mla_w_uq"] = ca(np.concatenate([wq[:, :, :64], wq[:, :, 64:], wq[:, :, 64:][:, :, p16]], 2).reshape(384, 2048))
    wkv = inp["mla_w_ukv"][0].reshape(256, 16, 128)
    sh["mla_w_uk"] = ca(wkv[:, :, :64].reshape(256, 1024))
    sh["mla_w_uv"] = ca(wkv[:, :, 64:].reshape(256, 1024))
    sh["mla_w_o"] = ca(inp["mla_w_o"][0])
    sh["mla_gq"] = fm(inp["mla_q_norm_g"][0])
    sh["mla_gkv"] = fm(inp["mla_kv_norm_g"][0])
    c1, s1 = rope_tables(32)
    cs1 = np.zeros((128, T), f32)
    sn1 = np.zeros((128, T), f32)
    for d in range(32):
        m = d % 16
        cs1[64 + d] = c1[:, m]
        sn1[64 + d] = -s1[:, m] if d < 16 else s1[:, m]
    sh["cs1"], sh["sn1"] = cs1, sn1
    w = inp["ab_w_in"][0]
    sh["w_ssd"] = ca(np.concatenate([w[:, 1024:3072], w[:, 0:1024], w[:, 3072:3104]], 1))
    p32 = (np.arange(64) + 32) % 64
    q = w[:, 3104:4128].reshape(D, 16, 64)
    k = w[:, 4128:4384].reshape(D, 4, 64)
    sh["w_att"] = ca(np.concatenate([q.reshape(D, 1024), q[:, :, p32].reshape(D, 1024), k.reshape(D, 256),
                                     k[:, :, p32].reshape(D, 256), w[:, 4384:4640]], 1))
    sh["ab_w_out"] = ca(inp["ab_w_out"][0])
    cw = inp["ssd_conv_w"][0]
    sh["convw"] = ca(cw.T.reshape(16, 128, 3).transpose(1, 0, 2))
    sh["convb"] = fm(inp["ssd_conv_b"][0])
    sh["dtb_bc"] = ca(np.tile(inp["ssd_dt_bias"][0].reshape(1, 32), (128, 1)))
    sh["alog_bc"] = ca(np.tile(inp["ssd_a_log"][0].reshape(1, 32), (128, 1)))
    sh["dsk_bc"] = ca(np.tile(inp["ssd_d"][0].reshape(1, 16), (128, 1)))
    sh["ssdg_bc"] = ca(np.tile(inp["ssd_norm_g"][0].reshape(1, 1024), (128, 1)))
    one = np.ones((128, 128), f32)
    sh["masks"] = ca(np.stack([np.triu(one), np.tril(one), np.tril(one, -1), np.triu(one, 1)]))
    blk = np.zeros((128, 128), f32)
    blk[:64, :64] = 1
    blk[64:, 64:] = 1
    sh["blk64"] = blk
    gq = inp["att_q_g"][0]
    gk = inp["att_k_g"][0]
    d64 = np.arange(128) % 64
    sh["gqk"] = ca(np.stack([gq[d64], gq[p32[d64]], gk[d64], gk[p32[d64]]], 1))
    c0, s0 = rope_tables(64)
    cs0 = np.zeros((128, T), f32)
    sn0 = np.zeros((128, T), f32)
    for r in range(128):
        d = r % 64
        m = d % 32
        cs0[r] = c0[:, m]
        sn0[r] = -s0[:, m] if d < 32 else s0[:, m]
    sh["cs0"], sh["sn0"] = cs0, sn0
    return sh


def make_in_maps(inp, n_cores=8):
    maps = []
    shared = make_shared(inp)
    for ci in range(n_cores):
        bs = slice(ci * NB, (ci + 1) * NB)
        m = dict(shared)
        m["xT"] = np.ascontiguousarray(np.swapaxes(inp["x"][bs], 1, 2))
        m["cxT"] = np.ascontiguousarray(np.swapaxes(inp["ctx"][bs], 1, 2))
        cv = np.stack([inp["c"][ci * NB], inp["c"][ci * NB + 1], inp["c_ctx"]], axis=-1)
        m["cT"] = np.ascontiguousarray(cv.reshape(8, 128, 3).transpose(1, 0, 2))
        maps.append(m)
    return maps


_CACHE = {}


def get_prog(cfg_key=()):
    if cfg_key not in _CACHE:
        p = Prog(dict(cfg_key))
        p.build()
        _CACHE[cfg_key] = p
    return _CACHE[cfg_key]


def kernel(**inputs):
    inp = {k: np.asarray(v) for k, v in inputs.items()}
    p = get_prog()
    maps = make_in_maps(inp)
    res = run_bass_kernel_spmd(p.c.nc, maps, core_ids=list(range(8)), trace=True)
    out = np.empty((16, T, D), np.float32)
    for ci in range(8):
        o = res.results[ci]["outT"]
        out[ci * NB:(ci + 1) * NB] = np.swapaxes(o, 1, 2)
    return out
```

```python
import contextlib
import math
import numpy as np
import concourse.bass as bass
import concourse.mybir as mybir
from concourse.bass_utils import run_bass_kernel_spmd

F32 = mybir.dt.float32
BF16 = mybir.dt.bfloat16
AF = mybir.ActivationFunctionType
ALU = mybir.AluOpType

ENGS = ("pe", "act", "dve", "pool", "sp")
EPS = 1e-6
D = 1024
T = 2048
CT = 256
TT = T + CT
NB = 2
FH = 2816


class Res:
    __slots__ = ("name", "t", "last_w", "readers", "dsem", "dcount")

    def __init__(self, name, t=None):
        self.name = name
        self.t = t
        self.last_w = None
        self.readers = []
        self.dsem = None
        self.dcount = 0

    def __getitem__(self, k):
        return self.t[k]


class PhysSem:
    __slots__ = ("count", "handle", "idx")

    def __init__(self, idx):
        self.count = 0
        self.handle = None
        self.idx = idx


class Op:
    __slots__ = ("eng", "fn", "deps", "signal", "sig_idx", "is_dma", "dres", "dval", "dsem", "strict")

    def __init__(self, eng, fn):
        self.eng = eng
        self.fn = fn
        self.deps = []
        self.signal = False
        self.sig_idx = None
        self.is_dma = False
        self.dres = None
        self.dval = 0
        self.strict = False


class Ctx:
    def __init__(self, same_engine_sync=False):
        self.nc = bass.Bass("TRN2", target_bir_lowering=False)
        self.es = contextlib.ExitStack()
        self.ops = []
        self.same_engine_sync = same_engine_sync
        self.n_sems = 0
        self._uid = 0
        self.dres = {}
        self._bar_pos = 0
        self._sem_free = []
        self._sem_all = []
        self._scope_res = [[]]

    def uid(self, p):
        self._uid += 1
        return f"{p}{self._uid}"

    def sb(self, shape, dtype, name=None):
        name = "s_" + (name or self.uid("sb"))
        t = self.es.enter_context(self.nc.sbuf_tensor(name, list(shape), dtype))
        r = Res(name, t)
        self._scope_res[-1].append(r)
        return r

    def ps(self, shape, dtype=F32, name=None):
        name = "p_" + (name or self.uid("ps"))
        t = self.es.enter_context(self.nc.psum_tensor(name, list(shape), dtype))
        return Res(name, t)

    def dram(self, name, shape, dtype, kind="Internal"):
        return self.nc.dram_tensor(name, list(shape), dtype, kind=kind).ap()

    def dr(self, *key):
        r = self.dres.get(key)
        if r is None:
            r = Res("dr_" + "_".join(str(k) for k in key))
            self.dres[key] = r
        return r

    def sem(self, name):
        self.n_sems += 1
        return self.es.enter_context(self.nc.semaphore(name))

    def _track(self, op, reads, writes):
        deps = op.deps
        for r in reads:
            if r.last_w is not None:
                deps.append(r.last_w)
        for w in writes:
            if w.last_w is not None:
                deps.append(w.last_w)
            deps.extend(w.readers)
        for r in reads:
            r.readers.append(op)
        for w in writes:
            w.last_w = op
            w.readers = []
        self.ops.append(op)

    def op(self, eng, fn, reads=(), writes=(), strict=False):
        o = Op(eng, fn)
        o.strict = strict
        self._track(o, reads, writes)
        return o

    def dma(self, eng, out_ap, in_ap, sbres, reads=(), writes=(), **kw):
        o = Op(eng, lambda e: e.dma_start(out=out_ap, in_=in_ap, **kw))
        o.is_dma = True
        o.dres = sbres
        self._track(o, reads, writes)
        if sbres.dsem is None:
            if self._sem_free:
                sbres.dsem = self._sem_free.pop()
            else:
                sbres.dsem = PhysSem(len(self._sem_all))
                self._sem_all.append(sbres.dsem)
        sbres.dsem.count += 1
        o.dval = 16 * sbres.dsem.count
        o.dsem = sbres.dsem
        return o

    @contextlib.contextmanager
    def scope(self):
        outer = self.es
        inner = contextlib.ExitStack()
        self.es = inner
        self._scope_res.append([])
        try:
            with inner:
                yield
                self.barrier()
                for r in self._scope_res[-1]:
                    if r.dsem is not None:
                        self._sem_free.append(r.dsem)
                        r.dsem = None
        finally:
            self._scope_res.pop()
            self.es = outer

    def barrier(self):
        last = {}
        dmas = []
        for o in self.ops[self._bar_pos:]:
            if o.is_dma:
                dmas.append(o)
            else:
                last[o.eng] = o
        deps = list(last.values()) + dmas
        for e in ENGS:
            o = Op(e, lambda en: en.nop())
            o.deps.extend(deps)
            self.ops.append(o)
        self._bar_pos = len(self.ops)

    def finish(self, final_dmas):
        o = Op("sp", lambda e: e.nop())
        o.deps.extend(final_dmas)
        self.ops.append(o)

    def emit(self):
        nc = self.nc
        engobj = {"pe": nc.tensor, "act": nc.scalar, "dve": nc.vector, "pool": nc.gpsimd, "sp": nc.sync}
        ses = self.same_engine_sync
        for o in self.ops:
            for d in o.deps:
                if not d.is_dma:
                    if d.eng == o.eng and not o.is_dma and not ses and not o.strict:
                        continue
                    d.signal = True
        sidx = {e: 0 for e in ENGS}
        for o in self.ops:
            if not o.is_dma and o.signal:
                sidx[o.eng] += 1
                o.sig_idx = sidx[o.eng]
        esem = {e: self.sem("e_" + e) for e in ENGS}
        for ps_ in self._sem_all:
            ps_.handle = self.sem(f"dq{ps_.idx}")
        waited = {e: {} for e in ENGS}
        n_wait = 0
        for o in self.ops:
            eng = engobj[o.eng]
            w = {}
            for d in o.deps:
                if d.is_dma:
                    key = ("d", d.dsem.idx)
                    val = d.dval
                    sem = d.dsem
                else:
                    if d.eng == o.eng and not o.is_dma and not ses and not o.strict:
                        continue
                    key = ("e", d.eng)
                    val = d.sig_idx
                    sem = esem[d.eng]
                if key not in w or w[key][1] < val:
                    w[key] = (sem, val)
            wd = waited[o.eng]
            for key, (sem, val) in w.items():
                if wd.get(key, 0) >= val:
                    continue
                wd[key] = val
                eng.wait_ge(sem.handle if isinstance(sem, PhysSem) else sem, val)
                n_wait += 1
            if o.is_dma:
                o.fn(eng).then_inc(o.dsem.handle, 16)
            else:
                ins = o.fn(eng)
                if o.signal:
                    ins.then_inc(esem[o.eng], 1)
        self.stats = dict(n_ops=len(self.ops), n_wait=n_wait, n_sems=self.n_sems, sig=dict(sidx))
        return nc


class Pool:
    def __init__(self, ctx, shape, dtype, n, name, psum=False):
        mk = ctx.ps if psum else ctx.sb
        self.bufs = [mk(shape, dtype, name=f"{name}{i}") for i in range(n)]
        self.i = 0

    def get(self):
        b = self.bufs[self.i % len(self.bufs)]
        self.i += 1
        return b


def token_blocks():
    blks = [("c", 0, 0, CT)]
    for i in range(T // 512):
        blks.append(("l", i, CT + i * 512, 512))
    return blks


class Prog:
    def __init__(self, cfg):
        self.cfg = cfg
        self.c = Ctx(same_engine_sync=cfg.get("ses", False))
        self.dump = cfg.get("dump", ())
        self.stq = cfg.get("stq", "pool")
        self.outs = []

    def scratch(self, name, shape, dtype):
        kind = "ExternalOutput" if name in self.dump else "Internal"
        return self.c.dram(name, shape, dtype, kind)

    def build(self):
        c = self.c
        cfg = self.cfg
        nc = c.nc
        I = lambda n, s, d=F32: c.dram(n, s, d, "ExternalInput")
        self.I = I
        self.xT = I("xT", [NB, D, T])
        self.cxT = I("cxT", [NB, D, CT])
        self.cT = I("cT", [128, 8, 3])
        self.ada_w = I("ada_w", [2, D, 6 * D])
        self.ada_b3 = I("ada_b3", [2, 3, 6 * D])
        self.g1T = I("g1T", [2, 128, 8])
        self.g2T = I("g2T", [2, 128, 8])
        self.gfT = I("gfT", [128, 8])
        self.ffn_up = I("ffn_up", [2, D, 2 * FH])
        self.ffn_down = I("ffn_down", [2, FH, D])
        self.ident_in = I("ident", [128, 128])
        self.declare_mixer_inputs()
        self.outT = c.dram("outT", [NB, D, T], F32, "ExternalOutput")
        self.xres = self.scratch("xres", [NB, D, TT], F32)
        self.hT = self.scratch("hT", [NB, D, TT], BF16)
        self.ycT = self.scratch("ycT", [NB, 2 * D, TT], BF16)
        with c.es:
            self.alloc_common()
            with c.scope():
                self.phase_mods()
            self.copy_in_residual()
            for l in range(2):
                last = l == 1
                if not (l == 0 and cfg.get("skip_ab")) and not (l == 1 and cfg.get("skip_mla")):
                    self.phase_norm(l, which=1, ctx=True)
                    if l == 0:
                        self.layer0_mixer()
                        self.phase_outproj(l, self.ab_w_out, 16, ctx=True)
                    else:
                        self.layer1_mixer()
                        self.phase_outproj(l, self.mla_w_o, 8, ctx=False)
                if cfg.get("stop") == f"mix{l}":
                    break
                self.phase_norm(l, which=2, ctx=not last)
                self.phase_ffn(l, ctx=not last)
                if cfg.get("stop") == f"ffn{l}":
                    break
            self.phase_final()
            c.finish(self.outs)
            c.emit()
        return nc

    def alloc_common(self):
        c = self.c
        self.psA = Pool(c, [128, 512], F32, 4, "psA", psum=True)
        self.psB = Pool(c, [128, 512], F32, 2, "psB", psum=True)
        self.psC = Pool(c, [128, 512], F32, 2, "psC", psum=True)
        self.psum = self.psA
        self.ident = c.sb([128, 128], F32, "ident_sb")
        c.dma("sp", self.ident[:], self.ident_in[:, :], self.ident, writes=[self.ident])
        self.ident_bf = c.sb([128, 128], BF16, "ident_bf")
        c.op("dve", lambda e: e.tensor_copy(self.ident_bf[:], self.ident[:]), [self.ident], [self.ident_bf])
        self.epsc = c.sb([128, 1], F32, "epsc")
        c.op("dve", lambda e: e.memset(self.epsc[:], EPS), [], [self.epsc])
        self.ones_bf = c.sb([128, 128], BF16, "ones_bf")
        c.op("dve", lambda e: e.memset(self.ones_bf[:], 1.0), [], [self.ones_bf])
        self.ones_f = c.sb([128, 128], F32, "ones_f")
        c.op("dve", lambda e: e.memset(self.ones_f[:], 1.0), [], [self.ones_f])
        self.tab = {}
        for l in range(2):
            for nm in ("gs1", "sh1", "gate1", "gs2", "sh2", "gate2"):
                self.tab[(l, nm)] = c.sb([128, 8, 3], F32, f"tab_{nm}{l}")
        self.gf = c.sb([128, 8], F32, "gf")
        c.dma("sp", self.gf[:], self.gfT[:, :], self.gf, writes=[self.gf])

    def alloc_tok_pools(self):
        c = self.c
        u = c.uid("tp")
        self.xpool = Pool(c, [128, 8, 512], F32, 2, u + "xt")
        self.hpool = Pool(c, [128, 8, 512], BF16, 2, u + "ht")
        self.sqpool = Pool(c, [128, 8, 512], BF16, 1, u + "sq")
        self.xnpool = Pool(c, [128, 8, 512], F32, 1, u + "xn")
        self.rspool = Pool(c, [128, 512], F32, 2, u + "rs")

    def phase_mods(self):
        c = self.c
        cT = c.sb([128, 8, 3], F32, "cT")
        sc = c.sb([128, 8, 3], F32, "sc")
        c.dma("sp", cT[:], self.cT[:, :, :], cT, writes=[cT])
        c.op("act", lambda e: e.activation(out=sc[:], in_=cT[:], func=AF.Silu), [cT], [sc])
        wpool = Pool(c, [128, 8, 512], F32, 2, "adaw")
        adab = c.sb([3, 6 * D], F32, "adab")
        modtok = c.sb([3, 6 * D], F32, "modtok")
        for l in range(2):
            g1 = c.sb([128, 8], F32, f"g1_{l}")
            g2 = c.sb([128, 8], F32, f"g2_{l}")
            c.dma("sp", adab[:], self.ada_b3[l], adab, writes=[adab])
            c.dma("sp", g1[:], self.g1T[l], g1, writes=[g1])
            c.dma("sp", g2[:], self.g2T[l], g2, writes=[g2])
            wv = self.ada_w[l].rearrange("(k p) n -> p k n", p=128)
            for nb in range(12):
                wt = wpool.get()
                c.dma("sp", wt[:], wv[:, :, nb * 512:(nb + 1) * 512], wt, writes=[wt])
                ps = self.psum.get()
                for k in range(8):
                    c.op("pe", lambda e, ps=ps, wt=wt, k=k: e.matmul(ps[0:3, :], sc[:, k, :], wt[:, k, :],
                                                                  start=(k == 0), stop=(k == 7)),
                         [sc, wt], [ps])
                c.op("dve", lambda e, ps=ps, nb=nb: e.tensor_tensor(modtok[:, nb * 512:(nb + 1) * 512], ps[0:3, :],
                                                                  adab[:, nb * 512:(nb + 1) * 512], op=ALU.add),
                     [ps, adab], [modtok])
            ps = self.psum.get()
            for m in range(48):
                c.op("pe", lambda e, ps=ps, m=m: e.matmul(ps[:, m * 3:m * 3 + 3], modtok[0:3, m * 128:(m + 1) * 128],
                                                         self.ident[0:3, 0:3], start=True, stop=True),
                     [modtok, self.ident], [ps])
            modT = c.sb([128, 48, 3], F32, f"modT{l}")
            c.op("dve", lambda e, ps=ps, modT=modT: e.tensor_copy(modT[:].rearrange("p a b -> p (a b)"), ps[:, 0:144]),
                 [ps], [modT])
            tb = lambda nm: self.tab[(l, nm)]
            for nm, lo in (("sh1", 0), ("gate1", 16), ("sh2", 24), ("gate2", 40)):
                c.op("dve", lambda e, nm=nm, lo=lo, modT=modT, l=l: e.tensor_copy(self.tab[(l, nm)][:], modT[:, lo:lo + 8, :]),
                     [modT], [tb(nm)])
            for nm, lo, g in (("gs1", 8, g1), ("gs2", 32, g2)):
                c.op("dve", lambda e, nm=nm, lo=lo, g=g, modT=modT, l=l: e.scalar_tensor_tensor(
                    out=self.tab[(l, nm)][:], in0=modT[:, lo:lo + 8, :], scalar=1.0,
                    in1=g[:].unsqueeze(2).to_broadcast([128, 8, 3]), op0=ALU.add, op1=ALU.mult),
                     [modT, g], [tb(nm)])

    def copy_in_residual(self):
        c = self.c
        with c.scope():
            pool = Pool(c, [128, 8, 512], F32, 3, "cpin")
            for b in range(NB):
                for (seg, i, t0, n) in token_blocks():
                    xt = pool.get()
                    src = (self.cxT[b] if seg == "c" else self.xT[b]).rearrange("(k p) t -> p k t", p=128)
                    s0 = 0 if seg == "c" else i * 512
                    c.dma("sp", xt[:, :, :n], src[:, :, s0:s0 + n], xt, writes=[xt])
                    dst = self.xres[b].rearrange("(k p) t -> p k t", p=128)
                    c.dma(self.stq, dst[:, :, t0:t0 + n], xt[:, :, :n], xt, reads=[xt], writes=[c.dr("xres", b, seg, i)])

    def rstd(self, rs, ps, n, inv_n, rows=128, r0=0):
        c = self.c
        c.op("act", lambda e: e.activation(out=rs[r0:r0 + rows, :n], in_=ps[r0:r0 + rows, :n], func=AF.Sqrt, scale=inv_n,
                                           bias=self.epsc[r0:r0 + rows, :]),
             [ps, self.epsc], [rs])
        c.op("dve", lambda e: e.reciprocal(rs[r0:r0 + rows, :n], rs[r0:r0 + rows, :n]), [rs], [rs])

    def norm_tile(self, xt, n, gs, sh, out_fn):
        c = self.c
        sq = self.sqpool.get()
        c.op("act", lambda e: e.activation(out=sq[:, :, :n], in_=xt[:, :, :n], func=AF.Square), [xt], [sq])
        ps = self.psum.get()
        for k in range(8):
            c.op("pe", lambda e, k=k: e.matmul(ps[:, :n], self.ones_bf[:, :], sq[:, k, :n], start=(k == 0), stop=(k == 7)),
                 [sq, self.ones_bf], [ps])
        rs = self.rspool.get()
        self.rstd(rs, ps, n, 1.0 / D)
        xn = self.xnpool.get()
        c.op("dve", lambda e: e.tensor_tensor(xn[:, :, :n], xt[:, :, :n],
                                              rs[:, :n].unsqueeze(1).to_broadcast([128, 8, n]), op=ALU.mult),
             [xt, rs], [xn])
        for k in range(8):
            oap, ores = out_fn(k)
            if sh is not None:
                c.op("act", lambda e, k=k, oap=oap: e.activation(out=oap, in_=xn[:, k, :n], func=AF.Identity,
                                                                 scale=gs[:, k:k + 1], bias=sh[:, k:k + 1]),
                     [xn], [ores])
            else:
                c.op("act", lambda e, k=k, oap=oap: e.activation(out=oap, in_=xn[:, k, :n], func=AF.Identity,
                                                                 scale=gs[:, k:k + 1]),
                     [xn], [ores])

    def phase_norm(self, l, which, ctx):
        c = self.c
        gs_t = self.tab[(l, f"gs{which}")]
        sh_t = self.tab[(l, f"sh{which}")]
        with c.scope():
            self.alloc_tok_pools()
            for b in range(NB):
                for (seg, i, t0, n) in token_blocks():
                    if seg == "c" and not ctx:
                        continue
                    j = 2 if seg == "c" else b
                    xt = self.xpool.get()
                    src = self.xres[b].rearrange("(k p) t -> p k t", p=128)
                    c.dma("sp", xt[:, :, :n], src[:, :, t0:t0 + n], xt, reads=[c.dr("xres", b, seg, i)], writes=[xt])
                    ht = self.hpool.get()
                    self.norm_tile(xt, n, gs_t[:, :, j], sh_t[:, :, j], lambda k, ht=ht, n=n: (ht[:, k, :n], ht))
                    dst = self.hT[b].rearrange("(k p) t -> p k t", p=128)
                    c.dma(self.stq, dst[:, :, t0:t0 + n], ht[:, :, :n], ht, reads=[ht], writes=[c.dr("hT", b, seg, i)])

    def phase_ffn(self, l, ctx):
        c = self.c
        HH = FH // 2
        NM = HH // 128
        with c.scope():
            w_up = c.sb([128, 8, 2 * HH], BF16, f"w_up{l}")
            w_dn = c.sb([128, NM, D], BF16, f"w_dn{l}")
            actpool = Pool(c, [128, NM, 512], BF16, 2, f"act{l}_")
            sgpool = Pool(c, [128, 512], F32, 2, f"sg{l}_")
            xpool = Pool(c, [128, 8, 512], F32, 2, f"fx{l}_")
            hpool = Pool(c, [128, 8, 512], BF16, 2, f"fh{l}_")
            gate = self.tab[(l, "gate2")]
            upv = self.ffn_up[l].rearrange("(k p) n -> p k n", p=128)
            dnv = self.ffn_down[l].rearrange("(m p) n -> p m n", p=128)
            for half in range(2):
                for k in range(8):
                    c.dma("pool", w_up[:, k, 0:HH], upv[:, k, half * HH:(half + 1) * HH], w_up, writes=[w_up])
                    c.dma("pool", w_up[:, k, HH:2 * HH], upv[:, k, FH + half * HH:FH + (half + 1) * HH], w_up, writes=[w_up])
                for m in range(NM):
                    c.dma("pool", w_dn[:, m, :], dnv[:, half * NM + m, :], w_dn, writes=[w_dn])
                for b in range(NB):
                    for (seg, i, t0, n) in token_blocks():
                        if seg == "c" and not ctx:
                            continue
                        j = 2 if seg == "c" else b
                        ht = hpool.get()
                        src = self.hT[b].rearrange("(k p) t -> p k t", p=128)
                        c.dma("sp", ht[:, :, :n], src[:, :, t0:t0 + n], ht, reads=[c.dr("hT", b, seg, i)], writes=[ht])
                        xt = xpool.get()
                        xsrc = self.xres[b].rearrange("(k p) t -> p k t", p=128)
                        c.dma("sp", xt[:, :, :n], xsrc[:, :, t0:t0 + n], xt, reads=[c.dr("xres", b, seg, i)], writes=[xt])
                        act = actpool.get()
                        for m in range(NM):
                            pg = self.psA.get()
                            pu = self.psA.get()
                            for k in range(8):
                                c.op("pe", lambda e, pg=pg, k=k, m=m, ht=ht, n=n: e.matmul(
                                    pg[:, :n], w_up[:, k, m * 128:(m + 1) * 128], ht[:, k, :n], start=(k == 0), stop=(k == 7)),
                                     [w_up, ht], [pg])
                            for k in range(8):
                                c.op("pe", lambda e, pu=pu, k=k, m=m, ht=ht, n=n: e.matmul(
                                    pu[:, :n], w_up[:, k, HH + m * 128:HH + (m + 1) * 128], ht[:, k, :n], start=(k == 0), stop=(k == 7)),
                                     [w_up, ht], [pu])
                            sg = sgpool.get()
                            c.op("act", lambda e, sg=sg, pg=pg, n=n: e.activation(out=sg[:, :n], in_=pg[:, :n], func=AF.Silu), [pg], [sg])
                            c.op("dve", lambda e, sg=sg, pu=pu, act=act, m=m, n=n: e.tensor_tensor(
                                act[:, m, :n], sg[:, :n], pu[:, :n], op=ALU.mult), [sg, pu], [act])
                        for k in range(8):
                            po = self.psB.get()
                            for m in range(NM):
                                c.op("pe", lambda e, po=po, k=k, m=m, act=act, n=n: e.matmul(
                                    po[:, :n], w_dn[:, m, k * 128:(k + 1) * 128], act[:, m, :n], start=(m == 0), stop=(m == NM - 1)),
                                     [w_dn, act], [po])
                            c.op("dve", lambda e, po=po, k=k, xt=xt, n=n, j=j: e.scalar_tensor_tensor(
                                out=xt[:, k, :n], in0=po[:, :n], scalar=gate[:, k, j:j + 1], in1=xt[:, k, :n],
                                op0=ALU.mult, op1=ALU.add), [po, xt, gate], [xt])
                        c.dma(self.stq, xsrc[:, :, t0:t0 + n], xt[:, :, :n], xt, reads=[xt], writes=[c.dr("xres", b, seg, i)])

    def phase_outproj(self, l, w_dram, nch, ctx):
        c = self.c
        with c.scope():
            w = c.sb([128, nch, D], BF16, f"wo{l}")
            wv = w_dram.rearrange("(m p) n -> p m n", p=128)
            for m in range(nch):
                c.dma("pool", w[:, m, :], wv[:, m, :], w, writes=[w])
            ypool = Pool(c, [128, nch, 512], BF16, 2, f"oy{l}_")
            xpool = Pool(c, [128, 8, 512], F32, 2, f"ox{l}_")
            gate = self.tab[(l, "gate1")]
            for b in range(NB):
                for (seg, i, t0, n) in token_blocks():
                    if seg == "c" and not ctx:
                        continue
                    j = 2 if seg == "c" else b
                    yt = ypool.get()
                    src = self.ycT[b].rearrange("(k p) t -> p k t", p=128)
                    c.dma("sp", yt[:, :, :n], src[:, 0:nch, t0:t0 + n], yt,
                          reads=[c.dr("ycT", b, seg, i, 0), c.dr("ycT", b, seg, i, 1)], writes=[yt])
                    xt = xpool.get()
                    xsrc = self.xres[b].rearrange("(k p) t -> p k t", p=128)
                    c.dma("sp", xt[:, :, :n], xsrc[:, :, t0:t0 + n], xt, reads=[c.dr("xres", b, seg, i)], writes=[xt])
                    for k in range(8):
                        po = self.psA.get()
                        for m in range(nch):
                            c.op("pe", lambda e, po=po, k=k, m=m, yt=yt, n=n: e.matmul(
                                po[:, :n], w[:, m, k * 128:(k + 1) * 128], yt[:, m, :n], start=(m == 0), stop=(m == nch - 1)),
                                 [w, yt], [po])
                        c.op("dve", lambda e, po=po, k=k, xt=xt, n=n, j=j: e.scalar_tensor_tensor(
                            out=xt[:, k, :n], in0=po[:, :n], scalar=gate[:, k, j:j + 1], in1=xt[:, k, :n],
                            op0=ALU.mult, op1=ALU.add), [po, xt, gate], [xt])
                    c.dma(self.stq, xsrc[:, :, t0:t0 + n], xt[:, :, :n], xt, reads=[xt], writes=[c.dr("xres", b, seg, i)])

    def phase_final(self):
        c = self.c
        with c.scope():
            self.alloc_tok_pools()
            for b in range(NB):
                for (seg, i, t0, n) in token_blocks():
                    if seg == "c":
                        continue
                    xt = self.xpool.get()
                    src = self.xres[b].rearrange("(k p) t -> p k t", p=128)
                    c.dma("sp", xt[:, :, :n], src[:, :, t0:t0 + n], xt, reads=[c.dr("xres", b, seg, i)], writes=[xt])
                    ot = self.xpool.get()
                    self.norm_tile(xt, n, self.gf[:, :], None, lambda k, ot=ot, n=n: (ot[:, k, :n], ot))
                    dst = self.outT[b].rearrange("(k p) t -> p k t", p=128)
                    d = c.dma(self.stq, dst[:, :, i * 512:i * 512 + n], ot[:, :, :n], ot, reads=[ot])
                    self.outs.append(d)
    def declare_mixer_inputs(self):
        I = self.I
        self.mla_w_in = I("mla_w_in", [D, 768])
        self.mla_w_uq = I("mla_w_uq", [384, 16 * 128])
        self.mla_w_uk = I("mla_w_uk", [256, 1024])
        self.mla_w_uv = I("mla_w_uv", [256, 1024])
        self.mla_w_o = I("mla_w_o", [1024, 1024])
        self.mla_gq = I("mla_gq", [128, 3])
        self.mla_gkv = I("mla_gkv", [128, 2])
        self.cs1 = I("cs1", [128, T])
        self.sn1 = I("sn1", [128, T])
        self.Q1T = self.scratch("Q1T", [NB, 16, 96, TT], BF16)
        self.K1T = self.scratch("K1T", [NB, 16, 96, TT], BF16)
        self.V1 = self.scratch("V1", [NB, TT, 16, 128], BF16)
        self.w_ssd = I("w_ssd", [D, 3104])
        self.w_att = I("w_att", [D, 2816])
        self.ab_w_out = I("ab_w_out", [2 * D, D])
        self.convw = I("convw", [128, 16, 3])
        self.convb = I("convb", [128, 16])
        self.dtb_bc = I("dtb_bc", [128, 32])
        self.alog_bc = I("alog_bc", [128, 32])
        self.dsk_bc = I("dsk_bc", [128, 16])
        self.ssdg_bc = I("ssdg_bc", [128, 1024])
        self.masks = I("masks", [4, 128, 128])
        self.blk64 = I("blk64", [128, 128])
        self.gqk = I("gqk", [128, 4])
        self.cs0 = I("cs0", [128, T])
        self.sn0 = I("sn0", [128, T])
        self.Q0T = self.scratch("Q0T", [NB, 16, 64, TT], BF16)
        self.K0T = self.scratch("K0T", [NB, 4, 64, TT], BF16)
        self.V0 = self.scratch("V0", [NB, TT, 4, 128], BF16)
        self.xB = self.scratch("xB", [NB, TT, 1536], BF16)
        self.BCT = self.scratch("BCT", [NB, 8, 128, TT], BF16)
        self.zs = self.scratch("zs", [NB, TT, 1024], F32)
        self.dtr = self.scratch("dtr", [NB, TT, 32], F32)
        self.yf = self.scratch("yf", [NB, TT, 1024], F32)
        if "ybd" in self.dump:
            self.ybd = self.scratch("ybd", [NB, TT, 1024], F32)

    def attention(self, tag, b, H, n_kv, dk, scale, KT_d, QT_d, V_d, row0, qsets):
        c = self.c
        with c.scope():
            KT = c.sb([dk, n_kv, TT], BF16, f"KT{tag}")
            VA = c.sb([128, 18, n_kv, 128], BF16, f"VA{tag}")
            for g in range(n_kv):
                c.dma("sp", KT[:, g, :], KT_d[b, g], KT, writes=[KT])
            vv = V_d[b].rearrange("(tt p) g e -> p tt g e", p=128)
            for tt in range(18):
                c.dma("sp", VA[:, tt, :, :], vv[:, tt, :, :], VA, writes=[VA])
            qpool = Pool(c, [dk, 512], BF16, 4, f"aq{tag}")
            epool = Pool(c, [128, 512], BF16, 8, f"ae{tag}")
            opool = Pool(c, [128, 8, 512], BF16, 2, f"ao{tag}")
            rcp = Pool(c, [128, 512], F32, 2, f"ar{tag}")
            bcp = Pool(c, [64, 512], F32, 2, f"ab{tag}")
            gsz = H // n_kv
            spool = Pool.__new__(Pool)
            spool.bufs = self.psA.bufs + self.psC.bufs
            spool.i = 0
            LA = 3
            for (seg, i, t0, n, nkt) in qsets:
                ot = opool.get()
                items = [(h, kt) for h in range(H) for kt in range(nkt)]
                qts, pos, ets = {}, {}, {}

                def load_q(h):
                    qt = qpool.get()
                    c.dma("sp", qt[:, :n], QT_d[b, h, :, t0:t0 + n], qt, writes=[qt])
                    qts[h] = qt

                def emit_s(h, kt):
                    g = h // gsz
                    if kt == 0:
                        if h == 0:
                            load_q(0)
                        if h + 1 < H:
                            load_q(h + 1)
                    qt = qts[h]
                    pss = spool.get()
                    for _rep in range(self.cfg.get("s_rep", 1)):
                        c.op("pe", lambda e, pss=pss, g=g, kt=kt, qt=qt, n=n: e.matmul(
                            pss[:, :n], KT[:, g, kt * 128:(kt + 1) * 128], qt[:, :n], start=True, stop=True), [KT, qt], [pss])
                    et = epool.get()
                    c.op("act", lambda e, et=et, pss=pss, n=n: e.activation(out=et[:, :n], in_=pss[:, :n], func=AF.Exp, scale=scale),
                         [pss], [et])
                    ets[(h, kt)] = et

                def emit_pv(h, kt):
                    g = h // gsz
                    if kt == 0:
                        pos[h] = self.psB.get()
                    po = pos[h]
                    et = ets.pop((h, kt))
                    c.op("pe", lambda e, po=po, g=g, kt=kt, et=et, n=n, nkt=nkt: e.matmul(
                        po[:, :n], VA[:, kt, g, :], et[:, :n], start=(kt == 0), stop=(kt == nkt - 1)), [VA, et], [po])
                    if kt == nkt - 1:
                        bs = bcp.get()
                        c.op("act", lambda e, bs=bs, po=po, n=n: e.activation(out=bs[0:64, :n], in_=po[64:128, :n], func=AF.Copy), [po], [bs])
                        c.op("dve", lambda e, bs=bs, n=n: e.reciprocal(bs[0:64, :n], bs[0:64, :n]), [bs], [bs])
                        r = (h % 2) * 64
                        c.op("dve", lambda e, r=r, h=h, po=po, bs=bs, n=n, ot=ot: e.tensor_tensor(
                            ot[r:r + 64, h // 2, :n], po[0:64, :n], bs[0:64, :n], op=ALU.mult), [po, bs], [ot])

                G = self.cfg.get("att_group", 3)
                if G == 1:
                    for j in range(len(items) + LA):
                        if j < len(items):
                            emit_s(*items[j])
                        if j >= LA:
                            emit_pv(*items[j - LA])
                else:
                    groups = [items[k:k + G] for k in range(0, len(items), G)]
                    for gi in range(len(groups) + 1):
                        if gi < len(groups):
                            for it in groups[gi]:
                                emit_s(*it)
                        if gi >= 1:
                            for it in groups[gi - 1]:
                                emit_pv(*it)
                dst = self.ycT[b][row0:row0 + 1024, :].rearrange("(k p) t -> p k t", p=128)
                c.dma(self.stq, dst[:, :, t0:t0 + n], ot[:, :, :n], ot, reads=[ot], writes=[c.dr("ycT", b, seg, i, row0 // 1024)])

    def layer1_mixer(self):
        c = self.c
        blks = token_blocks()
        with c.scope():
            w_in = c.sb([128, 8, 768], BF16, "m_w_in")
            w_uq = c.sb([128, 3, 2048], BF16, "m_w_uq")
            w_uk = c.sb([128, 2, 1024], BF16, "m_w_uk")
            w_uv = c.sb([128, 2, 1024], BF16, "m_w_uv")
            gq = c.sb([128, 3], F32, "m_gq")
            gkv = c.sb([128, 2], F32, "m_gkv")
            cs = c.sb([128, T], F32, "m_cs")
            sn = c.sb([128, T], F32, "m_sn")
            wv = self.mla_w_in.rearrange("(k p) n -> p k n", p=128)
            for k in range(8):
                c.dma("pool", w_in[:, k, :], wv[:, k, :], w_in, writes=[w_in])
            for (wt, src) in ((w_uq, self.mla_w_uq), (w_uk, self.mla_w_uk), (w_uv, self.mla_w_uv)):
                sv = src.rearrange("(k p) n -> p k n", p=128)
                for k in range(sv.shape[1]):
                    c.dma("pool", wt[:, k, :], sv[:, k, :], wt, writes=[wt])
            c.dma("sp", gq[:], self.mla_gq[:, :], gq, writes=[gq])
            c.dma("sp", gkv[:], self.mla_gkv[:, :], gkv, writes=[gkv])
            c.dma("sp", cs[:], self.cs1[:, :], cs, writes=[cs])
            c.dma("sp", sn[:], self.sn1[:, :], sn, writes=[sn])
            hTb = c.sb([128, 8, TT], BF16, "m_hT")
            cqn = c.sb([128, 3, TT], BF16, "m_cqn")
            ckvn = c.sb([128, 2, TT], BF16, "m_ckvn")
            kper = c.sb([128, TT], BF16, "m_kper")
            rawp = Pool(c, [128, 3, 512], F32, 2, "m_raw")
            sqp = Pool(c, [128, 3, 512], BF16, 2, "m_sq")
            rsp = Pool(c, [128, 512], F32, 2, "m_rs")
            tmp = Pool(c, [128, 512], F32, 4, "m_tmp")
            qop = Pool(c, [128, 512], BF16, 3, "m_qo")
            khp = Pool(c, [96, TT], BF16, 2, "m_kh")
            vtp = Pool(c, [128, 16, 128], BF16, 3, "m_vt")
            for b in range(NB):
                hv = self.hT[b].rearrange("(k p) t -> p k t", p=128)
                for k in range(8):
                    c.dma("sp", hTb[:, k, :], hv[:, k, :], hTb, writes=[hTb])
                for (seg, i, t0, n) in blks:
                    for (c0, nc_, gt, dst, inv) in ((0, 3, gq, cqn, 1.0 / 384), (3, 2, gkv, ckvn, 1.0 / 256)):
                        raw = rawp.get()
                        sq = sqp.get()
                        for cc in range(nc_):
                            ps = self.psA.get()
                            for k in range(8):
                                c.op("pe", lambda e, ps=ps, k=k, cc=cc, c0=c0, t0=t0, n=n: e.matmul(
                                    ps[:, :n], w_in[:, k, (c0 + cc) * 128:(c0 + cc + 1) * 128], hTb[:, k, t0:t0 + n],
                                    start=(k == 0), stop=(k == 7)), [w_in, hTb], [ps])
                            c.op("act", lambda e, ps=ps, raw=raw, cc=cc, n=n: e.activation(out=raw[:, cc, :n], in_=ps[:, :n], func=AF.Copy),
                                 [ps], [raw])
                            c.op("act", lambda e, ps=ps, sq=sq, cc=cc, n=n: e.activation(out=sq[:, cc, :n], in_=ps[:, :n], func=AF.Square),
                                 [ps], [sq])
                        pss = self.psC.get()
                        for cc in range(nc_):
                            c.op("pe", lambda e, pss=pss, sq=sq, cc=cc, n=n, nc_=nc_: e.matmul(
                                pss[:, :n], self.ones_bf[:, :], sq[:, cc, :n], start=(cc == 0), stop=(cc == nc_ - 1)),
                                 [sq, self.ones_bf], [pss])
                        rs = rsp.get()
                        self.rstd(rs, pss, n, inv)
                        for cc in range(nc_):
                            c.op("dve", lambda e, dst=dst, raw=raw, rs=rs, gt=gt, cc=cc, t0=t0, n=n: e.scalar_tensor_tensor(
                                out=dst[:, cc, t0:t0 + n], in0=raw[:, cc, :n], scalar=gt[:, cc:cc + 1], in1=rs[:, :n],
                                op0=ALU.mult, op1=ALU.mult), [raw, rs, gt], [dst])
                    ps = self.psA.get()
                    for k in range(8):
                        c.op("pe", lambda e, ps=ps, k=k, t0=t0, n=n: e.matmul(
                            ps[:, :n], w_in[:, k, 640:768], hTb[:, k, t0:t0 + n], start=(k == 0), stop=(k == 7)), [w_in, hTb], [ps])
                    if seg == "c":
                        c.op("act", lambda e, ps=ps, t0=t0, n=n: e.activation(out=kper[64:96, t0:t0 + n], in_=ps[64:96, :n], func=AF.Copy),
                             [ps], [kper])
                    else:
                        l0 = t0 - CT
                        self.rope32(ps, kper, t0, n, l0, cs, sn, tmp)
                for h in range(16):
                    for (seg, i, t0, n) in blks:
                        if seg == "c":
                            continue
                        ps = self.psA.get()
                        for kk in range(3):
                            c.op("pe", lambda e, ps=ps, kk=kk, h=h, t0=t0, n=n: e.matmul(
                                ps[:, :n], w_uq[:, kk, h * 128:(h + 1) * 128], cqn[:, kk, t0:t0 + n], start=(kk == 0), stop=(kk == 2)),
                                 [w_uq, cqn], [ps])
                        qo = qop.get()
                        c.op("act", lambda e, ps=ps, qo=qo, n=n: e.activation(out=qo[0:64, :n], in_=ps[0:64, :n], func=AF.Copy), [ps], [qo])
                        self.rope32(ps, qo, 0, n, t0 - CT, cs, sn, tmp)
                        c.dma("sp", self.Q1T[b, h, :, t0:t0 + n], qo[0:96, :n], qo, reads=[qo])
                    kh = khp.get()
                    for (seg, i, t0, n) in blks:
                        ps = self.psA.get()
                        for kk in range(2):
                            c.op("pe", lambda e, ps=ps, kk=kk, h=h, t0=t0, n=n: e.matmul(
                                ps[0:64, :n], w_uk[:, kk, h * 64:(h + 1) * 64], ckvn[:, kk, t0:t0 + n], start=(kk == 0), stop=(kk == 1)),
                                 [w_uk, ckvn], [ps])
                        c.op("act", lambda e, ps=ps, kh=kh, t0=t0, n=n: e.activation(out=kh[0:64, t0:t0 + n], in_=ps[0:64, :n], func=AF.Copy),
                             [ps], [kh])
                    c.op("dve", lambda e, kh=kh: e.tensor_copy(kh[64:96, :], kper[64:96, :]), [kper], [kh])
                    c.dma("sp", self.K1T[b, h], kh[:, :], kh, reads=[kh])
                for tt in range(18):
                    vt = vtp.get()
                    c.op("pool", lambda e, vt=vt: e.memset(vt[:], 1.0), [], [vt])
                    for hf in range(2):
                        ps = self.psA.get()
                        for kk in range(2):
                            c.op("pe", lambda e, ps=ps, kk=kk, hf=hf, tt=tt: e.matmul(
                                ps[:, :], ckvn[:, kk, tt * 128:(tt + 1) * 128], w_uv[:, kk, hf * 512:(hf + 1) * 512],
                                start=(kk == 0), stop=(kk == 1)), [ckvn, w_uv], [ps])
                        c.op("act", lambda e, ps=ps, vt=vt, hf=hf: e.activation(
                            out=vt[:, hf * 8:(hf + 1) * 8, 0:64], in_=ps[:, :].rearrange("p (h d) -> p h d", d=64), func=AF.Copy),
                             [ps], [vt])
                    c.dma("sp", self.V1[b, tt * 128:(tt + 1) * 128], vt[:], vt, reads=[vt])
        for b in range(NB):
            qsets = [(seg, i, t0, n, 18) for (seg, i, t0, n) in blks if seg == "l"]
            self.attention(f"m{b}", b, 16, 16, 96, 96 ** -0.5, self.K1T, self.Q1T, self.V1, 0, qsets)

    def rope32(self, ps, dst, d0, n, l0, cs, sn, tmp):
        c = self.c
        t1 = tmp.get()
        t2 = tmp.get()
        c.op("act", lambda e: e.activation(out=t1[64:96, :n], in_=ps[96:128, :n], func=AF.Copy), [ps], [t1])
        c.op("dve", lambda e: e.tensor_tensor(t1[64:96, :n], t1[64:96, :n], sn[64:96, l0:l0 + n], op=ALU.mult), [t1, sn], [t1])
        c.op("dve", lambda e: e.tensor_tensor(t2[64:96, :n], ps[64:96, :n], cs[64:96, l0:l0 + n], op=ALU.mult), [ps, cs], [t2])
        c.op("dve", lambda e: e.tensor_tensor(dst[64:96, d0:d0 + n], t1[64:96, :n], t2[64:96, :n], op=ALU.add), [t1, t2], [dst])
    def layer0_mixer(self):
        self.l0_proj_ssd()
        self.l0_proj_att()
        self.l0_ssd_scan()
        blks = token_blocks()
        for b in range(NB):
            qsets = [(seg, i, t0, n, 2 if seg == "c" else 18) for (seg, i, t0, n) in blks]
            self.attention(f"a{b}", b, 16, 4, 64, 0.125, self.K0T, self.Q0T, self.V0, 1024, qsets)

    def load_hTb(self, hTb, b):
        c = self.c
        hv = self.hT[b].rearrange("(k p) t -> p k t", p=128)
        for k in range(8):
            c.dma("sp", hTb[:, k, :], hv[:, k, :], hTb, writes=[hTb])

    def l0_proj_ssd(self):
        c = self.c
        blks = token_blocks()
        with c.scope():
            w = c.sb([128, 8, 3104], BF16, "s_w")
            wv = self.w_ssd.rearrange("(k p) n -> p k n", p=128)
            for k in range(8):
                c.dma("pool", w[:, k, :], wv[:, k, :], w, writes=[w])
            cw = c.sb([128, 16, 3], F32, "s_cw")
            cb = c.sb([128, 16], F32, "s_cb")
            c.dma("sp", cw[:], self.convw[:, :, :], cw, writes=[cw])
            c.dma("sp", cb[:], self.convb[:, :], cb, writes=[cb])
            hTb = c.sb([128, 8, TT], BF16, "s_hT")
            rawp = Pool(c, [128, TT], F32, 2, "s_raw")
            yp = Pool(c, [128, TT], F32, 2, "s_y")
            ysp = Pool(c, [128, TT], BF16, 2, "s_ys")
            xBt = c.sb([128, 18, 1536], BF16, "s_xBt")
            ztp = Pool(c, [128, 1024], F32, 2, "s_zt")
            dta = c.sb([128, 18, 32], F32, "s_dta")
            for b in range(NB):
                self.load_hTb(hTb, b)
                for cc in range(16):
                    raw = rawp.get()
                    for (seg, i, t0, n) in blks:
                        ps = self.psA.get()
                        for k in range(8):
                            c.op("pe", lambda e, ps=ps, k=k, cc=cc, t0=t0, n=n: e.matmul(
                                ps[:, :n], w[:, k, cc * 128:(cc + 1) * 128], hTb[:, k, t0:t0 + n], start=(k == 0), stop=(k == 7)),
                                 [w, hTb], [ps])
                        c.op("act", lambda e, ps=ps, raw=raw, t0=t0, n=n: e.activation(out=raw[:, t0:t0 + n], in_=ps[:, :n], func=AF.Copy),
                             [ps], [raw])
                    y = yp.get()
                    c.op("act", lambda e, y=y, raw=raw, cc=cc: e.activation(out=y[:, :], in_=raw[:, :], func=AF.Identity,
                                                                           scale=cw[:, cc, 1:2], bias=cb[:, cc:cc + 1]), [raw, cw, cb], [y])
                    for (s0, s1) in ((0, CT), (CT, TT)):
                        c.op("dve", lambda e, y=y, raw=raw, cc=cc, s0=s0, s1=s1: e.scalar_tensor_tensor(
                            out=y[:, s0 + 1:s1], in0=raw[:, s0:s1 - 1], scalar=cw[:, cc, 0:1], in1=y[:, s0 + 1:s1],
                            op0=ALU.mult, op1=ALU.add), [raw, y, cw], [y])
                        c.op("dve", lambda e, y=y, raw=raw, cc=cc, s0=s0, s1=s1: e.scalar_tensor_tensor(
                            out=y[:, s0:s1 - 1], in0=raw[:, s0 + 1:s1], scalar=cw[:, cc, 2:3], in1=y[:, s0:s1 - 1],
                            op0=ALU.mult, op1=ALU.add), [raw, y, cw], [y])
                    ys = ysp.get()
                    c.op("act", lambda e, ys=ys, y=y: e.activation(out=ys[:, :], in_=y[:, :], func=AF.Silu), [y], [ys])
                    if cc < 12:
                        for t4 in range(0, 18, 4):
                            nt = min(4, 18 - t4)
                            ps = self.psA.get()
                            for q in range(nt):
                                tt = t4 + q
                                c.op("pe", lambda e, ps=ps, ys=ys, tt=tt, q=q: e.matmul(
                                    ps[:, q * 128:(q + 1) * 128], ys[:, tt * 128:(tt + 1) * 128], self.ident_bf[:, :], start=True, stop=True),
                                     [ys, self.ident_bf], [ps])
                            c.op("dve", lambda e, ps=ps, t4=t4, nt=nt, cc=cc: e.tensor_copy(
                                xBt[:, t4:t4 + nt, cc * 128:(cc + 1) * 128], ps[:, 0:nt * 128].rearrange("p (a b) -> p a b", b=128)),
                                 [ps], [xBt])
                    if cc >= 8:
                        c.dma("sp", self.BCT[b, cc - 8], ys[:, :], ys, reads=[ys])
                c.dma("sp", self.xB[b].rearrange("(tt p) f -> p tt f", p=128), xBt[:], xBt, reads=[xBt])
                for tt in range(18):
                    zt = ztp.get()
                    for hf in range(2):
                        ps = self.psA.get()
                        for k in range(8):
                            c.op("pe", lambda e, ps=ps, k=k, tt=tt, hf=hf: e.matmul(
                                ps[:, :], hTb[:, k, tt * 128:(tt + 1) * 128], w[:, k, 2048 + hf * 512:2048 + (hf + 1) * 512],
                                start=(k == 0), stop=(k == 7)), [w, hTb], [ps])
                        c.op("act", lambda e, ps=ps, zt=zt, hf=hf: e.activation(out=zt[:, hf * 512:(hf + 1) * 512], in_=ps[:, :], func=AF.Silu),
                             [ps], [zt])
                    c.dma("sp", self.zs[b, tt * 128:(tt + 1) * 128, :], zt[:], zt, reads=[zt])
                    ps = self.psA.get()
                    for k in range(8):
                        c.op("pe", lambda e, ps=ps, k=k, tt=tt: e.matmul(
                            ps[:, 0:32], hTb[:, k, tt * 128:(tt + 1) * 128], w[:, k, 3072:3104], start=(k == 0), stop=(k == 7)),
                             [w, hTb], [ps])
                    c.op("dve", lambda e, ps=ps, tt=tt: e.tensor_copy(dta[:, tt, :], ps[:, 0:32]), [ps], [dta])
                c.dma("sp", self.dtr[b].rearrange("(tt p) f -> p tt f", p=128), dta[:], dta, reads=[dta])

    def l0_proj_att(self):
        c = self.c
        blks = token_blocks()
        with c.scope():
            w = c.sb([128, 8, 2816], BF16, "a_w")
            wv = self.w_att.rearrange("(k p) n -> p k n", p=128)
            for k in range(8):
                c.dma("pool", w[:, k, :], wv[:, k, :], w, writes=[w])
            hTb = c.sb([128, 8, TT], BF16, "a_hT")
            cs = c.sb([128, T], F32, "a_cs")
            sn = c.sb([128, T], F32, "a_sn")
            gqk = c.sb([128, 4], F32, "a_gqk")
            blk = c.sb([128, 128], F32, "a_blk")
            blkb = c.sb([128, 128], BF16, "a_blkb")
            c.dma("sp", cs[:], self.cs0[:, :], cs, writes=[cs])
            c.dma("sp", sn[:], self.sn0[:, :], sn, writes=[sn])
            c.dma("sp", gqk[:], self.gqk[:, :], gqk, writes=[gqk])
            c.dma("sp", blk[:], self.blk64[:, :], blk, writes=[blk])
            c.op("dve", lambda e: e.tensor_copy(blkb[:], blk[:]), [blk], [blkb])
            tabs = []
            for j, src in enumerate((cs, sn, cs, sn)):
                t = c.sb([128, T], F32, f"a_tab{j}")
                c.op("pool", lambda e, t=t, src=src, j=j: e.tensor_scalar(t[:], src[:], gqk[:, j:j + 1], None, op0=ALU.mult), [src, gqk], [t])
                tabs.append(t)
            sqp = Pool(c, [128, 512], BF16, 2, "a_sq")
            rsp = Pool(c, [128, 512], F32, 2, "a_rs")
            tp = Pool(c, [128, 512], F32, 4, "a_t")
            op_ = Pool(c, [128, 512], BF16, 3, "a_o")
            vtp = Pool(c, [128, 4, 128], BF16, 3, "a_vt")
            for b in range(NB):
                self.load_hTb(hTb, b)
                for cc in range(10):
                    isq = cc < 8
                    c0 = cc * 128 if isq else 2048 + (cc - 8) * 128
                    r0 = c0 + (1024 if isq else 256)
                    gc, gs = (tabs[0], tabs[1]) if isq else (tabs[2], tabs[3])
                    gcol = 0 if isq else 2
                    for (seg, i, t0, n) in blks:
                        psq = self.psA.get()
                        for k in range(8):
                            c.op("pe", lambda e, ps=psq, k=k, c0=c0, t0=t0, n=n: e.matmul(
                                ps[:, :n], w[:, k, c0:c0 + 128], hTb[:, k, t0:t0 + n], start=(k == 0), stop=(k == 7)), [w, hTb], [psq])
                        sq = sqp.get()
                        c.op("act", lambda e, sq=sq, ps=psq, n=n: e.activation(out=sq[:, :n], in_=ps[:, :n], func=AF.Square), [psq], [sq])
                        pss = self.psC.get()
                        c.op("pe", lambda e, pss=pss, sq=sq, n=n: e.matmul(pss[:, :n], blkb[:, :], sq[:, :n], start=True, stop=True),
                             [blkb, sq], [pss])
                        rs = rsp.get()
                        self.rstd(rs, pss, n, 1.0 / 64)
                        o = op_.get()
                        if seg == "c":
                            c.op("dve", lambda e, o=o, ps=psq, rs=rs, gcol=gcol, n=n: e.scalar_tensor_tensor(
                                out=o[:, :n], in0=ps[:, :n], scalar=gqk[:, gcol:gcol + 1], in1=rs[:, :n], op0=ALU.mult, op1=ALU.mult),
                                 [psq, rs, gqk], [o])
                        else:
                            l0 = t0 - CT
                            psr = self.psA.get()
                            for k in range(8):
                                c.op("pe", lambda e, ps=psr, k=k, r0=r0, t0=t0, n=n: e.matmul(
                                    ps[:, :n], w[:, k, r0:r0 + 128], hTb[:, k, t0:t0 + n], start=(k == 0), stop=(k == 7)), [w, hTb], [psr])
                            t1 = tp.get()
                            t2 = tp.get()
                            c.op("dve", lambda e, t1=t1, ps=psq, gc=gc, l0=l0, n=n: e.tensor_tensor(t1[:, :n], ps[:, :n], gc[:, l0:l0 + n], op=ALU.mult),
                                 [psq, gc], [t1])
                            c.op("dve", lambda e, t2=t2, ps=psr, gs=gs, l0=l0, n=n: e.tensor_tensor(t2[:, :n], ps[:, :n], gs[:, l0:l0 + n], op=ALU.mult),
                                 [psr, gs], [t2])
                            c.op("pool", lambda e, t1=t1, t2=t2, n=n: e.tensor_tensor(t1[:, :n], t1[:, :n], t2[:, :n], op=ALU.add), [t1, t2], [t1])
                            c.op("dve", lambda e, o=o, t1=t1, rs=rs, n=n: e.tensor_tensor(o[:, :n], t1[:, :n], rs[:, :n], op=ALU.mult), [t1, rs], [o])
                        for hh in range(2):
                            if isq:
                                dst = self.Q0T[b, 2 * cc + hh, :, t0:t0 + n]
                            else:
                                dst = self.K0T[b, 2 * (cc - 8) + hh, :, t0:t0 + n]
                            c.dma("sp", dst, o[hh * 64:(hh + 1) * 64, :n], o, reads=[o])
                for tt in range(18):
                    vt = vtp.get()
                    c.op("pool", lambda e, vt=vt: e.memset(vt[:], 1.0), [], [vt])
                    ps = self.psA.get()
                    for k in range(8):
                        c.op("pe", lambda e, ps=ps, k=k, tt=tt: e.matmul(
                            ps[:, 0:256], hTb[:, k, tt * 128:(tt + 1) * 128], w[:, k, 2560:2816], start=(k == 0), stop=(k == 7)),
                             [w, hTb], [ps])
                    c.op("act", lambda e, ps=ps, vt=vt: e.activation(out=vt[:, :, 0:64], in_=ps[:, 0:256].rearrange("p (h d) -> p h d", d=64),
                                                                     func=AF.Copy), [ps], [vt])
                    c.dma("sp", self.V0[b, tt * 128:(tt + 1) * 128], vt[:], vt, reads=[vt])

    def l0_ssd_scan(self):
        c = self.c
        with c.scope():
            S = type("S", (), {})()
            mk = c.sb([128, 4, 128], F32, "d_masks")
            c.dma("sp", mk[:], self.masks.rearrange("m p f -> p m f"), mk, writes=[mk])
            S.mk = mk
            dtb = c.sb([128, 32], F32, "d_dtb")
            alog = c.sb([128, 32], F32, "d_alog")
            S.abc = c.sb([128, 32], F32, "d_abc")
            S.dsk = c.sb([128, 16], F32, "d_dsk")
            S.ssdg = c.sb([128, 1024], F32, "d_ssdg")
            S.onec = c.sb([128, 1], F32, "d_onec")
            c.op("dve", lambda e: e.memset(S.onec[:], 1.0), [], [S.onec])
            c.dma("sp", dtb[:], self.dtb_bc[:, :], dtb, writes=[dtb])
            c.dma("sp", alog[:], self.alog_bc[:, :], alog, writes=[alog])
            c.dma("sp", S.dsk[:], self.dsk_bc[:, :], S.dsk, writes=[S.dsk])
            c.dma("sp", S.ssdg[:], self.ssdg_bc[:, :], S.ssdg, writes=[S.ssdg])
            c.op("act", lambda e: e.activation(out=S.abc[:], in_=alog[:], func=AF.Exp), [alog], [S.abc])
            c.op("dve", lambda e: e.tensor_scalar(S.abc[:], S.abc[:], -1.0, None, op0=ALU.mult), [S.abc], [S.abc])
            S.dtb = dtb
            S.hsts = [c.sb([128, 1024], F32, f"d_hst{i}") for i in range(NB)]
            S.hbfs = [c.sb([128, 1024], BF16, f"d_hbf{i}") for i in range(NB)]
            S.xbp = Pool(c, [128, 1536], BF16, 5, "d_xb")
            S.bcp = Pool(c, [128, 8, 128], BF16, 5, "d_bc")
            S.dtp = Pool(c, [128, 32], F32, 5, "d_dt")
            S.smp = Pool(c, [128, 32], F32, 32, "d_sm")
            S.xdtp = Pool(c, [128, 16, 64], BF16, 6, "d_xdt")
            S.yop = Pool(c, [128, 1024], F32, 3, "d_yo")
            S.cbp = Pool(c, [128, 128], F32, 5, "d_cb")
            S.rgp = Pool(c, [128, 4, 128], F32, 5, "d_rg")
            S.ep = Pool(c, [128, 512], F32, 5, "d_e")
            S.mtp = Pool(c, [128, 4, 128], BF16, 5, "d_mt")
            S.ydp = Pool(c, [128, 1024], F32, 3, "d_yd")
            S.zp = Pool(c, [128, 1024], F32, 5, "d_z")
            S.yfp = Pool(c, [128, 1024], F32, 5, "d_yfl")
            S.ynp = Pool(c, [128, 1024], BF16, 2, "d_yn")
            S.ytp = Pool(c, [128, 8, 128], BF16, 2, "d_yt")
            S.pX = self.psA
            S.pY = self.psB
            S.pZ = self.psC
            for dr_ in range(2):
                for b in range(NB):
                    c.op("dve", lambda e, b=b: e.memset(S.hsts[b][:], 0.0), [], [S.hsts[b]])
                    c.op("dve", lambda e, b=b: e.memset(S.hbfs[b][:], 0.0), [], [S.hbfs[b]])
                order = list(range(18)) if dr_ == 0 else [1, 0] + list(range(17, 1, -1))
                steps = [(b, ch) for ch in order for b in range(NB)]
                PF = 2
                loads = {}
                for j in range(len(steps) + PF):
                    if j < len(steps):
                        loads[j] = self.ssd_loads(steps[j][0], steps[j][1], dr_, S)
                    if j >= PF:
                        bb, ch = steps[j - PF]
                        self.ssd_chunk(bb, ch, dr_, S, loads.pop(j - PF))

    def ssd_loads(self, b, ch, dr_, S):
        c = self.c
        tok = slice(ch * 128, (ch + 1) * 128)
        xBt = S.xbp.get()
        c.dma("sp", xBt[:], self.xB[b, tok, :], xBt, writes=[xBt])
        BCt = S.bcp.get()
        c.dma("sp", BCt[:], self.BCT[b].rearrange("g p t -> p g t")[:, :, tok], BCt, writes=[BCt])
        dtt = S.dtp.get()
        c.dma("sp", dtt[:], self.dtr[b, tok, :], dtt, writes=[dtt])
        yfl = zt = None
        if dr_ == 1:
            yfl = S.yfp.get()
            c.dma("sp", yfl[:], self.yf[b, tok, :], yfl, reads=[c.dr("yf", b, ch)], writes=[yfl])
            zt = S.zp.get()
            c.dma("sp", zt[:], self.zs[b, tok, :], zt, writes=[zt])
        return xBt, BCt, dtt, yfl, zt

    def ssd_chunk(self, b, ch, dr_, S, L):
        c = self.c
        hst = S.hsts[b]
        hbf = S.hbfs[b]
        tok = slice(ch * 128, (ch + 1) * 128)
        cols = slice(dr_ * 16, dr_ * 16 + 16)
        Tm = S.mk[:, dr_, :]
        U = S.mk[:, 2 + dr_, :]
        xBt, BCt, dtt, yfl, zt = L
        xs, ax, dt, dA, sm1, sm2, dtd = (S.smp.get() for _ in range(7))
        c.op("dve", lambda e: e.tensor_tensor(xs[:], dtt[:], S.dtb[:], op=ALU.add), [dtt, S.dtb], [xs])
        c.op("act", lambda e: e.activation(out=ax[:], in_=xs[:], func=AF.Abs), [xs], [ax])
        c.op("act", lambda e: e.activation(out=ax[:], in_=ax[:], func=AF.Exp, scale=-1.0), [ax], [ax])
        c.op("act", lambda e: e.activation(out=ax[:], in_=ax[:], func=AF.Ln, bias=S.onec[:, :]), [ax, S.onec], [ax])
        c.op("dve", lambda e: e.scalar_tensor_tensor(out=dt[:], in0=xs[:], scalar=0.0, in1=ax[:], op0=ALU.max, op1=ALU.add), [xs, ax], [dt])
        c.op("dve", lambda e: e.tensor_tensor(dA[:], dt[:], S.abc[:], op=ALU.mult), [dt, S.abc], [dA])
        pc = S.pX.get()
        c.op("pe", lambda e: e.matmul(pc[:, 0:16], Tm, dA[:, cols], start=True, stop=True), [S.mk, dA], [pc])
        c.op("pe", lambda e: e.matmul(pc[:, 16:32], self.ones_f[:, :], dA[:, cols], start=True, stop=True), [self.ones_f, dA], [pc])
        c.op("act", lambda e: e.activation(out=sm1[:], in_=pc[:, 0:32], func=AF.Copy), [pc], [sm1])
        c.op("act", lambda e: e.activation(out=sm2[:], in_=pc[:, 0:32], func=AF.Exp), [pc], [sm2])
        c.op("dve", lambda e: e.tensor_tensor(dtd[:, 0:16], sm1[:, 16:32], sm1[:, 0:16], op=ALU.subtract), [sm1], [dtd])
        c.op("act", lambda e: e.activation(out=dtd[:, 0:16], in_=dtd[:, 0:16], func=AF.Exp), [dtd], [dtd])
        c.op("dve", lambda e: e.tensor_tensor(dtd[:, 0:16], dtd[:, 0:16], dt[:, cols], op=ALU.mult), [dtd, dt], [dtd])
        xdt = S.xdtp.get()
        xdtd = S.xdtp.get()
        xv = xBt[:, 0:1024].rearrange("p (h d) -> p h d", d=64)
        c.op("pool", lambda e: e.tensor_tensor(xdt[:], xv, dt[:, cols].unsqueeze(2).to_broadcast([128, 16, 64]), op=ALU.mult),
             [xBt, dt], [xdt])
        c.op("pool", lambda e: e.tensor_tensor(xdtd[:], xv, dtd[:, 0:16].unsqueeze(2).to_broadcast([128, 16, 64]), op=ALU.mult),
             [xBt, dtd], [xdtd])
        pyo = [S.pZ.get(), S.pZ.get()]
        for g in range(4):
            c.op("pe", lambda e, g=g: e.matmul(pyo[g // 2][:, (g % 2) * 256:(g % 2 + 1) * 256], BCt[:, 4 + g, :],
                                               hbf[:, g * 256:(g + 1) * 256], start=True, stop=True), [BCt, hbf], [pyo[g // 2]])
        yo = S.yop.get()
        for hf in range(2):
            c.op("dve", lambda e, hf=hf: e.tensor_tensor(
                yo[:, hf * 512:(hf + 1) * 512].rearrange("p (h d) -> p h d", d=64),
                pyo[hf][:, :].rearrange("p (h d) -> p h d", d=64),
                sm2[:, hf * 8:(hf + 1) * 8].unsqueeze(2).to_broadcast([128, 8, 64]), op=ALU.mult), [pyo[hf], sm2], [yo])
        pyd = [S.pY.get(), S.pY.get()]
        cbms, rgs, psegs, Es, MTs = [], [], [], [], []
        for g in range(4):
            pcb = S.pX.get()
            c.op("pe", lambda e, pcb=pcb, g=g: e.matmul(pcb[:, 0:128], BCt[:, g, :], BCt[:, 4 + g, :], start=True, stop=True), [BCt], [pcb])
            cbm = S.cbp.get()
            c.op("dve", lambda e, cbm=cbm, pcb=pcb: e.tensor_tensor(cbm[:], pcb[:, 0:128], Tm, op=ALU.mult), [pcb, S.mk], [cbm])
            cbms.append(cbm)
            rg = S.rgp.get()
            c.op("pool", lambda e, rg=rg, g=g: e.tensor_tensor(
                rg[:], Tm.unsqueeze(1).to_broadcast([128, 4, 128]),
                dA[:, dr_ * 16 + 4 * g:dr_ * 16 + 4 * g + 4].unsqueeze(2).to_broadcast([128, 4, 128]), op=ALU.mult), [S.mk, dA], [rg])
            rgs.append(rg)
        for g in range(4):
            pseg = S.pX.get()
            rg = rgs[g]
            c.op("pe", lambda e, pseg=pseg, rg=rg: e.matmul(pseg[:, :], U, rg[:].rearrange("p a b -> p (a b)"), start=True, stop=True),
                 [S.mk, rg], [pseg])
            E = S.ep.get()
            c.op("act", lambda e, E=E, pseg=pseg: e.activation(out=E[:], in_=pseg[:, :], func=AF.Exp), [pseg], [E])
            Es.append(E)
        for g in range(4):
            MT = S.mtp.get()
            E, cbm = Es[g], cbms[g]
            c.op("dve", lambda e, MT=MT, E=E, cbm=cbm: e.tensor_tensor(
                MT[:], E[:].rearrange("p (a b) -> p a b", b=128), cbm[:].unsqueeze(1).to_broadcast([128, 4, 128]), op=ALU.mult),
                 [E, cbm], [MT])
            MTs.append(MT)
        for g in range(4):
            MT = MTs[g]
            for hh in range(4):
                h = 4 * g + hh
                c.op("pe", lambda e, MT=MT, hh=hh, h=h: e.matmul(pyd[h // 8][:, (h % 8) * 64:(h % 8 + 1) * 64], MT[:, hh, :], xdt[:, h, :],
                                                             start=True, stop=True), [MT, xdt], [pyd[h // 8]])
        yd = S.ydp.get()
        for hf in range(2):
            c.op("dve", lambda e, hf=hf: e.tensor_tensor(yd[:, hf * 512:(hf + 1) * 512], pyd[hf][:, :], yo[:, hf * 512:(hf + 1) * 512], op=ALU.add),
                 [pyd[hf], yo], [yd])
        pst = [S.pZ.get(), S.pZ.get()]
        for g in range(4):
            c.op("pe", lambda e, g=g: e.matmul(pst[g // 2][:, (g % 2) * 256:(g % 2 + 1) * 256], xBt[:, 1024 + g * 128:1024 + (g + 1) * 128],
                                               xdtd[:, 4 * g:4 * g + 4, :].rearrange("p a b -> p (a b)"), start=True, stop=True),
                 [xBt, xdtd], [pst[g // 2]])
        c.op("dve", lambda e: e.tensor_tensor(hst[:].rearrange("p (h d) -> p h d", d=64), hst[:].rearrange("p (h d) -> p h d", d=64),
                                              sm2[:, 16:32].unsqueeze(2).to_broadcast([128, 16, 64]), op=ALU.mult), [hst, sm2], [hst])
        for hf in range(2):
            c.op("dve", lambda e, hf=hf: e.tensor_tensor(hst[:, hf * 512:(hf + 1) * 512], hst[:, hf * 512:(hf + 1) * 512], pst[hf][:, :],
                                                        op=ALU.add), [hst, pst[hf]], [hst])
        c.op("act", lambda e: e.activation(out=hbf[:], in_=hst[:], func=AF.Copy), [hst], [hbf])
        if dr_ == 0:
            c.dma("sp", self.yf[b, tok, :], yd[:], yd, reads=[yd], writes=[c.dr("yf", b, ch)])
            return
        if "ybd" in self.dump:
            c.dma("sp", self.ybd[b, tok, :], yd[:], yd, reads=[yd])
        c.op("dve", lambda e: e.tensor_tensor(yd[:], yd[:], yfl[:], op=ALU.add), [yd, yfl], [yd])
        c.op("pool", lambda e: e.tensor_tensor(yfl[:].rearrange("p (h d) -> p h d", d=64), xv,
                                               S.dsk[:, :].unsqueeze(2).to_broadcast([128, 16, 64]), op=ALU.mult), [xBt, S.dsk], [yfl])
        c.op("dve", lambda e: e.tensor_tensor(yd[:], yd[:], yfl[:], op=ALU.add), [yd, yfl], [yd])
        c.op("dve", lambda e: e.tensor_tensor(yd[:], yd[:], zt[:], op=ALU.mult), [yd, zt], [yd])
        c.op("pool", lambda e: e.tensor_tensor(zt[:], yd[:], yd[:], op=ALU.mult), [yd], [zt])
        ss = S.smp.get()
        c.op("dve", lambda e: e.reduce_sum(ss[:, 0:1], zt[:], axis=mybir.AxisListType.X), [zt], [ss])
        c.op("act", lambda e: e.activation(out=ss[:, 0:1], in_=ss[:, 0:1], func=AF.Sqrt, scale=1.0 / 1024, bias=self.epsc[:, :]),
             [ss, self.epsc], [ss])
        c.op("dve", lambda e: e.reciprocal(ss[:, 0:1], ss[:, 0:1]), [ss], [ss])
        yn = S.ynp.get()
        c.op("dve", lambda e: e.scalar_tensor_tensor(out=yn[:], in0=yd[:], scalar=ss[:, 0:1], in1=S.ssdg[:], op0=ALU.mult, op1=ALU.mult),
             [yd, ss, S.ssdg], [yn], strict=True)
        yt = S.ytp.get()
        for q4 in range(2):
            ps = S.pX.get()
            for q in range(4):
                cc = q4 * 4 + q
                c.op("pe", lambda e, ps=ps, q=q, cc=cc: e.matmul(ps[:, q * 128:(q + 1) * 128], yn[:, cc * 128:(cc + 1) * 128], self.ident_bf[:, :],
                                                                start=True, stop=True), [yn, self.ident_bf], [ps])
            c.op("act", lambda e, ps=ps, q4=q4: e.activation(out=yt[:, q4 * 4:(q4 + 1) * 4, :], in_=ps[:, :].rearrange("p (a b) -> p a b", b=128),
                                                            func=AF.Copy), [ps], [yt])
        seg, i = ("c", 0) if ch < 2 else ("l", (ch - 2) // 4)
        dst = self.ycT[b][0:1024, :].rearrange("(k p) t -> p k t", p=128)
        c.dma("sp", dst[:, :, tok], yt[:], yt, reads=[yt], writes=[c.dr("ycT", b, "ssd", ch)])


def fm(v):
    v = np.asarray(v)
    n = v.shape[-1] // 128
    return np.ascontiguousarray(np.swapaxes(v.reshape(v.shape[:-1] + (n, 128)), -1, -2))


def rope_tables(rot_dim):
    n_freq = rot_dim // 4
    rows = T // 64
    row = np.repeat(np.arange(rows, dtype=np.float32), 64)
    col = np.tile(np.arange(64, dtype=np.float32), rows)
    inv = (np.float32(10000.0) ** (-np.arange(n_freq, dtype=np.float32) / np.float32(n_freq))).astype(np.float32)
    ang = np.concatenate([row[:, None] * inv, col[:, None] * inv], -1).astype(np.float32)
    return np.cos(ang).astype(np.float32), np.sin(ang).astype(np.float32)


def make_shared(inp):
    f32 = np.float32
    ca = lambda a: np.ascontiguousarray(a, dtype=f32)
    sh = dict(
        ada_w=ca(inp["ada_w"]),
        ada_b3=ca(np.repeat(inp["ada_b"][:, None, :], 3, axis=1)),
        g1T=fm(inp["norm1_g"]), g2T=fm(inp["norm2_g"]), gfT=fm(inp["final_norm_g"]),
        ffn_up=ca(inp["ffn_w_up"]), ffn_down=ca(inp["ffn_w_down"]),
        ident=np.eye(128, dtype=f32),
    )
    p16 = (np.arange(32) + 16) % 32
    w = inp["mla_w_in"][0]
    sh["mla_w_in"] = ca(np.concatenate([w[:, :640], np.zeros((D, 64), f32), w[:, 640:672], w[:, 640:672][:, p16]], 1))
    wq = inp["mla_w_uq"][0].reshape(384, 16, 96)
    sh["mla_w_uq"] = ca(np.concatenate([wq[:, :, :64], wq[:, :, 64:], wq[:, :, 64:][:, :, p16]], 2).reshape(384, 2048))
    wkv = inp["mla_w_ukv"][0].reshape(256, 16, 128)
    sh["mla_w_uk"] = ca(wkv[:, :, :64].reshape(256, 1024))
    sh["mla_w_uv"] = ca(wkv[:, :, 64:].reshape(256, 1024))
    sh["mla_w_o"] = ca(inp["mla_w_o"][0])
    sh["mla_gq"] = fm(inp["mla_q_norm_g"][0])
    sh["mla_gkv"] = fm(inp["mla_kv_norm_g"][0])
    c1, s1 = rope_tables(32)
    cs1 = np.zeros((128, T), f32)
    sn1 = np.zeros((128, T), f32)
    for d in range(32):
        m = d % 16
        cs1[64 + d] = c1[:, m]
        sn1[64 + d] = -s1[:, m] if d < 16 else s1[:, m]
    sh["cs1"], sh["sn1"] = cs1, sn1
    w = inp["ab_w_in"][0]
    sh["w_ssd"] = ca(np.concatenate([w[:, 1024:3072], w[:, 0:1024], w[:, 3072:3104]], 1))
    p32 = (np.arange(64) + 32) % 64
    q = w[:, 3104:4128].reshape(D, 16, 64)
    k = w[:, 4128:4384].reshape(D, 4, 64)
    sh["w_att"] = ca(np.concatenate([q.reshape(D, 1024), q[:, :, p32].reshape(D, 1024), k.reshape(D, 256),
                                     k[:, :, p32].reshape(D, 256), w[:, 4384:4640]], 1))
    sh["ab_w_out"] = ca(inp["ab_w_out"][0])
    cw = inp["ssd_conv_w"][0]
    sh["convw"] = ca(cw.T.reshape(16, 128, 3).transpose(1, 0, 2))
    sh["convb"] = fm(inp["ssd_conv_b"][0])
    sh["dtb_bc"] = ca(np.tile(inp["ssd_dt_bias"][0].reshape(1, 32), (128, 1)))
    sh["alog_bc"] = ca(np.tile(inp["ssd_a_log"][0].reshape(1, 32), (128, 1)))
    sh["dsk_bc"] = ca(np.tile(inp["ssd_d"][0].reshape(1, 16), (128, 1)))
    sh["ssdg_bc"] = ca(np.tile(inp["ssd_norm_g"][0].reshape(1, 1024), (128, 1)))
    one = np.ones((128, 128), f32)
    sh["masks"] = ca(np.stack([np.triu(one), np.tril(one), np.tril(one, -1), np.triu(one, 1)]))
    blk = np.zeros((128, 128), f32)
    blk[:64, :64] = 1
    blk[64:, 64:] = 1
    sh["blk64"] = blk
    gq = inp["att_q_g"][0]
    gk = inp["att_k_g"][0]
    d64 = np.arange(128) % 64
    sh["gqk"] = ca(np.stack([gq[d64], gq[p32[d64]], gk[d64], gk[p32[d64]]], 1))
    c0, s0 = rope_tables(64)
    cs0 = np.zeros((128, T), f32)
    sn0 = np.zeros((128, T), f32)
    for r in range(128):
        d = r % 64
        m = d % 32
        cs0[r] = c0[:, m]
        sn0[r] = -s0[:, m] if d < 32 else s0[:, m]
    sh["cs0"], sh["sn0"] = cs0, sn0
    return sh


def make_in_maps(inp, n_cores=8):
    maps = []
    shared = make_shared(inp)
    for ci in range(n_cores):
        bs = slice(ci * NB, (ci + 1) * NB)
        m = dict(shared)
        m["xT"] = np.ascontiguousarray(np.swapaxes(inp["x"][bs], 1, 2))
        m["cxT"] = np.ascontiguousarray(np.swapaxes(inp["ctx"][bs], 1, 2))
        cv = np.stack([inp["c"][ci * NB], inp["c"][ci * NB + 1], inp["c_ctx"]], axis=-1)
        m["cT"] = np.ascontiguousarray(cv.reshape(8, 128, 3).transpose(1, 0, 2))
        maps.append(m)
    return maps


_CACHE = {}


def get_prog(cfg_key=()):
    if cfg_key not in _CACHE:
        p = Prog(dict(cfg_key))
        p.build()
        _CACHE[cfg_key] = p
    return _CACHE[cfg_key]


def kernel(**inputs):
    inp = {k: np.asarray(v) for k, v in inputs.items()}
    p = get_prog()
    maps = make_in_maps(inp)
    res = run_bass_kernel_spmd(p.c.nc, maps, core_ids=list(range(8)), trace=True)
    out = np.empty((16, T, D), np.float32)
    for ci in range(8):
        o = res.results[ci]["outT"]
        out[ci * NB:(ci + 1) * NB] = np.swapaxes(o, 1, 2)
    return out
```
